# Optimizing a Trainium2 kernel written in Bass

```python
import jax
import jax.numpy as jnp
from jax import lax
import numpy as np

D_MODEL = 2048
BATCH = 8
SEQ = 4096
DEPTH = 1
DEC_BATCH = 16
DEC_SEQ = 2048
PAST_LEN = 128

MIX_WIDTH = D_MODEL
ATTN_HEAD_DIM = 128
ATTN_WIDTH = MIX_WIDTH // 2
ATTN_HEADS = ATTN_WIDTH // ATTN_HEAD_DIM
DILATED_PATTERNS = ((128, 1), (512, 4), (2048, 16))
ATTN_BLOCK = 64
DN_KEY_DIM = 128
DN_VAL_DIM = 128
DN_VAL_WIDTH = MIX_WIDTH - ATTN_WIDTH
DN_HEADS = DN_VAL_WIDTH // DN_VAL_DIM
DN_KEY_WIDTH = DN_HEADS * DN_KEY_DIM
DN_CONV_CH = 2 * DN_KEY_WIDTH + DN_VAL_WIDTH
CONV_WIDTH = 5
CHUNK = 64
N_DIR = 2
PROJ_SIZES = (ATTN_WIDTH, ATTN_WIDTH, ATTN_WIDTH, ATTN_WIDTH,
              DN_KEY_WIDTH, DN_KEY_WIDTH, DN_VAL_WIDTH, DN_VAL_WIDTH,
              N_DIR * DN_HEADS, N_DIR * DN_HEADS)
PROJ_WIDTH = sum(PROJ_SIZES)
NORM_EPS = 1e-6
NEG_BIG = -1e30

kernel_name = 'hybrid_dilated_attn_gated_deltanet_encoder'


def rms_norm(x, gain):
    xf = x.astype(jnp.float32)
    y = xf * lax.rsqrt(jnp.mean(xf * xf, axis=-1, keepdims=True) + NORM_EPS)
    return (y * gain.astype(jnp.float32)).astype(x.dtype)


def l2_norm(x):
    return x * lax.rsqrt(jnp.sum(x * x, axis=-1, keepdims=True) + NORM_EPS)


def alibi_slopes(n_heads):
    return jnp.asarray([2.0 ** (-8.0 * (h + 1) / n_heads) for h in range(n_heads)], jnp.float32)


def dilated_window_branch(q, k, v, slopes, window, dilation):
    B, S, H, Dh = q.shape
    half = window // (2 * dilation)
    blk = ATTN_BLOCK
    L = S // dilation
    nblk = -(-L // blk)
    Lp = nblk * blk

    def to_blocks(t):
        t = t.reshape(B, L, dilation, H, Dh).transpose(0, 2, 1, 3, 4)
        t = jnp.pad(t, ((0, 0), (0, 0), (0, Lp - L), (0, 0), (0, 0)))
        return t.reshape(B, dilation, nblk, blk, H, Dh)

    def neighbours(t):
        tp = jnp.pad(t, ((0, 0), (0, 0), (1, 1), (0, 0), (0, 0), (0, 0)))
        return jnp.concatenate([tp[:, :, :-2], tp[:, :, 1:-1], tp[:, :, 2:]], axis=3)

    qb = to_blocks(q)
    kw = neighbours(to_blocks(k))
    vw = neighbours(to_blocks(v))

    qi = jnp.arange(nblk)[:, None] * blk + jnp.arange(blk)[None, :]
    ki = jnp.arange(nblk)[:, None] * blk - blk + jnp.arange(3 * blk)[None, :]
    off = jnp.abs(ki[:, None, :] - qi[:, :, None])
    valid = (off <= half) & ((ki >= 0) & (ki < L))[:, None, :]
    bias = -slopes[None, :, None, None] * (off * dilation).astype(jnp.float32)[:, None]
    valid = valid[:, None]

    s = jnp.einsum('brnqhe,brnkhe->brnhqk', qb, kw) * (Dh ** -0.5) + bias
    s = jnp.where(valid, s, NEG_BIG)
    m = jnp.max(s, axis=-1)
    p = jnp.where(valid, jnp.exp(s - m[..., None]), 0.0)
    l = jnp.sum(p, axis=-1)
    num = jnp.einsum('brnhqk,brnkhe->brnqhe', p, vw)

    def from_blocks(t):
        t = t.reshape((B, dilation, Lp) + t.shape[4:])[:, :, :L]
        t = jnp.moveaxis(t, 1, 2)
        return t.reshape((B, S) + t.shape[3:])

    return (from_blocks(jnp.swapaxes(m, -1, -2)),
            from_blocks(jnp.swapaxes(l, -1, -2)),
            from_blocks(num))


def dilated_attention(q, k, v):
    slopes = alibi_slopes(q.shape[2])
    outs = [dilated_window_branch(q, k, v, slopes, w, d) for (w, d) in DILATED_PATTERNS]
    m_all = jnp.stack([o[0] for o in outs], axis=0)
    l_all = jnp.stack([o[1] for o in outs], axis=0)
    n_all = jnp.stack([o[2] for o in outs], axis=0)
    scale = jnp.exp(m_all - jnp.max(m_all, axis=0, keepdims=True))
    den = jnp.sum(l_all * scale, axis=0)
    return jnp.sum(n_all * scale[..., None], axis=0) / den[..., None]


def short_conv(x, w):
    K, C = w.shape
    return lax.conv_general_dilated(x, w[:, None, :], window_strides=(1,),
                                    padding=[(K // 2, K // 2)],
                                    dimension_numbers=('NWC', 'WIO', 'NWC'),
                                    feature_group_count=C)


def chunk_gated_delta_rule(q, k, v, g, beta):
    B, S, H, Dk = q.shape
    Dv = v.shape[-1]
    n = S // CHUNK

    def chunks(t):
        return jnp.moveaxis(t.reshape((B, n, CHUNK, H) + t.shape[3:]), 3, 1)

    q, k, v, g, beta = chunks(q), chunks(k), chunks(v), chunks(g), chunks(beta)
    q = q * (Dk ** -0.5)
    gc = jnp.cumsum(g, axis=-1)
    incl = jnp.tril(jnp.ones((CHUNK, CHUNK), bool))
    strict = jnp.tril(jnp.ones((CHUNK, CHUNK), bool), -1)
    diff = gc[..., :, None] - gc[..., None, :]
    decay = jnp.where(incl, jnp.exp(jnp.where(incl, diff, 0.0)), 0.0)
    k_beta = k * beta[..., None]
    a = jnp.where(strict, jnp.einsum('bhnid,bhnjd->bhnij', k_beta, k) * decay, 0.0)
    t_mat = a + jnp.eye(CHUNK, dtype=a.dtype)
    u = lax.linalg.triangular_solve(t_mat, v * beta[..., None], left_side=True, lower=True, unit_diagonal=True)
    w = lax.linalg.triangular_solve(t_mat, k_beta * jnp.exp(gc)[..., None], left_side=True, lower=True, unit_diagonal=True)
    intra = jnp.einsum('bhnid,bhnjd->bhnij', q, k) * decay

    def step(state, xs):
        qc, kc, uc, wc, gcc, ic = xs
        v_new = uc - jnp.einsum('bhck,bhkv->bhcv', wc, state)
        o = (jnp.einsum('bhck,bhkv->bhcv', qc * jnp.exp(gcc)[..., None], state)
             + jnp.einsum('bhij,bhjv->bhiv', ic, v_new))
        g_last = gcc[..., -1]
        state = (state * jnp.exp(g_last)[..., None, None]
                 + jnp.einsum('bhck,bhcv->bhkv', kc * jnp.exp(g_last[..., None] - gcc)[..., None], v_new))
        return state, o

    xs = tuple(jnp.moveaxis(t, 2, 0) for t in (q, k, u, w, gc, intra))
    state0 = jnp.zeros((B, H, Dk, Dv), jnp.float32)
    _, o = lax.scan(step, state0, xs)
    o = jnp.moveaxis(o, 0, 2)
    return jnp.moveaxis(o, 1, 3).reshape(B, S, H, Dv)


def mixer_layer(x, norm_g, w_in, conv_w, a_log, dt_bias, dn_gain, w_out):
    B, S, _ = x.shape
    f32 = jnp.float32
    h = rms_norm(x, norm_g)
    proj = jnp.einsum('bsd,dp->bsp', h, w_in.astype(h.dtype)).astype(f32)
    split_at = [int(i) for i in np.cumsum(PROJ_SIZES)[:-1]]
    aq, ak, av, ag, dq, dk, dv, dg, b_raw, a_raw = jnp.split(proj, split_at, axis=-1)

    attn = dilated_attention(aq.reshape(B, S, ATTN_HEADS, ATTN_HEAD_DIM),
                             ak.reshape(B, S, ATTN_HEADS, ATTN_HEAD_DIM),
                             av.reshape(B, S, ATTN_HEADS, ATTN_HEAD_DIM))
    attn = attn.reshape(B, S, ATTN_WIDTH) * jax.nn.silu(ag)

    qkv = jax.nn.silu(short_conv(jnp.concatenate([dq, dk, dv], axis=-1), conv_w.astype(f32)))
    dq, dk, dv = jnp.split(qkv, [DN_KEY_WIDTH, 2 * DN_KEY_WIDTH], axis=-1)
    dq = l2_norm(dq.reshape(B, S, DN_HEADS, DN_KEY_DIM))
    dk = l2_norm(dk.reshape(B, S, DN_HEADS, DN_KEY_DIM))
    dv = dv.reshape(B, S, DN_HEADS, DN_VAL_DIM)
    beta = jax.nn.sigmoid(b_raw).reshape(B, S, N_DIR, DN_HEADS)
    g = -jnp.exp(a_log.astype(f32)) * jax.nn.softplus(a_raw.reshape(B, S, N_DIR, DN_HEADS) + dt_bias.astype(f32))
    o_fwd = chunk_gated_delta_rule(dq, dk, dv, g[:, :, 0], beta[:, :, 0])
    flip = lambda t: jnp.flip(t, axis=1)
    o_bwd = flip(chunk_gated_delta_rule(flip(dq), flip(dk), flip(dv), flip(g[:, :, 1]), flip(beta[:, :, 1])))
    dn = rms_norm(o_fwd + o_bwd, dn_gain).reshape(B, S, DN_VAL_WIDTH) * jax.nn.silu(dg)

    mixed = jnp.concatenate([attn, dn], axis=-1).astype(x.dtype)
    return x + jnp.einsum('bsm,md->bsd', mixed, w_out.astype(x.dtype))


def encoder(x, norm_in_gain, w_in, conv_w, a_log, dt_bias, delta_norm_gain, w_out, final_norm_gain):
    for l in range(DEPTH):
        x = mixer_layer(x, norm_in_gain[l], w_in[l], conv_w[l], a_log[l], dt_bias[l],
                        delta_norm_gain[l], w_out[l])
    return rms_norm(x, final_norm_gain)


def setup_inputs(seed: int = 0) -> dict:
    key = jax.random.key(seed)
    ks = jax.random.split(key, 11)
    f32 = jnp.float32
    x_prompt = jax.random.normal(ks[0], (BATCH, SEQ, D_MODEL), f32)
    x_sample = jax.random.normal(ks[1], (DEC_BATCH, DEC_SEQ, D_MODEL), f32)
    norm_in_gain = 1.0 + 0.02 * jax.random.normal(ks[2], (DEPTH, D_MODEL), f32)
    w_in = jax.random.normal(ks[3], (DEPTH, D_MODEL, PROJ_WIDTH), f32) * (D_MODEL ** -0.5)
    conv_w = jax.random.normal(ks[4], (DEPTH, CONV_WIDTH, DN_CONV_CH), f32) * (CONV_WIDTH ** -0.5)
    a_log = jnp.log(jax.random.uniform(ks[5], (DEPTH, N_DIR, DN_HEADS), f32, minval=1.0, maxval=16.0))
    dt = jnp.exp(jax.random.uniform(ks[6], (DEPTH, N_DIR, DN_HEADS), f32,
                                    minval=float(np.log(1e-3)), maxval=float(np.log(1e-1))))
    dt_bias = dt + jnp.log(-jnp.expm1(-dt))
    delta_norm_gain = 1.0 + 0.02 * jax.random.normal(ks[7], (DEPTH, DN_VAL_DIM), f32)
    w_out = jax.random.normal(ks[8], (DEPTH, MIX_WIDTH, D_MODEL), f32) * (MIX_WIDTH ** -0.5)
    final_norm_gain = 1.0 + 0.02 * jax.random.normal(ks[9], (D_MODEL,), f32)
    return {'x_prompt': x_prompt, 'x_sample': x_sample, 'norm_in_gain': norm_in_gain, 'w_in': w_in,
            'conv_w': conv_w, 'a_log': a_log, 'dt_bias': dt_bias, 'delta_norm_gain': delta_norm_gain,
            'w_out': w_out, 'final_norm_gain': final_norm_gain}


def reference(x_prompt, x_sample, norm_in_gain, w_in, conv_w, a_log, dt_bias, delta_norm_gain, w_out, final_norm_gain):
    y_prompt = encoder(x_prompt, norm_in_gain, w_in, conv_w, a_log, dt_bias, delta_norm_gain, w_out, final_norm_gain)
    y_sample = encoder(x_sample, norm_in_gain, w_in, conv_w, a_log, dt_bias, delta_norm_gain, w_out, final_norm_gain)
    return (y_prompt, y_sample)
```

```python
import contextlib
import numpy as np
import concourse.bass as bass
import concourse.mybir as mybir
from concourse.bass_utils import run_bass_kernel_spmd

F32 = mybir.dt.float32
BF16 = mybir.dt.bfloat16
AF = mybir.ActivationFunctionType
ALU = mybir.AluOpType

D = 2048
PW = 8224
NDS = 40
EPS = 1e-6
NEG = -30000.0
PATTERNS = (1, 4, 16)


class Dep:
    __slots__ = ("w", "r", "const")

    def __init__(self, const=False):
        self.w = None
        self.r = {}
        self.const = const


class Sched:
    def __init__(self, nc, stack):
        self.nc = nc
        self.names = ["pe", "act", "dve", "pool", "sp"]
        self.sem = {}
        self.cnt = {}
        self.waited = {}
        for e in self.names:
            self.sem[e] = stack.enter_context(nc.semaphore("s_" + e))
            self.cnt[e] = 0
            self.waited[e] = {}
        self.dsem = [stack.enter_context(nc.semaphore("d%d" % i)) for i in range(NDS)]
        self.dcnt = [0] * NDS
        self.ndma = 0
        self.ndma_sw = 0
        self.allsems = {}
        for e in self.names:
            self.allsems[id(self.sem[e])] = self.sem[e]
        for s in self.dsem:
            self.allsems[id(s)] = s
        self.nins = 0
        self.q = {e: [] for e in self.names}

    def flush(self):
        nc = self.nc
        q = self.q

        def replay(e, items):
            for it in items:
                if it[0] == "w":
                    e.wait_ge(it[1], it[2])
                else:
                    _, name, kw, inc = it
                    ins = getattr(e, name)(**kw)
                    if inc is not None:
                        ins.then_inc(inc[0], inc[1])

        with nc.Block() as block:
            @block.tensor
            def _(e):
                replay(e, q["pe"])

            @block.scalar
            def _(e):
                replay(e, q["act"])

            @block.vector
            def _(e):
                replay(e, q["dve"])

            @block.gpsimd
            def _(e):
                replay(e, q["pool"])

            @block.sync
            def _(e):
                replay(e, q["sp"])
        self.q = {e: [] for e in self.names}

    def _collect(self, eng, reads, writes, extra=()):
        waits = {}

        def need(ev):
            if ev is None:
                return
            sem, val, src = ev
            if src == "pe" and eng == "pe":
                return
            k = id(sem)
            if waits.get(k, 0) < val:
                waits[k] = val

        for d in reads:
            need(d.w)
        for d in writes:
            need(d.w)
            for ev in d.r.values():
                need(ev)
        for ev in extra:
            need(ev)
        out = []
        wd = self.waited[eng]
        for k, val in waits.items():
            if wd.get(k, 0) >= val:
                continue
            wd[k] = val
            out.append((self.allsems[k], val))
        return out

    def _record(self, ev, reads, writes):
        k = id(ev[0])
        for d in reads:
            if d.const:
                continue
            old = d.r.get(k)
            if old is None or old[1] < ev[1]:
                d.r[k] = ev
        for d in writes:
            d.w = ev
            d.r = {}

    def op(self, eng, name, reads=(), writes=(), signal=True, **kw):
        q = self.q[eng]
        for sem, val in self._collect(eng, reads, writes):
            q.append(("w", sem, val))
        self.nins += 1
        if signal:
            self.cnt[eng] += 1
            q.append(("i", name, kw, (self.sem[eng], 1)))
            ev = (self.sem[eng], self.cnt[eng], eng)
        else:
            assert eng == "pe"
            q.append(("i", name, kw, None))
            ev = (self.sem[eng], self.cnt[eng] + 1, eng)
        self._record(ev, reads, writes)
        return ev

    def dma(self, q, out, in_, reads=(), writes=(), **dkw):
        if q == "sp":
            s = self.ndma % (NDS - 8)
            self.ndma += 1
        else:
            s = (NDS - 8) + self.ndma_sw % 8
            self.ndma_sw += 1
        sem = self.dsem[s]
        prev = self.dcnt[s]
        self.dcnt[s] += 16
        val = self.dcnt[s]
        extra = [(sem, prev, "dma")] if prev > 0 else []
        for sm, v in self._collect(q, reads, writes, extra):
            self.q[q].append(("w", sm, v))
        self.q[q].append(("i", "dma_start", dict(out=out, in_=in_, **dkw), (sem, 16)))
        self.nins += 1
        ev = (sem, val, "dma")
        self._record(ev, reads, writes)
        return ev

    def barrier(self):
        for eng in self.names:
            wd = self.waited[eng]
            for src in self.names:
                if src == eng or self.cnt[src] == 0:
                    continue
                k = id(self.sem[src])
                if wd.get(k, 0) < self.cnt[src]:
                    wd[k] = self.cnt[src]
                    self.q[eng].append(("w", self.sem[src], self.cnt[src]))
            for s in range(NDS):
                if self.dcnt[s] == 0:
                    continue
                k = id(self.dsem[s])
                if wd.get(k, 0) < self.dcnt[s]:
                    wd[k] = self.dcnt[s]
                    self.q[eng].append(("w", self.dsem[s], self.dcnt[s]))

    def phase_end(self):
        self.barrier()
        self.flush()


_UID = [0]


def _uniq(name):
    _UID[0] += 1
    return "%s_u%d" % (name, _UID[0])


class Rot:
    def __init__(self, items):
        self.items = items
        self.i = 0

    def next(self):
        it = self.items[self.i % len(self.items)]
        self.i += 1
        return it


def _consts():
    c = {}
    c["ident"] = np.eye(128, dtype=np.float32)
    k = np.arange(128)[:, None]
    q = np.arange(128)[None, :]
    slopes = np.array([2.0 ** (-8.0 * (h + 1) / 8) for h in range(8)], np.float64)
    et = np.zeros((128, 8, 3, 2, 128), np.float32)
    for h in range(8):
        for p, d in enumerate(PATTERNS):
            sc_ = 128.0 ** -0.5
            lo = np.where(k >= q, -slopes[h] * d * np.abs(64 + q - k) / sc_, NEG)
            hi = np.where(k <= q, -slopes[h] * d * np.abs(q - k - 64) / sc_, NEG)
            et[:, h, p, 0, :] = lo
            et[:, h, p, 1, :] = hi
    c["etab"] = et.reshape(128, 8 * 3 * 2 * 128)
    j = np.arange(128)[:, None]
    i = np.arange(128)[None, :]
    same = (j // 64) == (i // 64)
    nm = np.zeros((128, 2, 2, 128), np.float32)
    nm[:, 0, 0, :] = np.where(same & (j < i), 0.0, NEG)
    nm[:, 0, 1, :] = np.where(same & (j <= i), 0.0, NEG)
    nm[:, 1, 0, :] = np.where(same & (j > i), 0.0, NEG)
    nm[:, 1, 1, :] = np.where(same & (j >= i), 0.0, NEG)
    c["negmask"] = nm.reshape(128, 512)
    cm = np.zeros((128, 5, 128), np.float32)
    cm[:, 0, :] = (same & (j <= i))
    cm[:, 1, :] = (same & (j >= i))
    cm[:, 2, :] = same
    cm[:, 3, :] = (j < 64)
    cm[:, 4, :] = (j >= 64)
    c["cmask"] = cm.reshape(128, 640)
    i2 = np.zeros((128, 2, 128), np.float32)
    for t in range(128):
        i2[t, t // 64, t] = 1.0
    c["ident2"] = i2.reshape(128, 256)
    hm = np.zeros((128, 2), np.float32)
    hm[:64, 0] = 1.0
    hm[64:, 1] = 1.0
    c["hmask"] = hm
    return c


def build(seqs, phases="0ABCD", dbg=False, ext_in=False):
    NTOK = sum(seqs)
    assert ext_in or (NTOK % 2048 == 0 and all(s % 2048 == 0 for s in seqs))
    NSB = NTOK // 2048
    nc = bass.Bass("TRN2", target_bir_lowering=False)
    inp = lambda name, shape: nc.dram_tensor(name, shape, F32, kind="ExternalInput").ap()
    x = inp("x", [NTOK, D])
    w_in = inp("w_in", [D, PW])
    w_out = inp("w_out", [D, D])
    gin = inp("gin", [128, 16])
    convw = inp("convw", [128, 24 * 5])
    alog = inp("alog", [128, 32 * 16])
    dtb = inp("dtb", [128, 32 * 16])
    dngain = inp("dngain", [128, 1])
    fgain = inp("fgain", [128, D])
    c_ident = inp("ident", [128, 128])
    c_etab = inp("etab", [128, 8 * 3 * 2 * 128])
    c_negmask = inp("negmask", [128, 512])
    c_cmask = inp("cmask", [128, 640])
    c_ident2 = inp("ident2", [128, 256])
    c_hmask = inp("hmask", [128, 2])
    okind = "ExternalOutput"
    y = nc.dram_tensor("y", [NTOK, D], F32, kind=okind).ap()
    skind = "ExternalOutput" if dbg else "Internal"
    winb = nc.dram_tensor("winb", [128, 16, PW], BF16).ap()
    ikind = "ExternalInput" if ext_in else skind
    FT = nc.dram_tensor("FT", [56, 128, NTOK], BF16, kind=ikind).ap()
    AV = nc.dram_tensor("AV", [NTOK, 1024], BF16, kind=ikind).ap()
    GB = nc.dram_tensor("GB", [NTOK, 32], F32, kind=ikind).ap()
    ROWS = nc.dram_tensor("ROWS", [32, NTOK], F32, kind=skind).ap()
    MT = nc.dram_tensor("MT", [16, 128, NTOK], BF16, kind=skind).ap()

    with contextlib.ExitStack() as gst:
        S = Sched(nc, gst)
        GT = lambda name, shape, dt: gst.enter_context(nc.sbuf_tensor(name, shape, dt))
        identf = GT("identf", [128, 128], F32)
        identb = GT("identb", [128, 128], BF16)
        onesb = GT("onesb", [128, 128], BF16)
        d_const = Dep(const=True)
        S.dma("sp", identf[:], c_ident, writes=[d_const])
        S.op("dve", "tensor_copy", reads=[d_const], writes=[d_const], out=identb[:], in_=identf[:])
        S.op("dve", "memset", writes=[d_const], ap=onesb[:], constant=1.0)
        S.phase_end()

        if "0" in phases:
            _phase0(nc, S, w_in, gin, winb)
        if "A" in phases:
            _phaseA(nc, S, x, winb, FT, AV, GB, NSB, identb, d_const)
        off = 0
        for si, slen in enumerate(seqs):
            if "B" in phases:
                _phaseB(nc, S, FT, AV, MT, off, slen, c_etab, onesb, identb, d_const)
            if "C" in phases:
                _phaseC(nc, S, FT, GB, ROWS, MT, off, slen, convw, alog, dtb, dngain,
                        c_negmask, c_cmask, c_ident2, c_hmask, identf, identb, d_const)
            off += slen
        if "D" in phases:
            _phaseD(nc, S, x, w_out, MT, fgain, y, NTOK, use_mixed=("B" in phases or "C" in phases),
                    heads=(range(16) if ("B" in phases and "C" in phases) else (range(8) if "B" in phases else range(8, 16))))
        S.phase_end()
    return nc


def _phase0(nc, S, w_in, gin, winb):
    with contextlib.ExitStack() as st:
        T = lambda name, shape, dt: st.enter_context(nc.sbuf_tensor(_uniq(name), shape, dt))
        gint = T("gint", [128, 16], F32)
        d_g = Dep()
        S.dma("sp", gint[:], gin, writes=[d_g])
        wf = Rot([(T("p0wf%d" % i, [128, 2056], F32), Dep()) for i in range(3)])
        wb = Rot([(T("p0wb%d" % i, [128, 2056], BF16), Dep()) for i in range(3)])
        n = 0
        for kc in range(16):
            for sl in range(4):
                f, d_f = wf.next()
                b, d_b = wb.next()
                S.dma("sp", f[:], w_in[kc * 128:(kc + 1) * 128, sl * 2056:(sl + 1) * 2056], writes=[d_f])
                if n % 2 == 0:
                    S.op("act", "activation", reads=[d_f, d_g], writes=[d_b], out=b[:], in_=f[:], func=AF.Copy, scale=gint[:, kc:kc + 1])
                else:
                    S.op("dve", "tensor_scalar", reads=[d_f, d_g], writes=[d_b], out=b[:], in0=f[:], scalar1=gint[:, kc:kc + 1], scalar2=None, op0=ALU.mult)
                S.dma("pool", winb[:, kc, sl * 2056:(sl + 1) * 2056], b[:], reads=[d_b])
                n += 1
        S.phase_end()


def _ft_index(cb):
    if cb < 16:
        return cb
    if cb < 24:
        return None
    return cb - 8


def _phaseA(nc, S, x, winb, FT, AV, GB, NSB, identb, d_const):
    with contextlib.ExitStack() as st:
        T = lambda name, shape, dt: st.enter_context(nc.sbuf_tensor(_uniq(name), shape, dt))
        P = lambda name, shape, dt: st.enter_context(nc.psum_tensor(_uniq(name), shape, dt))
        hTs = [(T("hT%d" % i, [128, 16, 2048], BF16), Dep()) for i in range(2)]
        xts = Rot([(T("xt%d" % i, [128, D], F32), Dep()) for i in range(2)])
        hbs = Rot([(T("hb%d" % i, [128, D], BF16), Dep()) for i in range(2)])
        junk = T("junkA", [128, D], BF16)
        d_junk = Dep()
        sms = Rot([(T("smA%d" % i, [128, 2], F32), Dep()) for i in range(2)])
        wsl = Rot([(T("wsl%d" % i, [128, 16, 512], BF16), Dep()) for i in range(2)])
        stg = Rot([(T("stg%d" % i, [128, 2048], BF16), Dep()) for i in range(2)])
        stv = Rot([(T("stv%d" % i, [128, 512], BF16), Dep()) for i in range(3)])
        gbt = T("gbt", [128, 16, 32], F32)
        d_gbt = Dep()
        wtail = T("wtail", [128, 16, 32], BF16)
        d_wtail = Dep()
        ptr = Rot([(P("ptr%d" % i, [128, 4, 128], BF16), Dep()) for i in range(2)])
        pac = Rot([(P("pac%d" % i, [128, 512], F32), Dep()) for i in range(6)])
        S.dma("sp", wtail[:], winb[:, :, 8192:8224], writes=[d_wtail])
        ne = 0

        def a1(sb):
            t0 = sb * 2048
            hT, d_hT = hTs[sb % 2]
            for tt in range(16):
                xt, d_xt = xts.next()
                hb, d_hb = hbs.next()
                sm, d_sm = sms.next()
                S.dma("sp", xt[:], x[t0 + tt * 128:t0 + (tt + 1) * 128, :], writes=[d_xt])
                S.op("act", "activation", reads=[d_xt], writes=[d_junk, d_sm], out=junk[:], in_=xt[:], func=AF.Square, accum_out=sm[:, 0:1])
                S.op("dve", "tensor_scalar", reads=[d_sm], writes=[d_sm], out=sm[:, 1:2], in0=sm[:, 0:1], scalar1=1.0 / D, scalar2=EPS, op0=ALU.mult, op1=ALU.add)
                S.op("act", "activation", reads=[d_sm], writes=[d_sm], out=sm[:, 1:2], in_=sm[:, 1:2], func=AF.Sqrt)
                S.op("dve", "reciprocal", reads=[d_sm], writes=[d_sm], out=sm[:, 1:2], in_=sm[:, 1:2])
                S.op("act", "activation", reads=[d_xt, d_sm], writes=[d_hb], out=hb[:], in_=xt[:], func=AF.Copy, scale=sm[:, 1:2])
                for g in range(4):
                    pt, d_pt = ptr.next()
                    for j in range(4):
                        kc = g * 4 + j
                        S.op("pe", "transpose", reads=[d_hb, d_const], writes=[d_pt], signal=(j == 3),
                             out=pt[:, j, :], in_=hb[:, kc * 128:(kc + 1) * 128], identity=identb[:])
                    S.op("dve", "tensor_copy", reads=[d_pt], writes=[d_hT],
                         out=hT[:, g * 4:(g + 1) * 4, tt * 128:(tt + 1) * 128], in_=pt[:])
                yield

        for _ in a1(0):
            pass
        for sb in range(NSB):
            t0 = sb * 2048
            hT, d_hT = hTs[sb % 2]
            bg = a1(sb + 1) if sb + 1 < NSB else None
            for slab in range(16):
                if bg is not None:
                    try:
                        next(bg)
                    except StopIteration:
                        bg = None
                ws, d_ws = wsl.next()
                S.dma("sp", ws[:], winb[:, :, slab * 512:(slab + 1) * 512], writes=[d_ws])
                if 4 <= slab < 6:
                    for tt in range(16):
                        pa, d_pa = pac.next()
                        for kc in range(16):
                            S.op("pe", "matmul", reads=[d_hT, d_ws], writes=[d_pa], signal=(kc == 15),
                                 out=pa[:], lhsT=hT[:, kc, tt * 128:(tt + 1) * 128], rhs=ws[:, kc, :], start=(kc == 0), stop=(kc == 15))
                        sv, d_sv = stv.next()
                        if ne % 2 == 0:
                            S.op("act", "activation", reads=[d_pa], writes=[d_sv], out=sv[:], in_=pa[:], func=AF.Copy)
                        else:
                            S.op("dve", "tensor_copy", reads=[d_pa], writes=[d_sv], out=sv[:], in_=pa[:])
                        ne += 1
                        S.dma("pool", AV[t0 + tt * 128:t0 + (tt + 1) * 128, (slab - 4) * 512:(slab - 3) * 512], sv[:], reads=[d_sv])
                    continue
                for cbi in range(4):
                    cb = slab * 4 + cbi
                    pas = [pac.next() for _ in range(4)]
                    for kc in range(16):
                        for tb in range(4):
                            pa, d_pa = pas[tb]
                            S.op("pe", "matmul", reads=[d_hT, d_ws], writes=[d_pa], signal=(kc == 15),
                                 out=pa[:], lhsT=ws[:, kc, cbi * 128:(cbi + 1) * 128], rhs=hT[:, kc, tb * 512:(tb + 1) * 512],
                                 start=(kc == 0), stop=(kc == 15))
                    sg, d_sg = stg.next()
                    for tb in range(4):
                        pa, d_pa = pas[tb]
                        if ne % 2 == 0:
                            S.op("act", "activation", reads=[d_pa], writes=[d_sg], out=sg[:, tb * 512:(tb + 1) * 512], in_=pa[:], func=AF.Copy)
                        else:
                            S.op("dve", "tensor_copy", reads=[d_pa], writes=[d_sg], out=sg[:, tb * 512:(tb + 1) * 512], in_=pa[:])
                        ne += 1
                    S.dma("pool", FT[_ft_index(cb), :, t0:t0 + 2048], sg[:], reads=[d_sg])
            if bg is not None:
                for _ in bg:
                    pass
            for tt in range(16):
                pa, d_pa = pac.next()
                for kc in range(16):
                    S.op("pe", "matmul", reads=[d_hT, d_wtail], writes=[d_pa], signal=(kc == 15),
                         out=pa[:, 0:32], lhsT=hT[:, kc, tt * 128:(tt + 1) * 128], rhs=wtail[:, kc, :], start=(kc == 0), stop=(kc == 15))
                S.op("dve", "tensor_copy", reads=[d_pa], writes=[d_gbt], out=gbt[:, tt, :], in_=pa[:, 0:32])
            for t8 in range(2):
                S.dma("pool", GB[t0 + t8 * 1024:t0 + (t8 + 1) * 1024, :].rearrange("(t p) c -> p t c", p=128), gbt[:, t8 * 8:(t8 + 1) * 8, :], reads=[d_gbt])
        S.phase_end()


def _phaseD(nc, S, x, w_out, MT, fgain, y, NTOK, use_mixed, heads):
    heads = list(heads)
    with contextlib.ExitStack() as st:
        T = lambda name, shape, dt: st.enter_context(nc.sbuf_tensor(_uniq(name), shape, dt))
        P = lambda name, shape, dt: st.enter_context(nc.psum_tensor(_uniq(name), shape, dt))
        fg = T("fg", [128, D], F32)
        d_fg = Dep()
        S.dma("sp", fg[:], fgain, writes=[d_fg])
        wo = T("wo", [128, 16, D], BF16)
        d_wo = Dep()
        if use_mixed:
            wf = Rot([(T("dwf%d" % i, [128, D], F32), Dep()) for i in range(2)])
            for kc in range(16):
                f, d_f = wf.next()
                S.dma("sp", f[:], w_out[kc * 128:(kc + 1) * 128, :], writes=[d_f])
                if kc % 2 == 0:
                    S.op("act", "activation", reads=[d_f], writes=[d_wo], out=wo[:, kc, :], in_=f[:], func=AF.Copy)
                else:
                    S.op("dve", "tensor_copy", reads=[d_f], writes=[d_wo], out=wo[:, kc, :], in_=f[:])
        xts = Rot([(T("dxt%d" % i, [128, D], F32), Dep()) for i in range(2)])
        yts = Rot([(T("dyt%d" % i, [128, D], F32), Dep()) for i in range(2)])
        mts = Rot([(T("dmt%d" % i, [128, 16, 512], BF16), Dep()) for i in range(2)])
        junk = T("junkD", [128, D], BF16)
        d_junk = Dep()
        sms = Rot([(T("smD%d" % i, [128, 2], F32), Dep()) for i in range(2)])
        pac = Rot([(P("dpac%d" % i, [128, 512], F32), Dep()) for i in range(6)])
        for t4 in range(NTOK // 512):
            if use_mixed:
                mt, d_mt = mts.next()
                for h in heads:
                    S.dma("sp", mt[:, h, :], MT[h, :, t4 * 512:(t4 + 1) * 512], writes=[d_mt])
            for ti in range(4):
                tt = t4 * 4 + ti
                xt, d_xt = xts.next()
                yt, d_yt = yts.next()
                sm, d_sm = sms.next()
                S.dma("sp", xt[:], x[tt * 128:(tt + 1) * 128, :], writes=[d_xt])
                if use_mixed:
                    for cb in range(4):
                        pa, d_pa = pac.next()
                        for n, h in enumerate(heads):
                            S.op("pe", "matmul", reads=[d_mt, d_wo], writes=[d_pa], signal=(n == len(heads) - 1),
                                 out=pa[:], lhsT=mt[:, h, ti * 128:(ti + 1) * 128], rhs=wo[:, h, cb * 512:(cb + 1) * 512],
                                 start=(n == 0), stop=(n == len(heads) - 1))
                        S.op("dve", "tensor_tensor", reads=[d_pa, d_xt], writes=[d_yt],
                             out=yt[:, cb * 512:(cb + 1) * 512], in0=pa[:], in1=xt[:, cb * 512:(cb + 1) * 512], op=ALU.add)
                    src, d_src = yt, d_yt
                else:
                    src, d_src = xt, d_xt
                S.op("act", "activation", reads=[d_src], writes=[d_junk, d_sm], out=junk[:], in_=src[:], func=AF.Square, accum_out=sm[:, 0:1])
                S.op("dve", "tensor_scalar", reads=[d_sm], writes=[d_sm], out=sm[:, 1:2], in0=sm[:, 0:1], scalar1=1.0 / D, scalar2=EPS, op0=ALU.mult, op1=ALU.add)
                S.op("act", "activation", reads=[d_sm], writes=[d_sm], out=sm[:, 1:2], in_=sm[:, 1:2], func=AF.Sqrt)
                S.op("dve", "reciprocal", reads=[d_sm], writes=[d_sm], out=sm[:, 1:2], in_=sm[:, 1:2])
                S.op("dve", "scalar_tensor_tensor", reads=[d_src, d_sm, d_fg], writes=[d_yt],
                     out=yt[:], in0=src[:], scalar=sm[:, 1:2], in1=fg[:], op0=ALU.mult, op1=ALU.mult)
                S.dma("pool", y[tt * 128:(tt + 1) * 128, :], yt[:], reads=[d_yt])
        S.phase_end()


def _sl(r, d, j0, j1):
    return slice(r + d * j0, r + d * (j1 - 1) + 1, d)


def _phaseB(nc, S, FT, AV, MT, off, slen, c_etab, onesb, identb, d_const, heads=range(8)):
    nb = slen // 128
    scale = 128.0 ** -0.5
    with contextlib.ExitStack() as st:
        T = lambda name, shape, dt: st.enter_context(nc.sbuf_tensor(_uniq(name), shape, dt))
        P = lambda name, shape, dt: st.enter_context(nc.psum_tensor(_uniq(name), shape, dt))
        etf = T("etf", [128, 768], F32)
        d_etf = Dep()
        etb = T("etb", [128, 8, 3, 2, 128], BF16)
        d_etb = Dep()
        for h in range(8):
            S.dma("sp", etf[:], c_etab[:, h * 768:(h + 1) * 768], writes=[d_etf])
            S.op("dve", "tensor_copy", reads=[d_etf], writes=[d_etb], out=etb[:, h].rearrange("p a b c -> p (a b c)"), in_=etf[:])
        qkg = Rot([([T("bq%d" % i, [128, slen], BF16), T("bk%d" % i, [128, slen], BF16), T("bg%d" % i, [128, slen], BF16)], Dep()) for i in range(2)])
        vts = Rot([([T("bv%d_%d" % (i, d), [128, d, nb // d, 128], BF16) for d in PATTERNS], Dep()) for i in range(2)])
        acc = T("bacc", [128, 2, slen], F32)
        d_acc = Dep()
        pes = Rot([(T("bpe%d" % i, [128, 2, 128], BF16), Dep()) for i in range(5)])
        tmps = Rot([(T("btm%d" % i, [128, 2, 128], F32), Dep()) for i in range(3)])
        nst = [0]
        sgt = T("bsg", [128, slen], BF16)
        d_sgt = Dep()
        outt = T("bout", [128, slen], BF16)
        d_out = Dep()
        pst = Rot([(P("bst%d" % i, [128, 2, 128], F32), Dep()) for i in range(3)])
        pol = Rot([(P("bol%d" % i, [128, 2, 256], F32), Dep()) for i in range(3)])
        for h in heads:
            (qt_, kt_, gt_), d_qkg = qkg.next()
            vt, d_v = vts.next()
            S.dma("sp", qt_[:], FT[h, :, off:off + slen], writes=[d_qkg])
            S.dma("sp", kt_[:], FT[8 + h, :, off:off + slen], writes=[d_qkg])
            S.dma("sp", gt_[:], FT[16 + h, :, off:off + slen], writes=[d_qkg])
            for pi, d in enumerate(PATTERNS):
                njh = nb // d
                for r in range(d):
                    src = AV[off:off + slen, h * 128:(h + 1) * 128].rearrange("(jh p r) c -> p r jh c", p=128, r=d)[:, r]
                    for j0 in range(0, njh, 8):
                        j1 = min(njh, j0 + 8)
                        S.dma("sp", vt[pi][:, r, j0:j1], src[:, j0:j1], writes=[d_v])
            tiles = []
            for pi, d in enumerate(PATTERNS):
                L = slen // d
                nkb = L // 128
                for r in range(d):
                    for qt in range(nkb + 1):
                        tiles.append((pi, d, L, nkb, r, qt))
            LAG = 2
            inflight = []
            groups = []
            for ti, (pi, d, L, nkb, r, qt) in enumerate(tiles):
                q0 = max(0, 128 * qt - 64)
                q1 = min(L, 128 * qt + 64)
                if groups and groups[-1]["key"] == (pi, r) and groups[-1]["q1"] == q0 and (q1 - groups[-1]["q0"]) <= 256:
                    groups[-1]["q1"] = q1
                    groups[-1]["last"] = ti
                else:
                    groups.append(dict(key=(pi, r), q0=q0, q1=q1, last=ti, po=None))
                tiles[ti] = (pi, d, L, nkb, r, qt, groups[-1])

            def stage2(ent):
                (ti, pi, d, r, q0, q1, nq, blocks, pe, d_pe, grp) = ent
                if grp["po"] is None:
                    grp["po"] = pol.next()
                po, d_po = grp["po"]
                go = q0 - grp["q0"]
                for bi, (b, kb) in enumerate(blocks):
                    S.op("pe", "matmul", reads=[d_pe, d_v], writes=[d_po], signal=False,
                         out=po[:, 0, go:go + nq], lhsT=vt[pi][:, r, kb, :], rhs=pe[:, b, 0:nq], start=(bi == 0), stop=(bi == len(blocks) - 1))
                for bi, (b, kb) in enumerate(blocks):
                    S.op("pe", "matmul", reads=[d_pe, d_const], writes=[d_po], signal=(bi == len(blocks) - 1),
                         out=po[:, 1, go:go + nq], lhsT=onesb[:], rhs=pe[:, b, 0:nq], start=(bi == 0), stop=(bi == len(blocks) - 1))
                if ti != grp["last"]:
                    return
                gn = grp["q1"] - grp["q0"]
                aap = acc[:, :, _sl(r, d, grp["q0"], grp["q1"])]
                if pi == 0:
                    S.op("dve", "tensor_copy", reads=[d_po], writes=[d_acc], out=aap, in_=po[:, :, 0:gn])
                else:
                    S.op("dve", "tensor_tensor", reads=[d_po, d_acc], writes=[d_acc], out=aap, in0=po[:, :, 0:gn], in1=aap, op=ALU.add)

            for ti, (pi, d, L, nkb, r, qt, grp) in enumerate(tiles):
                q0 = max(0, 128 * qt - 64)
                q1 = min(L, 128 * qt + 64)
                nq = q1 - q0
                qoff = 64 if qt == 0 else 0
                blocks = []
                if qt >= 1:
                    blocks.append((0, qt - 1))
                if qt < nkb:
                    blocks.append((1, qt))
                ps, d_ps = pst.next()
                pe, d_pe = pes.next()
                qap = qt_[:, _sl(r, d, q0, q1)]
                for bi, (b, kb) in enumerate(blocks):
                    S.op("pe", "matmul", reads=[d_qkg], writes=[d_ps], signal=False,
                         out=ps[:, b, 0:nq], lhsT=kt_[:, _sl(r, d, 128 * kb, 128 * (kb + 1))], rhs=qap, start=True, stop=False)
                    S.op("pe", "matmul", reads=[d_etb, d_const], writes=[d_ps], signal=(bi == len(blocks) - 1),
                         out=ps[:, b, 0:nq], lhsT=identb[:], rhs=etb[:, h, pi, b, qoff:qoff + nq], start=False, stop=True)
                b0 = blocks[0][0]
                b1 = blocks[-1][0] + 1
                S.op("act", "activation", reads=[d_ps], writes=[d_pe], out=pe[:, b0:b1, 0:nq], in_=ps[:, b0:b1, 0:nq], func=AF.Exp, scale=scale)
                inflight.append((ti, pi, d, r, q0, q1, nq, blocks, pe, d_pe, grp))
                if len(inflight) > LAG:
                    stage2(inflight.pop(0))
            while inflight:
                stage2(inflight.pop(0))
            S.op("act", "activation", reads=[d_acc], writes=[d_acc], out=acc[:, 1, :], in_=acc[:, 1, :], func=AF.Ln)
            S.op("act", "activation", reads=[d_acc], writes=[d_acc], out=acc[:, 1, :], in_=acc[:, 1, :], func=AF.Exp, scale=-1.0)
            S.op("act", "activation", reads=[d_qkg], writes=[d_sgt], out=sgt[:], in_=gt_[:], func=AF.Silu)
            S.op("dve", "tensor_tensor", reads=[d_acc], writes=[d_acc], out=acc[:, 0, :], in0=acc[:, 0, :], in1=acc[:, 1, :], op=ALU.mult)
            S.op("dve", "tensor_tensor", reads=[d_acc, d_sgt], writes=[d_out], out=outt[:], in0=acc[:, 0, :], in1=sgt[:], op=ALU.mult)
            S.dma("pool", MT[h, :, off:off + slen], outt[:], reads=[d_out])
        S.phase_end()


def _phaseC(nc, S, FT, GB, ROWS, MT, off, slen, convw, alog, dtb, dngain,
            c_negmask, c_cmask, c_ident2, c_hmask, identf, identb, d_const, heads=range(8)):
    nb = slen // 128
    NTOK = ROWS.shape[1]
    qscale = 128.0 ** -0.5
    with contextlib.ExitStack() as st:
        T = lambda name, shape, dt: st.enter_context(nc.sbuf_tensor(_uniq(name), shape, dt))
        P = lambda name, shape, dt: st.enter_context(nc.psum_tensor(_uniq(name), shape, dt))
        d_c = Dep()
        negm = T("negm", [128, 2, 256], F32)
        ident2b = T("ident2b", [128, 256], BF16)
        hmask = T("hmask", [128, 2], F32)
        cw = T("cw", [128, 24, 5], F32)
        dng = T("dng", [128, 1], F32)
        S.dma("sp", negm[:].rearrange("p a b -> p (a b)"), c_negmask, writes=[d_c])
        S.dma("sp", hmask[:], c_hmask, writes=[d_c])
        S.dma("sp", cw[:].rearrange("p a b -> p (a b)"), convw, writes=[d_c])
        S.dma("sp", dng[:], dngain, writes=[d_c])
        gc = T("gc", [128, nb, 16], F32)
        egc = T("egc", [128, nb, 16], F32)
        edec = T("edec", [128, nb, 16], F32)
        beta = T("beta", [128, nb, 16], F32)
        sdec = [T("sdec0", [128, nb, 16], F32), T("sdec1", [128, nb, 16], F32)]
        begc = T("begc", [128, nb, 16], F32)
        d_gate = Dep()
        bank = [P("cbank%d" % i, [128, 512], F32) for i in range(8)]
        d_bank = [Dep() for _ in range(8)]
        bankb = [bank[i][:].bitcast(BF16) for i in range(8)]
        with contextlib.ExitStack() as st2:
            T2 = lambda name, shape, dt: st2.enter_context(nc.sbuf_tensor(_uniq(name), shape, dt))
            cmask = T2("cmask", [128, 5, 128], F32)
            id2f = T2("id2f", [128, 256], F32)
            S.dma("sp", cmask[:].rearrange("p a b -> p (a b)"), c_cmask, writes=[d_c])
            S.dma("sp", id2f[:], c_ident2, writes=[d_c])
            S.op("dve", "tensor_copy", reads=[d_c], writes=[d_c], out=ident2b[:], in_=id2f[:])
            gbt = T2("gbt", [128, nb, 32], F32)
            al = T2("al", [128, nb, 16], F32)
            db = T2("db", [128, nb, 16], F32)
            t1 = T2("t1", [128, nb, 16], F32)
            t2 = T2("t2", [128, nb, 16], F32)
            z = T2("z", [128, nb, 16], F32)
            lnb = T2("lnb", [128, nb, 16], F32)
            g = T2("g", [128, nb, 16], F32)
            r1 = T2("r1", [128, nb, 16], F32)
            rowst = T2("rowst", [32, 32, 128], F32)
            d_g = Dep()
            S.dma("sp", gbt[:], GB[off:off + slen, :].rearrange("(t p) c -> p t c", p=128), writes=[d_g])
            S.dma("sp", al[:].rearrange("p a b -> p (a b)"), alog[:, 0:nb * 16], writes=[d_g])
            S.dma("sp", db[:].rearrange("p a b -> p (a b)"), dtb[:, 0:nb * 16], writes=[d_g])
            braw = gbt[:, :, 0:16]
            araw = gbt[:, :, 16:32]
            G = dict(reads=[d_g], writes=[d_g])
            S.op("dve", "scalar_tensor_tensor", out=t1[:], in0=braw, scalar=-1.0, in1=braw, op0=ALU.mult, op1=ALU.max, **G)
            S.op("act", "activation", out=t1[:], in_=t1[:], func=AF.Exp, scale=-1.0, **G)
            S.op("dve", "tensor_scalar", out=t1[:], in0=t1[:], scalar1=1.0, scalar2=None, op0=ALU.add, **G)
            S.op("act", "activation", out=t1[:], in_=t1[:], func=AF.Ln, **G)
            S.op("dve", "scalar_tensor_tensor", out=lnb[:], in0=braw, scalar=0.0, in1=t1[:], op0=ALU.min, op1=ALU.subtract, **G)
            S.op("act", "activation", reads=[d_g], writes=[d_gate], out=beta[:], in_=lnb[:], func=AF.Exp)
            S.op("dve", "tensor_tensor", out=z[:], in0=araw, in1=db[:], op=ALU.add, **G)
            S.op("dve", "scalar_tensor_tensor", out=t2[:], in0=z[:], scalar=-1.0, in1=z[:], op0=ALU.mult, op1=ALU.max, **G)
            S.op("act", "activation", out=t2[:], in_=t2[:], func=AF.Exp, scale=-1.0, **G)
            S.op("dve", "tensor_scalar", out=t2[:], in0=t2[:], scalar1=1.0, scalar2=None, op0=ALU.add, **G)
            S.op("act", "activation", out=t2[:], in_=t2[:], func=AF.Ln, **G)
            S.op("dve", "scalar_tensor_tensor", out=t2[:], in0=z[:], scalar=0.0, in1=t2[:], op0=ALU.max, op1=ALU.add, **G)
            S.op("act", "activation", out=al[:], in_=al[:], func=AF.Exp, **G)
            S.op("dve", "scalar_tensor_tensor", out=g[:], in0=t2[:], scalar=-1.0, in1=al[:], op0=ALU.mult, op1=ALU.mult, **G)
            d_pb = [Dep() for _ in range(5)]
            pgi = [0, 1, 2, 3, 5]
            pg2 = [bank[i][:, 0:nb * 16] for i in pgi]
            pg = [bank[i][:, 0:nb * 16].rearrange("p (a b) -> p a b", b=16) for i in pgi]
            g2 = g[:].rearrange("p a b -> p (a b)")
            S.op("pe", "matmul", reads=[d_g, d_c], writes=[d_pb[0], d_bank[0]], signal=True, out=pg2[0], lhsT=cmask[:, 0, :], rhs=g2, start=True, stop=True)
            S.op("pe", "matmul", reads=[d_g, d_c], writes=[d_pb[4], d_bank[5]], signal=True, out=pg2[4], lhsT=cmask[:, 1, :], rhs=g2, start=True, stop=True)
            for i in range(1, 4):
                S.op("pe", "matmul", reads=[d_g, d_c], writes=[d_pb[i], d_bank[i]], signal=True,
                     out=pg2[i], lhsT=cmask[:, 1 + i, :], rhs=g2, start=True, stop=True)
            S.op("dve", "tensor_copy", reads=[d_pb[0]], writes=[d_gate, d_bank[0]], out=gc[:, :, 0:8], in_=pg[0][:, :, 0:8])
            S.op("dve", "tensor_copy", reads=[d_pb[4]], writes=[d_gate, d_bank[5]], out=gc[:, :, 8:16], in_=pg[4][:, :, 8:16])
            S.op("act", "activation", reads=[d_gate], writes=[d_gate], out=egc[:], in_=gc[:], func=AF.Exp)
            S.op("dve", "tensor_tensor", reads=[d_gate], writes=[d_gate], out=begc[:], in0=egc[:], in1=beta[:], op=ALU.mult)
            S.op("dve", "tensor_tensor", reads=[d_pb[1], d_gate], writes=[d_gate, d_bank[1]], out=edec[:], in0=pg[1], in1=gc[:], op=ALU.subtract)
            S.op("act", "activation", reads=[d_gate], writes=[d_gate], out=edec[:], in_=edec[:], func=AF.Exp)
            S.op("act", "activation", reads=[d_pb[2]], writes=[d_gate, d_bank[2]], out=sdec[0][:], in_=pg[2], func=AF.Exp)
            S.op("act", "activation", reads=[d_pb[3]], writes=[d_gate, d_bank[3]], out=sdec[1][:], in_=pg[3], func=AF.Exp)
            S.op("dve", "tensor_tensor", reads=[d_gate, d_g], writes=[d_g], out=r1[:], in0=gc[:], in1=lnb[:], op=ALU.add)
            d_rowst = Dep()
            prow = [(bank[4][0:nb, 0:128], Dep()), (bank[4][0:nb, 128:256], Dep()), (bank[4][0:nb, 256:384], Dep()), (bank[4][0:nb, 384:512], Dep())]
            for q in range(32):
                dh, which = q // 2, q % 2
                src = (r1 if which == 0 else gc)[:, :, dh]
                pr, d_pr = prow[q % 4]
                S.op("pe", "transpose", reads=[d_g, d_gate, d_const], writes=[d_pr, d_bank[4]], out=pr, in_=src, identity=identf[:])
                if q % 2 == 0:
                    S.op("act", "activation", reads=[d_pr], writes=[d_rowst, d_bank[4]], out=rowst[0:nb, q, :], in_=pr, func=AF.Copy)
                else:
                    S.op("dve", "tensor_copy", reads=[d_pr], writes=[d_rowst, d_bank[4]], out=rowst[0:nb, q, :], in_=pr)
            d_rows = Dep()
            S.dma("sp", ROWS[:, off:off + slen].rearrange("q (b i) -> b q i", i=128), rowst[0:nb], reads=[d_rowst], writes=[d_rows])
            S.phase_end()

        W = 7
        xin = T("cx", [128, slen + 4], BF16)
        d_xin = Dep()
        S.op("pool", "memset", writes=[d_xin], ap=xin[:, 0:2], constant=0.0)
        S.op("pool", "memset", writes=[d_xin], ap=xin[:, slen + 2:slen + 4], constant=0.0)
        CH = min(512, slen)
        caccs = Rot([(T("cacc%d" % i, [128, CH], F32), Dep()) for i in range(2)])
        ybf = T("cy", [128, slen], BF16)
        d_ybf = Dep()
        ytms = [(T("ytm%d" % i, [128, nb, 3, 128], BF16), Dep()) for i in range(2)]
        sss = [(T("css%d" % i, [128, nb, 2], F32), Dep()) for i in range(2)]
        junkc = T("cjunk", [128, 128], BF16)
        d_junk = Dep()
        oacc = T("oacc", [128, nb, 128], F32)
        d_oacc = Dep()
        oss = T("coss", [128, 2, nb], F32)
        d_oss = Dep()
        sgT = T("csg", [128, slen], BF16)
        d_sgT = Dep()
        ontm = Rot([(T("con%d" % i, [128, 128], BF16), Dep()) for i in range(2)])
        vnew = [[(T("cvn%d_%d" % (i, c), [128, 128], BF16), Dep()) for c in range(2)] for i in range(2)]
        Sf = [(T("cSf%d" % i, [128, 128], F32), Dep()) for i in range(2)]
        Sbf = [(T("cSb%d" % i, [128, 128], BF16), Dep()) for i in range(2)]

        class Slot:
            pass

        slots = []
        for si in range(W):
            sl = Slot()
            mk = lambda name, shape, dt: (T("%s_s%d" % (name, si), shape, dt), Dep())
            sl.arg = mk("carg", [128, 256], F32)
            sl.at = mk("cat", [128, 128], F32)
            sl.it = mk("cit", [128, 128], BF16)
            sl.asb = mk("casb", [128, 128], F32)
            sl.p0 = mk("cp0", [128, 128], F32)
            sl.bp = [mk("cbp%d" % i, [128, 256], F32) for i in range(2)]
            sl.bt = [mk("cbt%d" % i, [128, 128], F32) for i in range(2)]
            sl.p4 = mk("cp4", [128, 128], F32)
            sl.pbf = mk("cpbf", [128, 128], BF16)
            sl.kbg = mk("ckbg", [128, 128], BF16)
            sl.vb = mk("cvb", [128, 128], BF16)
            sl.kd = mk("ckd", [128, 128], BF16)
            sl.yqg = mk("cyqg", [128, 128], BF16)
            sl.us = mk("cus", [128, 128], F32)
            sl.wt = mk("cwt", [128, 128], BF16)
            sl.qg = mk("cqg", [128, 256], BF16)
            slots.append(sl)

        class Reg:
            def __init__(self, b):
                self.f = bank[b]
                self.h = bankb[b]
                self.d = d_bank[b]

            def W(self):
                return [self.d]

        p_o = [Reg(2), Reg(3)]
        pany = Rot([Reg(b) for b in (0, 1, 4, 5, 6, 7)])
        kkalls = [(T("kkall%d" % i, [128, nb, 256], BF16), Dep()) for i in range(2)]
        kqs = Rot([(T("ckq%d" % i, [128, 256], BF16), Dep()) for i in range(1)])
        chain_done = [0, 0]

        def blockdir(sl, h, dr, it, ytm, d_ytm, kkall, d_kkall):
            blk = it if dr == 0 else nb - 1 - it
            dh = dr * 8 + h
            arg, d_arg = sl.arg
            S.dma("sp", arg[:].rearrange("p (a b) -> p a b", b=128), bass.AP(ROWS.tensor, (2 * dh) * NTOK + off + blk * 128, [[0, 128], [NTOK, 2], [1, 128]]),
                  reads=[d_rows], writes=[d_arg])
            kbg, d_kbg = sl.kbg
            vb, d_vb = sl.vb
            kd, d_kd = sl.kd
            yqg, d_yqg = sl.yqg
            S.op("act", "activation", reads=[d_ytm, d_gate], writes=[d_yqg], out=yqg[:], in_=ytm[:, blk, 0, :], func=AF.Copy, scale=egc[:, blk, dh:dh + 1])
            S.op("dve", "tensor_scalar", reads=[d_ytm, d_gate], writes=[d_kbg], out=kbg[:], in0=ytm[:, blk, 1, :], scalar1=begc[:, blk, dh:dh + 1], scalar2=None, op0=ALU.mult)
            yield
            gg = arg
            d_gg = d_arg
            pq = pany.next()
            S.op("pe", "matmul", reads=[d_yqg, d_c], writes=pq.W(), signal=True, out=pq.f[:, 0:256], lhsT=yqg[:], rhs=ident2b[:], start=True, stop=True)
            qg, d_qg = sl.qg
            S.op("act", "activation", reads=[pq.d], writes=[d_qg, pq.d], out=qg[:], in_=pq.f[:, 0:256], func=AF.Copy)
            S.op("act", "activation", reads=[d_ytm, d_gate], writes=[d_vb], out=vb[:], in_=ytm[:, blk, 2, :], func=AF.Copy, scale=beta[:, blk, dh:dh + 1])
            S.op("dve", "tensor_scalar", reads=[d_ytm, d_gate], writes=[d_kd], out=kd[:], in0=ytm[:, blk, 1, :], scalar1=edec[:, blk, dh:dh + 1], scalar2=None, op0=ALU.mult)
            yield
            S.op("dve", "scalar_tensor_tensor", reads=[d_arg, d_gate, d_c], writes=[d_arg], out=arg[:], in0=arg[:],
                 scalar=gc[:, blk, dh:dh + 1], in1=negm[:, dr, :], op0=ALU.subtract, op1=ALU.add)
            S.op("act", "activation", reads=[d_arg], writes=[d_arg], out=arg[:], in_=arg[:], func=AF.Exp)
            yield
            at, d_at = sl.at
            itt, d_it = sl.it
            S.op("dve", "tensor_tensor", reads=[d_kkall, d_gg], writes=[d_at], out=at[:], in0=kkall[:, blk, 0:128], in1=gg[:, 0:128], op=ALU.mult)
            S.op("pool", "tensor_tensor", reads=[d_kkall, d_gg], writes=[d_it], out=itt[:], in0=kkall[:, blk, 128:256], in1=gg[:, 128:256], op=ALU.mult)
            pA = pany.next()
            S.op("pe", "transpose", reads=[d_at, d_const], writes=pA.W(), signal=True, out=pA.f[:, 0:128], in_=at[:], identity=identf[:])
            asb, d_asb = sl.asb
            S.op("act", "activation", reads=[pA.d], writes=[d_asb, pA.d], out=asb[:], in_=pA.f[:, 0:128], func=AF.Copy)
            p0, d_p0 = sl.p0
            S.op("pool", "tensor_tensor", reads=[d_at, d_const], writes=[d_p0], out=p0[:], in0=identf[:], in1=at[:], op=ALU.subtract)
            yield
            bp1, d_bp1 = sl.bp[1]
            px = pany.next()
            S.op("pe", "matmul", reads=[d_asb, d_at], writes=px.W(), signal=True, out=px.f[:, 0:128], lhsT=asb[:], rhs=at[:], start=True, stop=True)
            py = pany.next()
            S.op("pe", "matmul", reads=[d_asb, d_at], writes=py.W(), signal=True, out=py.f[:, 0:128], lhsT=at[:], rhs=asb[:], start=True, stop=True)
            S.op("dve", "tensor_copy", reads=[px.d], writes=[d_bp1, px.d], out=bp1[:, 0:128], in_=px.f[:, 0:128])
            S.op("pool", "tensor_copy", reads=[d_p0], writes=[d_bp1], out=bp1[:, 128:256], in_=p0[:])
            bt, d_bt = sl.bt[1]
            S.op("act", "activation", reads=[py.d], writes=[d_bt, py.d], out=bt[:], in_=py.f[:, 0:128], func=AF.Copy)
            yield
            cur, d_cur = bp1, d_bp1
            for k in range(1, 4):
                nxt, d_nxt = sl.bp[(k + 1) % 2]
                px = pany.next()
                S.op("pe", "matmul", reads=[d_bt, d_cur], writes=px.W(), signal=True, out=px.f[:, 0:256], lhsT=bt[:], rhs=cur[:], start=True, stop=True)
                if k % 2 == 1:
                    S.op("dve", "tensor_copy", reads=[px.d], writes=[d_nxt, px.d], out=nxt[:], in_=px.f[:, 0:256])
                else:
                    S.op("act", "activation", reads=[px.d], writes=[d_nxt, px.d], out=nxt[:], in_=px.f[:, 0:256], func=AF.Copy)
                S.op("pool", "tensor_tensor", reads=[d_cur, d_nxt], writes=[d_nxt], out=nxt[:, 128:256], in0=nxt[:, 128:256], in1=cur[:, 128:256], op=ALU.add)
                yield
                py = pany.next()
                S.op("pe", "matmul", reads=[d_bt, d_cur], writes=py.W(), signal=True, out=py.f[:, 0:128], lhsT=cur[:, 0:128], rhs=bt[:], start=True, stop=True)
                nbt, d_nbt = sl.bt[(k + 1) % 2]
                if k % 2 == 1:
                    S.op("act", "activation", reads=[py.d], writes=[d_nbt, py.d], out=nbt[:], in_=py.f[:, 0:128], func=AF.Copy)
                else:
                    S.op("dve", "tensor_copy", reads=[py.d], writes=[d_nbt, py.d], out=nbt[:], in_=py.f[:, 0:128])
                cur, d_cur = nxt, d_nxt
                bt, d_bt = nbt, d_nbt
                yield
            px = pany.next()
            S.op("pe", "matmul", reads=[d_bt, d_cur], writes=px.W(), signal=True, out=px.f[:, 0:128], lhsT=bt[:], rhs=cur[:, 128:256], start=True, stop=True)
            py = pany.next()
            S.op("pe", "matmul", reads=[d_bt, d_cur], writes=py.W(), signal=True, out=py.f[:, 0:128], lhsT=cur[:, 0:128], rhs=bt[:], start=True, stop=True)
            p4, d_p4 = sl.p4
            S.op("dve", "tensor_tensor", reads=[px.d, d_cur], writes=[d_p4, px.d], out=p4[:], in0=px.f[:, 0:128], in1=cur[:, 128:256], op=ALU.add)
            bt5, d_bt5 = sl.bt[1]
            S.op("act", "activation", reads=[py.d], writes=[d_bt5, py.d], out=bt5[:], in_=py.f[:, 0:128], func=AF.Copy)
            yield
            px = pany.next()
            S.op("pe", "matmul", reads=[d_bt5, d_p4], writes=px.W(), signal=True, out=px.f[:, 0:128], lhsT=bt5[:], rhs=p4[:], start=True, stop=True)
            pbf, d_pbf = sl.pbf
            S.op("dve", "tensor_tensor", reads=[px.d, d_p4], writes=[d_pbf, px.d], out=pbf[:], in0=px.f[:, 0:128], in1=p4[:], op=ALU.add)
            yield
            pu = pany.next()
            S.op("pe", "matmul", reads=[d_pbf, d_vb], writes=pu.W(), signal=True, out=pu.f[:, 0:128], lhsT=pbf[:], rhs=vb[:], start=True, stop=True)
            us, d_us = sl.us
            S.op("act", "activation", reads=[pu.d], writes=[d_us, pu.d], out=us[:], in_=pu.f[:, 0:128], func=AF.Copy)
            pw = pany.next()
            S.op("pe", "matmul", reads=[d_pbf, d_kbg], writes=pw.W(), signal=True, out=pw.f[:, 0:128], lhsT=kbg[:], rhs=pbf[:], start=True, stop=True)
            wt, d_wt = sl.wt
            S.op("dve", "tensor_copy", reads=[pw.d], writes=[d_wt, pw.d], out=wt[:], in_=pw.f[:, 0:128])
            yield
            while chain_done[dr] < it:
                yield
            sf, d_sf = Sf[dr]
            sb, d_sb = Sbf[dr]
            po = p_o[dr]
            order = (0, 1) if dr == 0 else (1, 0)
            for n, c in enumerate(order):
                vn, d_vn = vnew[dr][c]
                pws = pany.next()
                rows = slice(64 * c, 64 * c + 64)
                S.op("pe", "matmul", reads=[d_wt, d_sb], writes=pws.W(), signal=True, out=pws.f[:, 0:128], lhsT=wt[:], rhs=sb[:], start=True, stop=True)
                S.op("pe", "matmul", reads=[d_qg, d_sb], writes=po.W(), signal=False, out=po.f[:, 0:128], lhsT=qg[:, c * 128:(c + 1) * 128], rhs=sb[:], start=(n == 0), stop=False)
                S.op("dve", "tensor_tensor", reads=[d_us, pws.d], writes=[d_vn, pws.d], out=vn[rows, :], in0=us[rows, :], in1=pws.f[rows, 0:128], op=ALU.subtract)
                yield
                psu = pany.next()
                S.op("pe", "matmul", reads=[d_kd, d_vn], writes=psu.W(), signal=True, out=psu.f[:, 0:128], lhsT=kd[:], rhs=vn[:], start=True, stop=True)
                S.op("pe", "matmul", reads=[d_it, d_vn], writes=po.W(), signal=(n == 1), out=po.f[:, 0:128], lhsT=itt[:], rhs=vn[:], start=False, stop=(n == 1))
                S.op("dve", "scalar_tensor_tensor", reads=[psu.d, d_sf, d_gate], writes=[d_sf, psu.d], out=sf[:], in0=sf[:], scalar=sdec[c][:, blk, dh:dh + 1], in1=psu.f[:, 0:128], op0=ALU.mult, op1=ALU.add)
                S.op("act", "activation", reads=[d_sf], writes=[d_sb], out=sb[:], in_=sf[:], func=AF.Copy)
                yield
            S.op("dve", "tensor_tensor", reads=[po.d, d_oacc], writes=[d_oacc, po.d], out=oacc[:, blk, :], in0=po.f[:, 0:128], in1=oacc[:, blk, :], op=ALU.add)
            chain_done[dr] = it + 1

        def prologue(h, buf):
            ytm, d_ytm = ytms[buf]
            ss, d_ss = sss[buf]
            for i in range(3):
                S.dma("sp", xin[:, 2:slen + 2], FT[24 + 8 * i + h, :, off:off + slen], writes=[d_xin])
                wcol = lambda j: cw[:, 8 * i + h, j:j + 1]
                for ch in range(slen // CH):
                    c0 = ch * CH
                    cacc, d_cacc = caccs.next()
                    S.op("dve", "tensor_scalar", reads=[d_xin, d_c], writes=[d_cacc], out=cacc[:], in0=xin[:, c0:c0 + CH], scalar1=wcol(0), scalar2=None, op0=ALU.mult)
                    yield
                    for j in range(1, 5):
                        S.op("dve", "scalar_tensor_tensor", reads=[d_xin, d_c, d_cacc], writes=[d_cacc], out=cacc[:], in0=xin[:, c0 + j:c0 + j + CH], scalar=wcol(j), in1=cacc[:], op0=ALU.mult, op1=ALU.add)
                        yield
                    S.op("act", "activation", reads=[d_cacc], writes=[d_ybf], out=ybf[:, c0:c0 + CH], in_=cacc[:], func=AF.Silu)
                    yield
                for b4 in range(nb // 4):
                    pt = pany.next()
                    ptv = pt.h[:, 0:512].rearrange("p (a b) -> p a b", b=128)
                    for j in range(4):
                        blk = b4 * 4 + j
                        S.op("pe", "transpose", reads=[d_ybf, d_const], writes=pt.W(), signal=(j == 3),
                             out=ptv[:, j, :], in_=ybf[:, blk * 128:(blk + 1) * 128], identity=identb[:])
                    if b4 % 2 == 0:
                        S.op("dve", "tensor_copy", reads=[pt.d], writes=[d_ytm, pt.d], out=ytm[:, b4 * 4:b4 * 4 + 4, i, :], in_=ptv)
                    else:
                        S.op("act", "activation", reads=[pt.d], writes=[d_ytm, pt.d], out=ytm[:, b4 * 4:b4 * 4 + 4, i, :], in_=ptv, func=AF.Copy)
                    yield
            for blk in range(nb):
                for i in range(2):
                    S.op("act", "activation", reads=[d_ytm], writes=[d_junk, d_ss], out=junkc[:], in_=ytm[:, blk, i, :], func=AF.Square, accum_out=ss[:, blk, i:i + 1])
                yield
            S.op("dve", "tensor_scalar", reads=[d_ss], writes=[d_ss], out=ss[:], in0=ss[:], scalar1=EPS, scalar2=None, op0=ALU.add)
            S.op("act", "activation", reads=[d_ss], writes=[d_ss], out=ss[:], in_=ss[:], func=AF.Sqrt)
            S.op("dve", "reciprocal", reads=[d_ss], writes=[d_ss], out=ss[:], in_=ss[:])
            S.op("dve", "tensor_scalar", reads=[d_ss], writes=[d_ss], out=ss[:, :, 0], in0=ss[:, :, 0], scalar1=qscale, scalar2=None, op0=ALU.mult)
            yield
            for blk in range(nb):
                S.op("dve", "tensor_scalar", reads=[d_ss, d_ytm], writes=[d_ytm], out=ytm[:, blk, 0, :], in0=ytm[:, blk, 0, :], scalar1=ss[:, blk, 0:1], scalar2=None, op0=ALU.mult)
                S.op("act", "activation", reads=[d_ss, d_ytm], writes=[d_ytm], out=ytm[:, blk, 1, :], in_=ytm[:, blk, 1, :], func=AF.Copy, scale=ss[:, blk, 1:2])
                yield
            kkall, d_kkall = kkalls[buf]
            for blk in range(nb):
                pk = pany.next()
                S.op("pe", "transpose", reads=[d_ytm, d_const], writes=pk.W(), signal=False, out=pk.h[:, 0:128], in_=ytm[:, blk, 1, :], identity=identb[:])
                S.op("pe", "transpose", reads=[d_ytm, d_const], writes=pk.W(), signal=True, out=pk.h[:, 128:256], in_=ytm[:, blk, 0, :], identity=identb[:])
                kq, d_kq = kqs.next()
                S.op("act", "activation", reads=[pk.d], writes=[d_kq, pk.d], out=kq[:], in_=pk.h[:, 0:256], func=AF.Copy)
                yield
                pkk = pany.next()
                S.op("pe", "matmul", reads=[d_kq], writes=pkk.W(), signal=True, out=pkk.f[:, 0:256], lhsT=kq[:, 0:128], rhs=kq[:], start=True, stop=True)
                S.op("dve", "tensor_copy", reads=[pkk.d], writes=[d_kkall, pkk.d], out=kkall[:, blk, :], in_=pkk.f[:, 0:256])
                yield

        hl = list(heads)
        for _ in prologue(hl[0], 0):
            pass
        for hi, h in enumerate(hl):
            ytm, d_ytm = ytms[hi % 2]
            kkall, d_kkall = kkalls[hi % 2]
            bg = prologue(hl[hi + 1], (hi + 1) % 2) if hi + 1 < len(hl) else None
            S.op("pool", "memset", reads=[], writes=[d_oacc], ap=oacc[:].rearrange("p a b -> p (a b)"), constant=0.0)
            for dr in range(2):
                S.op("pool", "memset", writes=[Sf[dr][1]], ap=Sf[dr][0][:], constant=0.0)
                S.op("pool", "memset", writes=[Sbf[dr][1]], ap=Sbf[dr][0][:], constant=0.0)
                for c in range(2):
                    S.op("pool", "memset", writes=[vnew[dr][c][1]], ap=vnew[dr][c][0][:], constant=0.0)
            chain_done[0] = 0
            chain_done[1] = 0
            pending = []
            for it in range(nb):
                pending.append((0, it))
                pending.append((1, it))
            pending.reverse()
            active = []
            free = list(range(W))
            while pending or active:
                if pending and free:
                    dr, it = pending.pop()
                    si = free.pop()
                    active.append((si, blockdir(slots[si], h, dr, it, ytm, d_ytm, kkall, d_kkall)))
                for ent in list(active):
                    try:
                        next(ent[1])
                    except StopIteration:
                        active.remove(ent)
                        free.append(ent[0])
                if bg is not None:
                    try:
                        next(bg)
                    except StopIteration:
                        bg = None
            if bg is not None:
                for _ in bg:
                    pass
            S.dma("sp", sgT[:], FT[48 + h, :, off:off + slen], writes=[d_sgT])
            S.op("act", "activation", reads=[d_sgT], writes=[d_sgT], out=sgT[:], in_=sgT[:], func=AF.Silu)
            for blk in range(nb):
                S.op("act", "activation", reads=[d_oacc], writes=[d_junk, d_oss], out=junkc[:], in_=oacc[:, blk, :], func=AF.Square, accum_out=oss[:, 0, blk:blk + 1])
            S.op("dve", "tensor_scalar", reads=[d_oss], writes=[d_oss], out=oss[:, 1, :], in0=oss[:, 0, :], scalar1=1.0 / 128, scalar2=EPS, op0=ALU.mult, op1=ALU.add)
            S.op("act", "activation", reads=[d_oss], writes=[d_oss], out=oss[:, 1, :], in_=oss[:, 1, :], func=AF.Sqrt)
            S.op("dve", "reciprocal", reads=[d_oss], writes=[d_oss], out=oss[:, 1, :], in_=oss[:, 1, :])
            for blk in range(nb):
                on, d_on = ontm.next()
                S.op("act", "activation", reads=[d_oacc, d_oss], writes=[d_on], out=on[:], in_=oacc[:, blk, :], func=AF.Copy, scale=oss[:, 1, blk:blk + 1])
                pt = pany.next()
                S.op("pe", "transpose", reads=[d_on, d_const], writes=pt.W(), signal=True, out=pt.h[:, 0:128], in_=on[:], identity=identb[:])
                S.op("dve", "scalar_tensor_tensor", reads=[pt.d, d_c, d_sgT], writes=[d_sgT, pt.d], out=sgT[:, blk * 128:(blk + 1) * 128], in0=pt.h[:, 0:128], scalar=dng[:, 0:1],
                     in1=sgT[:, blk * 128:(blk + 1) * 128], op0=ALU.mult, op1=ALU.mult)
            S.dma("pool", MT[8 + h, :, off:off + slen], sgT[:], reads=[d_sgT])
        S.phase_end()


def _host_maps(xs_per_core, w_in, w_out, norm_in_gain, conv_w, a_log, dt_bias, dn_gain, final_gain):
    c = _consts()
    gin = np.ascontiguousarray(norm_in_gain.reshape(16, 128).T)
    cw = np.ascontiguousarray(conv_w.reshape(5, 24, 128).transpose(2, 1, 0).reshape(128, 120))
    al = np.ascontiguousarray(np.broadcast_to(a_log.reshape(1, 1, 16), (128, 32, 16)).reshape(128, 512))
    db = np.ascontiguousarray(np.broadcast_to(dt_bias.reshape(1, 1, 16), (128, 32, 16)).reshape(128, 512))
    dg = np.ascontiguousarray(dn_gain.reshape(128, 1))
    fg = np.ascontiguousarray(np.broadcast_to(final_gain.reshape(1, D), (128, D)))
    maps = []
    for xc in xs_per_core:
        m = {"x": xc, "w_in": w_in, "w_out": w_out, "gin": gin, "convw": cw, "alog": al, "dtb": db,
             "dngain": dg, "fgain": fg}
        m.update(c)
        maps.append(m)
    return maps


PHASES = "0ABCD"


def kernel(x_prompt, x_sample, norm_in_gain, w_in, conv_w, a_log, dt_bias, delta_norm_gain, w_out, final_norm_gain):
    f = lambda a: np.ascontiguousarray(np.asarray(a, dtype=np.float32))
    x_prompt, x_sample = f(x_prompt), f(x_sample)
    seqs = [4096, 2048, 2048]
    xs = []
    for c in range(8):
        xs.append(np.ascontiguousarray(np.concatenate(
            [x_prompt[c], x_sample[2 * c], x_sample[2 * c + 1]], axis=0)))
    maps = _host_maps(xs, f(w_in)[0], f(w_out)[0], f(norm_in_gain)[0], f(conv_w)[0], f(a_log)[0],
                      f(dt_bias)[0], f(delta_norm_gain)[0], f(final_norm_gain))
    nc = build(seqs, phases=PHASES)
    res = run_bass_kernel_spmd(nc, maps, core_ids=list(range(8)))
    yp = np.empty((8, 4096, D), np.float32)
    ys = np.empty((16, 2048, D), np.float32)
    for c in range(8):
        yc = res.results[c]["y"]
        yp[c] = yc[0:4096]
        ys[2 * c] = yc[4096:6144]
        ys[2 * c + 1] = yc[6144:8192]
    return (yp, ys)
```

```python
import contextlib
import numpy as np
import concourse.bass as bass
import concourse.mybir as mybir
from concourse.bass_utils import run_bass_kernel_spmd

F32 = mybir.dt.float32
BF16 = mybir.dt.bfloat16
AF = mybir.ActivationFunctionType
ALU = mybir.AluOpType

D = 2048
PW = 8224
NDS = 40
EPS = 1e-6
NEG = -30000.0
PATTERNS = (1, 4, 16)


class Dep:
    __slots__ = ("w", "r", "const")

    def __init__(self, const=False):
        self.w = None
        self.r = {}
        self.const = const


class Sched:
    def __init__(self, nc, stack):
        self.nc = nc
        self.names = ["pe", "act", "dve", "pool", "sp"]
        self.sem = {}
        self.cnt = {}
        self.waited = {}
        for e in self.names:
            self.sem[e] = stack.enter_context(nc.semaphore("s_" + e))
            self.cnt[e] = 0
            self.waited[e] = {}
        self.dsem = [stack.enter_context(nc.semaphore("d%d" % i)) for i in range(NDS)]
        self.dcnt = [0] * NDS
        self.ndma = 0
        self.ndma_sw = 0
        self.allsems = {}
        for e in self.names:
            self.allsems[id(self.sem[e])] = self.sem[e]
        for s in self.dsem:
            self.allsems[id(s)] = s
        self.nins = 0
        self.q = {e: [] for e in self.names}

    def flush(self):
        nc = self.nc
        q = self.q

        def replay(e, items):
            for it in items:
                if it[0] == "w":
                    e.wait_ge(it[1], it[2])
                else:
                    _, name, kw, inc = it
                    ins = getattr(e, name)(**kw)
                    if inc is not None:
                        ins.then_inc(inc[0], inc[1])

        with nc.Block() as block:
            @block.tensor
            def _(e):
                replay(e, q["pe"])

            @block.scalar
            def _(e):
                replay(e, q["act"])

            @block.vector
            def _(e):
                replay(e, q["dve"])

            @block.gpsimd
            def _(e):
                replay(e, q["pool"])

            @block.sync
            def _(e):
                replay(e, q["sp"])
        self.q = {e: [] for e in self.names}

    def _collect(self, eng, reads, writes, extra=()):
        waits = {}

        def need(ev):
            if ev is None:
                return
            sem, val, src = ev
            if src == "pe" and eng == "pe":
                return
            k = id(sem)
            if waits.get(k, 0) < val:
                waits[k] = val

        for d in reads:
            need(d.w)
        for d in writes:
            need(d.w)
            for ev in d.r.values():
                need(ev)
        for ev in extra:
            need(ev)
        out = []
        wd = self.waited[eng]
        for k, val in waits.items():
            if wd.get(k, 0) >= val:
                continue
            wd[k] = val
            out.append((self.allsems[k], val))
        return out

    def _record(self, ev, reads, writes):
        k = id(ev[0])
        for d in reads:
            if d.const:
                continue
            old = d.r.get(k)
            if old is None or old[1] < ev[1]:
                d.r[k] = ev
        for d in writes:
            d.w = ev
            d.r = {}

    def op(self, eng, name, reads=(), writes=(), signal=True, **kw):
        q = self.q[eng]
        for sem, val in self._collect(eng, reads, writes):
            q.append(("w", sem, val))
        self.nins += 1
        if signal:
            self.cnt[eng] += 1
            q.append(("i", name, kw, (self.sem[eng], 1)))
            ev = (self.sem[eng], self.cnt[eng], eng)
        else:
            assert eng == "pe"
            q.append(("i", name, kw, None))
            ev = (self.sem[eng], self.cnt[eng] + 1, eng)
        self._record(ev, reads, writes)
        return ev

    def dma(self, q, out, in_, reads=(), writes=(), **dkw):
        if q == "sp":
            s = self.ndma % (NDS - 8)
            self.ndma += 1
        else:
            s = (NDS - 8) + self.ndma_sw % 8
            self.ndma_sw += 1
        sem = self.dsem[s]
        prev = self.dcnt[s]
        self.dcnt[s] += 16
        val = self.dcnt[s]
        extra = [(sem, prev, "dma")] if prev > 0 else []
        for sm, v in self._collect(q, reads, writes, extra):
            self.q[q].append(("w", sm, v))
        self.q[q].append(("i", "dma_start", dict(out=out, in_=in_, **dkw), (sem, 16)))
        self.nins += 1
        ev = (sem, val, "dma")
        self._record(ev, reads, writes)
        return ev

    def barrier(self):
        for eng in self.names:
            wd = self.waited[eng]
            for src in self.names:
                if src == eng or self.cnt[src] == 0:
                    continue
                k = id(self.sem[src])
                if wd.get(k, 0) < self.cnt[src]:
                    wd[k] = self.cnt[src]
                    self.q[eng].append(("w", self.sem[src], self.cnt[src]))
            for s in range(NDS):
                if self.dcnt[s] == 0:
                    continue
                k = id(self.dsem[s])
                if wd.get(k, 0) < self.dcnt[s]:
                    wd[k] = self.dcnt[s]
                    self.q[eng].append(("w", self.dsem[s], self.dcnt[s]))

    def phase_end(self):
        self.barrier()
        self.flush()


_UID = [0]


def _uniq(name):
    _UID[0] += 1
    return "%s_u%d" % (name, _UID[0])


class Rot:
    def __init__(self, items):
        self.items = items
        self.i = 0

    def next(self):
        it = self.items[self.i % len(self.items)]
        self.i += 1
        return it


def _consts():
    c = {}
    c["ident"] = np.eye(128, dtype=np.float32)
    k = np.arange(128)[:, None]
    q = np.arange(128)[None, :]
    slopes = np.array([2.0 ** (-8.0 * (h + 1) / 8) for h in range(8)], np.float64)
    et = np.zeros((128, 8, 3, 2, 128), np.float32)
    for h in range(8):
        for p, d in enumerate(PATTERNS):
            sc_ = 128.0 ** -0.5
            lo = np.where(k >= q, -slopes[h] * d * np.abs(64 + q - k) / sc_, NEG)
            hi = np.where(k <= q, -slopes[h] * d * np.abs(q - k - 64) / sc_, NEG)
            et[:, h, p, 0, :] = lo
            et[:, h, p, 1, :] = hi
    c["etab"] = et.reshape(128, 8 * 3 * 2 * 128)
    j = np.arange(128)[:, None]
    i = np.arange(128)[None, :]
    same = (j // 64) == (i // 64)
    nm = np.zeros((128, 2, 2, 128), np.float32)
    nm[:, 0, 0, :] = np.where(same & (j < i), 0.0, NEG)
    nm[:, 0, 1, :] = np.where(same & (j <= i), 0.0, NEG)
    nm[:, 1, 0, :] = np.where(same & (j > i), 0.0, NEG)
    nm[:, 1, 1, :] = np.where(same & (j >= i), 0.0, NEG)
    c["negmask"] = nm.reshape(128, 512)
    cm = np.zeros((128, 5, 128), np.float32)
    cm[:, 0, :] = (same & (j <= i))
    cm[:, 1, :] = (same & (j >= i))
    cm[:, 2, :] = same
    cm[:, 3, :] = (j < 64)
    cm[:, 4, :] = (j >= 64)
    c["cmask"] = cm.reshape(128, 640)
    i2 = np.zeros((128, 2, 128), np.float32)
    for t in range(128):
        i2[t, t // 64, t] = 1.0
    c["ident2"] = i2.reshape(128, 256)
    hm = np.zeros((128, 2), np.float32)
    hm[:64, 0] = 1.0
    hm[64:, 1] = 1.0
    c["hmask"] = hm
    return c


def build(seqs, phases="0ABCD", dbg=False, ext_in=False):
    NTOK = sum(seqs)
    assert ext_in or (NTOK % 2048 == 0 and all(s % 2048 == 0 for s in seqs))
    NSB = NTOK // 2048
    nc = bass.Bass("TRN2", target_bir_lowering=False)
    inp = lambda name, shape: nc.dram_tensor(name, shape, F32, kind="ExternalInput").ap()
    x = inp("x", [NTOK, D])
    w_in = inp("w_in", [D, PW])
    w_out = inp("w_out", [D, D])
    gin = inp("gin", [128, 16])
    convw = inp("convw", [128, 24 * 5])
    alog = inp("alog", [128, 32 * 16])
    dtb = inp("dtb", [128, 32 * 16])
    dngain = inp("dngain", [128, 1])
    fgain = inp("fgain", [128, D])
    c_ident = inp("ident", [128, 128])
    c_etab = inp("etab", [128, 8 * 3 * 2 * 128])
    c_negmask = inp("negmask", [128, 512])
    c_cmask = inp("cmask", [128, 640])
    c_ident2 = inp("ident2", [128, 256])
    c_hmask = inp("hmask", [128, 2])
    okind = "ExternalOutput"
    y = nc.dram_tensor("y", [NTOK, D], F32, kind=okind).ap()
    skind = "ExternalOutput" if dbg else "Internal"
    winb = nc.dram_tensor("winb", [128, 16, PW], BF16).ap()
    ikind = "ExternalInput" if ext_in else skind
    FT = nc.dram_tensor("FT", [56, 128, NTOK], BF16, kind=ikind).ap()
    AV = nc.dram_tensor("AV", [NTOK, 1024], BF16, kind=ikind).ap()
    GB = nc.dram_tensor("GB", [NTOK, 32], F32, kind=ikind).ap()
    ROWS = nc.dram_tensor("ROWS", [32, NTOK], F32, kind=skind).ap()
    MT = nc.dram_tensor("MT", [16, 128, NTOK], BF16, kind=skind).ap()

    with contextlib.ExitStack() as gst:
        S = Sched(nc, gst)
        GT = lambda name, shape, dt: gst.enter_context(nc.sbuf_tensor(name, shape, dt))
        identf = GT("identf", [128, 128], F32)
        identb = GT("identb", [128, 128], BF16)
        onesb = GT("onesb", [128, 128], BF16)
        d_const = Dep(const=True)
        S.dma("sp", identf[:], c_ident, writes=[d_const])
        S.op("dve", "tensor_copy", reads=[d_const], writes=[d_const], out=identb[:], in_=identf[:])
        S.op("dve", "memset", writes=[d_const], ap=onesb[:], constant=1.0)
        S.phase_end()

        if "0" in phases:
            _phase0(nc, S, w_in, gin, winb)
        if "A" in phases:
            _phaseA(nc, S, x, winb, FT, AV, GB, NSB, identb, d_const)
        off = 0
        for si, slen in enumerate(seqs):
            if "B" in phases:
                _phaseB(nc, S, FT, AV, MT, off, slen, c_etab, onesb, identb, d_const)
            if "C" in phases:
                _phaseC(nc, S, FT, GB, ROWS, MT, off, slen, convw, alog, dtb, dngain,
                        c_negmask, c_cmask, c_ident2, c_hmask, identf, identb, d_const)
            off += slen
        if "D" in phases:
            _phaseD(nc, S, x, w_out, MT, fgain, y, NTOK, use_mixed=("B" in phases or "C" in phases),
                    heads=(range(16) if ("B" in phases and "C" in phases) else (range(8) if "B" in phases else range(8, 16))))
        S.phase_end()
    return nc


def _phase0(nc, S, w_in, gin, winb):
    with contextlib.ExitStack() as st:
        T = lambda name, shape, dt: st.enter_context(nc.sbuf_tensor(_uniq(name), shape, dt))
        gint = T("gint", [128, 16], F32)
        d_g = Dep()
        S.dma("sp", gint[:], gin, writes=[d_g])
        wf = Rot([(T("p0wf%d" % i, [128, 2056], F32), Dep()) for i in range(3)])
        wb = Rot([(T("p0wb%d" % i, [128, 2056], BF16), Dep()) for i in range(3)])
        n = 0
        for kc in range(16):
            for sl in range(4):
                f, d_f = wf.next()
                b, d_b = wb.next()
                S.dma("sp", f[:], w_in[kc * 128:(kc + 1) * 128, sl * 2056:(sl + 1) * 2056], writes=[d_f])
                if n % 2 == 0:
                    S.op("act", "activation", reads=[d_f, d_g], writes=[d_b], out=b[:], in_=f[:], func=AF.Copy, scale=gint[:, kc:kc + 1])
                else:
                    S.op("dve", "tensor_scalar", reads=[d_f, d_g], writes=[d_b], out=b[:], in0=f[:], scalar1=gint[:, kc:kc + 1], scalar2=None, op0=ALU.mult)
                S.dma("pool", winb[:, kc, sl * 2056:(sl + 1) * 2056], b[:], reads=[d_b])
                n += 1
        S.phase_end()


def _ft_index(cb):
    if cb < 16:
        return cb
    if cb < 24:
        return None
    return cb - 8


def _phaseA(nc, S, x, winb, FT, AV, GB, NSB, identb, d_const):
    with contextlib.ExitStack() as st:
        T = lambda name, shape, dt: st.enter_context(nc.sbuf_tensor(_uniq(name), shape, dt))
        P = lambda name, shape, dt: st.enter_context(nc.psum_tensor(_uniq(name), shape, dt))
        hTs = [(T("hT%d" % i, [128, 16, 2048], BF16), Dep()) for i in range(2)]
        xts = Rot([(T("xt%d" % i, [128, D], F32), Dep()) for i in range(2)])
        hbs = Rot([(T("hb%d" % i, [128, D], BF16), Dep()) for i in range(2)])
        junk = T("junkA", [128, D], BF16)
        d_junk = Dep()
        sms = Rot([(T("smA%d" % i, [128, 2], F32), Dep()) for i in range(2)])
        wsl = Rot([(T("wsl%d" % i, [128, 16, 512], BF16), Dep()) for i in range(2)])
        stg = Rot([(T("stg%d" % i, [128, 2048], BF16), Dep()) for i in range(2)])
        stv = Rot([(T("stv%d" % i, [128, 512], BF16), Dep()) for i in range(3)])
        gbt = T("gbt", [128, 16, 32], F32)
        d_gbt = Dep()
        wtail = T("wtail", [128, 16, 32], BF16)
        d_wtail = Dep()
        ptr = Rot([(P("ptr%d" % i, [128, 4, 128], BF16), Dep()) for i in range(2)])
        pac = Rot([(P("pac%d" % i, [128, 512], F32), Dep()) for i in range(6)])
        S.dma("sp", wtail[:], winb[:, :, 8192:8224], writes=[d_wtail])
        ne = 0

        def a1(sb):
            t0 = sb * 2048
            hT, d_hT = hTs[sb % 2]
            for tt in range(16):
                xt, d_xt = xts.next()
                hb, d_hb = hbs.next()
                sm, d_sm = sms.next()
                S.dma("sp", xt[:], x[t0 + tt * 128:t0 + (tt + 1) * 128, :], writes=[d_xt])
                S.op("act", "activation", reads=[d_xt], writes=[d_junk, d_sm], out=junk[:], in_=xt[:], func=AF.Square, accum_out=sm[:, 0:1])
                S.op("dve", "tensor_scalar", reads=[d_sm], writes=[d_sm], out=sm[:, 1:2], in0=sm[:, 0:1], scalar1=1.0 / D, scalar2=EPS, op0=ALU.mult, op1=ALU.add)
                S.op("act", "activation", reads=[d_sm], writes=[d_sm], out=sm[:, 1:2], in_=sm[:, 1:2], func=AF.Sqrt)
                S.op("dve", "reciprocal", reads=[d_sm], writes=[d_sm], out=sm[:, 1:2], in_=sm[:, 1:2])
                S.op("act", "activation", reads=[d_xt, d_sm], writes=[d_hb], out=hb[:], in_=xt[:], func=AF.Copy, scale=sm[:, 1:2])
                for g in range(4):
                    pt, d_pt = ptr.next()
                    for j in range(4):
                        kc = g * 4 + j
                        S.op("pe", "transpose", reads=[d_hb, d_const], writes=[d_pt], signal=(j == 3),
                             out=pt[:, j, :], in_=hb[:, kc * 128:(kc + 1) * 128], identity=identb[:])
                    S.op("dve", "tensor_copy", reads=[d_pt], writes=[d_hT],
                         out=hT[:, g * 4:(g + 1) * 4, tt * 128:(tt + 1) * 128], in_=pt[:])
                yield

        for _ in a1(0):
            pass
        for sb in range(NSB):
            t0 = sb * 2048
            hT, d_hT = hTs[sb % 2]
            bg = a1(sb + 1) if sb + 1 < NSB else None
            for slab in range(16):
                if bg is not None:
                    try:
                        next(bg)
                    except StopIteration:
                        bg = None
                ws, d_ws = wsl.next()
                S.dma("sp", ws[:], winb[:, :, slab * 512:(slab + 1) * 512], writes=[d_ws])
                if 4 <= slab < 6:
                    for tt in range(16):
                        pa, d_pa = pac.next()
                        for kc in range(16):
                            S.op("pe", "matmul", reads=[d_hT, d_ws], writes=[d_pa], signal=(kc == 15),
                                 out=pa[:], lhsT=hT[:, kc, tt * 128:(tt + 1) * 128], rhs=ws[:, kc, :], start=(kc == 0), stop=(kc == 15))
                        sv, d_sv = stv.next()
                        if ne % 2 == 0:
                            S.op("act", "activation", reads=[d_pa], writes=[d_sv], out=sv[:], in_=pa[:], func=AF.Copy)
                        else:
                            S.op("dve", "tensor_copy", reads=[d_pa], writes=[d_sv], out=sv[:], in_=pa[:])
                        ne += 1
                        S.dma("pool", AV[t0 + tt * 128:t0 + (tt + 1) * 128, (slab - 4) * 512:(slab - 3) * 512], sv[:], reads=[d_sv])
                    continue
                for cbi in range(4):
                    cb = slab * 4 + cbi
                    pas = [pac.next() for _ in range(4)]
                    for kc in range(16):
                        for tb in range(4):
                            pa, d_pa = pas[tb]
                            S.op("pe", "matmul", reads=[d_hT, d_ws], writes=[d_pa], signal=(kc == 15),
                                 out=pa[:], lhsT=ws[:, kc, cbi * 128:(cbi + 1) * 128], rhs=hT[:, kc, tb * 512:(tb + 1) * 512],
                                 start=(kc == 0), stop=(kc == 15))
                    sg, d_sg = stg.next()
                    for tb in range(4):
                        pa, d_pa = pas[tb]
                        if ne % 2 == 0:
                            S.op("act", "activation", reads=[d_pa], writes=[d_sg], out=sg[:, tb * 512:(tb + 1) * 512], in_=pa[:], func=AF.Copy)
                        else:
                            S.op("dve", "tensor_copy", reads=[d_pa], writes=[d_sg], out=sg[:, tb * 512:(tb + 1) * 512], in_=pa[:])
                        ne += 1
                    S.dma("pool", FT[_ft_index(cb), :, t0:t0 + 2048], sg[:], reads=[d_sg])
            if bg is not None:
                for _ in bg:
                    pass
            for tt in range(16):
                pa, d_pa = pac.next()
                for kc in range(16):
                    S.op("pe", "matmul", reads=[d_hT, d_wtail], writes=[d_pa], signal=(kc == 15),
                         out=pa[:, 0:32], lhsT=hT[:, kc, tt * 128:(tt + 1) * 128], rhs=wtail[:, kc, :], start=(kc == 0), stop=(kc == 15))
                S.op("dve", "tensor_copy", reads=[d_pa], writes=[d_gbt], out=gbt[:, tt, :], in_=pa[:, 0:32])
            for t8 in range(2):
                S.dma("pool", GB[t0 + t8 * 1024:t0 + (t8 + 1) * 1024, :].rearrange("(t p) c -> p t c", p=128), gbt[:, t8 * 8:(t8 + 1) * 8, :], reads=[d_gbt])
        S.phase_end()


def _phaseD(nc, S, x, w_out, MT, fgain, y, NTOK, use_mixed, heads):
    heads = list(heads)
    with contextlib.ExitStack() as st:
        T = lambda name, shape, dt: st.enter_context(nc.sbuf_tensor(_uniq(name), shape, dt))
        P = lambda name, shape, dt: st.enter_context(nc.psum_tensor(_uniq(name), shape, dt))
        fg = T("fg", [128, D], F32)
        d_fg = Dep()
        S.dma("sp", fg[:], fgain, writes=[d_fg])
        wo = T("wo", [128, 16, D], BF16)
        d_wo = Dep()
        if use_mixed:
            wf = Rot([(T("dwf%d" % i, [128, D], F32), Dep()) for i in range(2)])
            for kc in range(16):
                f, d_f = wf.next()
                S.dma("sp", f[:], w_out[kc * 128:(kc + 1) * 128, :], writes=[d_f])
                if kc % 2 == 0:
                    S.op("act", "activation", reads=[d_f], writes=[d_wo], out=wo[:, kc, :], in_=f[:], func=AF.Copy)
                else:
                    S.op("dve", "tensor_copy", reads=[d_f], writes=[d_wo], out=wo[:, kc, :], in_=f[:])
        xts = Rot([(T("dxt%d" % i, [128, D], F32), Dep()) for i in range(2)])
        yts = Rot([(T("dyt%d" % i, [128, D], F32), Dep()) for i in range(2)])
        mts = Rot([(T("dmt%d" % i, [128, 16, 512], BF16), Dep()) for i in range(2)])
        junk = T("junkD", [128, D], BF16)
        d_junk = Dep()
        sms = Rot([(T("smD%d" % i, [128, 2], F32), Dep()) for i in range(2)])
        pac = Rot([(P("dpac%d" % i, [128, 512], F32), Dep()) for i in range(6)])
        for t4 in range(NTOK // 512):
            if use_mixed:
                mt, d_mt = mts.next()
                for h in heads:
                    S.dma("sp", mt[:, h, :], MT[h, :, t4 * 512:(t4 + 1) * 512], writes=[d_mt])
            for ti in range(4):
                tt = t4 * 4 + ti
                xt, d_xt = xts.next()
                yt, d_yt = yts.next()
                sm, d_sm = sms.next()
                S.dma("sp", xt[:], x[tt * 128:(tt + 1) * 128, :], writes=[d_xt])
                if use_mixed:
                    for cb in range(4):
                        pa, d_pa = pac.next()
                        for n, h in enumerate(heads):
                            S.op("pe", "matmul", reads=[d_mt, d_wo], writes=[d_pa], signal=(n == len(heads) - 1),
                                 out=pa[:], lhsT=mt[:, h, ti * 128:(ti + 1) * 128], rhs=wo[:, h, cb * 512:(cb + 1) * 512],
                                 start=(n == 0), stop=(n == len(heads) - 1))
                        S.op("dve", "tensor_tensor", reads=[d_pa, d_xt], writes=[d_yt],
                             out=yt[:, cb * 512:(cb + 1) * 512], in0=pa[:], in1=xt[:, cb * 512:(cb + 1) * 512], op=ALU.add)
                    src, d_src = yt, d_yt
                else:
                    src, d_src = xt, d_xt
                S.op("act", "activation", reads=[d_src], writes=[d_junk, d_sm], out=junk[:], in_=src[:], func=AF.Square, accum_out=sm[:, 0:1])
                S.op("dve", "tensor_scalar", reads=[d_sm], writes=[d_sm], out=sm[:, 1:2], in0=sm[:, 0:1], scalar1=1.0 / D, scalar2=EPS, op0=ALU.mult, op1=ALU.add)
                S.op("act", "activation", reads=[d_sm], writes=[d_sm], out=sm[:, 1:2], in_=sm[:, 1:2], func=AF.Sqrt)
                S.op("dve", "reciprocal", reads=[d_sm], writes=[d_sm], out=sm[:, 1:2], in_=sm[:, 1:2])
                S.op("dve", "scalar_tensor_tensor", reads=[d_src, d_sm, d_fg], writes=[d_yt],
                     out=yt[:], in0=src[:], scalar=sm[:, 1:2], in1=fg[:], op0=ALU.mult, op1=ALU.mult)
                S.dma("pool", y[tt * 128:(tt + 1) * 128, :], yt[:], reads=[d_yt])
        S.phase_end()


def _sl(r, d, j0, j1):
    return slice(r + d * j0, r + d * (j1 - 1) + 1, d)


def _phaseB(nc, S, FT, AV, MT, off, slen, c_etab, onesb, identb, d_const, heads=range(8)):
    nb = slen // 128
    scale = 128.0 ** -0.5
    with contextlib.ExitStack() as st:
        T = lambda name, shape, dt: st.enter_context(nc.sbuf_tensor(_uniq(name), shape, dt))
        P = lambda name, shape, dt: st.enter_context(nc.psum_tensor(_uniq(name), shape, dt))
        etf = T("etf", [128, 768], F32)
        d_etf = Dep()
        etb = T("etb", [128, 8, 3, 2, 128], BF16)
        d_etb = Dep()
        for h in range(8):
            S.dma("sp", etf[:], c_etab[:, h * 768:(h + 1) * 768], writes=[d_etf])
            S.op("dve", "tensor_copy", reads=[d_etf], writes=[d_etb], out=etb[:, h].rearrange("p a b c -> p (a b c)"), in_=etf[:])
        qkg = Rot([([T("bq%d" % i, [128, slen], BF16), T("bk%d" % i, [128, slen], BF16), T("bg%d" % i, [128, slen], BF16)], Dep()) for i in range(2)])
        vts = Rot([([T("bv%d_%d" % (i, d), [128, d, nb // d, 128], BF16) for d in PATTERNS], Dep()) for i in range(2)])
        acc = T("bacc", [128, 2, slen], F32)
        d_acc = Dep()
        pes = Rot([(T("bpe%d" % i, [128, 2, 128], BF16), Dep()) for i in range(5)])
        tmps = Rot([(T("btm%d" % i, [128, 2, 128], F32), Dep()) for i in range(3)])
        nst = [0]
        sgt = T("bsg", [128, slen], BF16)
        d_sgt = Dep()
        outt = T("bout", [128, slen], BF16)
        d_out = Dep()
        pst = Rot([(P("bst%d" % i, [128, 2, 128], F32), Dep()) for i in range(3)])
        pol = Rot([(P("bol%d" % i, [128, 2, 256], F32), Dep()) for i in range(3)])
        for h in heads:
            (qt_, kt_, gt_), d_qkg = qkg.next()
            vt, d_v = vts.next()
            S.dma("sp", qt_[:], FT[h, :, off:off + slen], writes=[d_qkg])
            S.dma("sp", kt_[:], FT[8 + h, :, off:off + slen], writes=[d_qkg])
            S.dma("sp", gt_[:], FT[16 + h, :, off:off + slen], writes=[d_qkg])
            for pi, d in enumerate(PATTERNS):
                njh = nb // d
                for r in range(d):
                    src = AV[off:off + slen, h * 128:(h + 1) * 128].rearrange("(jh p r) c -> p r jh c", p=128, r=d)[:, r]
                    for j0 in range(0, njh, 8):
                        j1 = min(njh, j0 + 8)
                        S.dma("sp", vt[pi][:, r, j0:j1], src[:, j0:j1], writes=[d_v])
            tiles = []
            for pi, d in enumerate(PATTERNS):
                L = slen // d
                nkb = L // 128
                for r in range(d):
                    for qt in range(nkb + 1):
                        tiles.append((pi, d, L, nkb, r, qt))
            LAG = 2
            inflight = []
            groups = []
            for ti, (pi, d, L, nkb, r, qt) in enumerate(tiles):
                q0 = max(0, 128 * qt - 64)
                q1 = min(L, 128 * qt + 64)
                if groups and groups[-1]["key"] == (pi, r) and groups[-1]["q1"] == q0 and (q1 - groups[-1]["q0"]) <= 256:
                    groups[-1]["q1"] = q1
                    groups[-1]["last"] = ti
                else:
                    groups.append(dict(key=(pi, r), q0=q0, q1=q1, last=ti, po=None))
                tiles[ti] = (pi, d, L, nkb, r, qt, groups[-1])

            def stage2(ent):
                (ti, pi, d, r, q0, q1, nq, blocks, pe, d_pe, grp) = ent
                if grp["po"] is None:
                    grp["po"] = pol.next()
                po, d_po = grp["po"]
                go = q0 - grp["q0"]
                for bi, (b, kb) in enumerate(blocks):
                    S.op("pe", "matmul", reads=[d_pe, d_v], writes=[d_po], signal=False,
                         out=po[:, 0, go:go + nq], lhsT=vt[pi][:, r, kb, :], rhs=pe[:, b, 0:nq], start=(bi == 0), stop=(bi == len(blocks) - 1))
                for bi, (b, kb) in enumerate(blocks):
                    S.op("pe", "matmul", reads=[d_pe, d_const], writes=[d_po], signal=(bi == len(blocks) - 1),
                         out=po[:, 1, go:go + nq], lhsT=onesb[:], rhs=pe[:, b, 0:nq], start=(bi == 0), stop=(bi == len(blocks) - 1))
                if ti != grp["last"]:
                    return
                gn = grp["q1"] - grp["q0"]
                aap = acc[:, :, _sl(r, d, grp["q0"], grp["q1"])]
                if pi == 0:
                    S.op("dve", "tensor_copy", reads=[d_po], writes=[d_acc], out=aap, in_=po[:, :, 0:gn])
                else:
                    S.op("dve", "tensor_tensor", reads=[d_po, d_acc], writes=[d_acc], out=aap, in0=po[:, :, 0:gn], in1=aap, op=ALU.add)

            for ti, (pi, d, L, nkb, r, qt, grp) in enumerate(tiles):
                q0 = max(0, 128 * qt - 64)
                q1 = min(L, 128 * qt + 64)
                nq = q1 - q0
                qoff = 64 if qt == 0 else 0
                blocks = []
                if qt >= 1:
                    blocks.append((0, qt - 1))
                if qt < nkb:
                    blocks.append((1, qt))
                ps, d_ps = pst.next()
                pe, d_pe = pes.next()
                qap = qt_[:, _sl(r, d, q0, q1)]
                for bi, (b, kb) in enumerate(blocks):
                    S.op("pe", "matmul", reads=[d_qkg], writes=[d_ps], signal=False,
                         out=ps[:, b, 0:nq], lhsT=kt_[:, _sl(r, d, 128 * kb, 128 * (kb + 1))], rhs=qap, start=True, stop=False)
                    S.op("pe", "matmul", reads=[d_etb, d_const], writes=[d_ps], signal=(bi == len(blocks) - 1),
                         out=ps[:, b, 0:nq], lhsT=identb[:], rhs=etb[:, h, pi, b, qoff:qoff + nq], start=False, stop=True)
                b0 = blocks[0][0]
                b1 = blocks[-1][0] + 1
                S.op("act", "activation", reads=[d_ps], writes=[d_pe], out=pe[:, b0:b1, 0:nq], in_=ps[:, b0:b1, 0:nq], func=AF.Exp, scale=scale)
                inflight.append((ti, pi, d, r, q0, q1, nq, blocks, pe, d_pe, grp))
                if len(inflight) > LAG:
                    stage2(inflight.pop(0))
            while inflight:
                stage2(inflight.pop(0))
            S.op("act", "activation", reads=[d_acc], writes=[d_acc], out=acc[:, 1, :], in_=acc[:, 1, :], func=AF.Ln)
            S.op("act", "activation", reads=[d_acc], writes=[d_acc], out=acc[:, 1, :], in_=acc[:, 1, :], func=AF.Exp, scale=-1.0)
            S.op("act", "activation", reads=[d_qkg], writes=[d_sgt], out=sgt[:], in_=gt_[:], func=AF.Silu)
            S.op("dve", "tensor_tensor", reads=[d_acc], writes=[d_acc], out=acc[:, 0, :], in0=acc[:, 0, :], in1=acc[:, 1, :], op=ALU.mult)
            S.op("dve", "tensor_tensor", reads=[d_acc, d_sgt], writes=[d_out], out=outt[:], in0=acc[:, 0, :], in1=sgt[:], op=ALU.mult)
            S.dma("pool", MT[h, :, off:off + slen], outt[:], reads=[d_out])
        S.phase_end()


def _phaseC(nc, S, FT, GB, ROWS, MT, off, slen, convw, alog, dtb, dngain,
            c_negmask, c_cmask, c_ident2, c_hmask, identf, identb, d_const, heads=range(8)):
    nb = slen // 128
    NTOK = ROWS.shape[1]
    qscale = 128.0 ** -0.5
    with contextlib.ExitStack() as st:
        T = lambda name, shape, dt: st.enter_context(nc.sbuf_tensor(_uniq(name), shape, dt))
        P = lambda name, shape, dt: st.enter_context(nc.psum_tensor(_uniq(name), shape, dt))
        d_c = Dep()
        negm = T("negm", [128, 2, 256], F32)
        ident2b = T("ident2b", [128, 256], BF16)
        hmask = T("hmask", [128, 2], F32)
        cw = T("cw", [128, 24, 5], F32)
        dng = T("dng", [128, 1], F32)
        S.dma("sp", negm[:].rearrange("p a b -> p (a b)"), c_negmask, writes=[d_c])
        S.dma("sp", hmask[:], c_hmask, writes=[d_c])
        S.dma("sp", cw[:].rearrange("p a b -> p (a b)"), convw, writes=[d_c])
        S.dma("sp", dng[:], dngain, writes=[d_c])
        gc = T("gc", [128, nb, 16], F32)
        egc = T("egc", [128, nb, 16], F32)
        edec = T("edec", [128, nb, 16], F32)
        beta = T("beta", [128, nb, 16], F32)
        sdec = [T("sdec0", [128, nb, 16], F32), T("sdec1", [128, nb, 16], F32)]
        begc = T("begc", [128, nb, 16], F32)
        d_gate = Dep()
        bank = [P("cbank%d" % i, [128, 512], F32) for i in range(8)]
        d_bank = [Dep() for _ in range(8)]
        bankb = [bank[i][:].bitcast(BF16) for i in range(8)]
        with contextlib.ExitStack() as st2:
            T2 = lambda name, shape, dt: st2.enter_context(nc.sbuf_tensor(_uniq(name), shape, dt))
            cmask = T2("cmask", [128, 5, 128], F32)
            id2f = T2("id2f", [128, 256], F32)
            S.dma("sp", cmask[:].rearrange("p a b -> p (a b)"), c_cmask, writes=[d_c])
            S.dma("sp", id2f[:], c_ident2, writes=[d_c])
            S.op("dve", "tensor_copy", reads=[d_c], writes=[d_c], out=ident2b[:], in_=id2f[:])
            gbt = T2("gbt", [128, nb, 32], F32)
            al = T2("al", [128, nb, 16], F32)
            db = T2("db", [128, nb, 16], F32)
            t1 = T2("t1", [128, nb, 16], F32)
            t2 = T2("t2", [128, nb, 16], F32)
            z = T2("z", [128, nb, 16], F32)
            lnb = T2("lnb", [128, nb, 16], F32)
            g = T2("g", [128, nb, 16], F32)
            r1 = T2("r1", [128, nb, 16], F32)
            rowst = T2("rowst", [32, 32, 128], F32)
            d_g = Dep()
            S.dma("sp", gbt[:], GB[off:off + slen, :].rearrange("(t p) c -> p t c", p=128), writes=[d_g])
            S.dma("sp", al[:].rearrange("p a b -> p (a b)"), alog[:, 0:nb * 16], writes=[d_g])
            S.dma("sp", db[:].rearrange("p a b -> p (a b)"), dtb[:, 0:nb * 16], writes=[d_g])
            braw = gbt[:, :, 0:16]
            araw = gbt[:, :, 16:32]
            G = dict(reads=[d_g], writes=[d_g])
            S.op("dve", "scalar_tensor_tensor", out=t1[:], in0=braw, scalar=-1.0, in1=braw, op0=ALU.mult, op1=ALU.max, **G)
            S.op("act", "activation", out=t1[:], in_=t1[:], func=AF.Exp, scale=-1.0, **G)
            S.op("dve", "tensor_scalar", out=t1[:], in0=t1[:], scalar1=1.0, scalar2=None, op0=ALU.add, **G)
            S.op("act", "activation", out=t1[:], in_=t1[:], func=AF.Ln, **G)
            S.op("dve", "scalar_tensor_tensor", out=lnb[:], in0=braw, scalar=0.0, in1=t1[:], op0=ALU.min, op1=ALU.subtract, **G)
            S.op("act", "activation", reads=[d_g], writes=[d_gate], out=beta[:], in_=lnb[:], func=AF.Exp)
            S.op("dve", "tensor_tensor", out=z[:], in0=araw, in1=db[:], op=ALU.add, **G)
            S.op("dve", "scalar_tensor_tensor", out=t2[:], in0=z[:], scalar=-1.0, in1=z[:], op0=ALU.mult, op1=ALU.max, **G)
            S.op("act", "activation", out=t2[:], in_=t2[:], func=AF.Exp, scale=-1.0, **G)
            S.op("dve", "tensor_scalar", out=t2[:], in0=t2[:], scalar1=1.0, scalar2=None, op0=ALU.add, **G)
            S.op("act", "activation", out=t2[:], in_=t2[:], func=AF.Ln, **G)
            S.op("dve", "scalar_tensor_tensor", out=t2[:], in0=z[:], scalar=0.0, in1=t2[:], op0=ALU.max, op1=ALU.add, **G)
            S.op("act", "activation", out=al[:], in_=al[:], func=AF.Exp, **G)
            S.op("dve", "scalar_tensor_tensor", out=g[:], in0=t2[:], scalar=-1.0, in1=al[:], op0=ALU.mult, op1=ALU.mult, **G)
            d_pb = [Dep() for _ in range(5)]
            pgi = [0, 1, 2, 3, 5]
            pg2 = [bank[i][:, 0:nb * 16] for i in pgi]
            pg = [bank[i][:, 0:nb * 16].rearrange("p (a b) -> p a b", b=16) for i in pgi]
            g2 = g[:].rearrange("p a b -> p (a b)")
            S.op("pe", "matmul", reads=[d_g, d_c], writes=[d_pb[0], d_bank[0]], signal=True, out=pg2[0], lhsT=cmask[:, 0, :], rhs=g2, start=True, stop=True)
            S.op("pe", "matmul", reads=[d_g, d_c], writes=[d_pb[4], d_bank[5]], signal=True, out=pg2[4], lhsT=cmask[:, 1, :], rhs=g2, start=True, stop=True)
            for i in range(1, 4):
                S.op("pe", "matmul", reads=[d_g, d_c], writes=[d_pb[i], d_bank[i]], signal=True,
                     out=pg2[i], lhsT=cmask[:, 1 + i, :], rhs=g2, start=True, stop=True)
            S.op("dve", "tensor_copy", reads=[d_pb[0]], writes=[d_gate, d_bank[0]], out=gc[:, :, 0:8], in_=pg[0][:, :, 0:8])
            S.op("dve", "tensor_copy", reads=[d_pb[4]], writes=[d_gate, d_bank[5]], out=gc[:, :, 8:16], in_=pg[4][:, :, 8:16])
            S.op("act", "activation", reads=[d_gate], writes=[d_gate], out=egc[:], in_=gc[:], func=AF.Exp)
            S.op("dve", "tensor_tensor", reads=[d_gate], writes=[d_gate], out=begc[:], in0=egc[:], in1=beta[:], op=ALU.mult)
            S.op("dve", "tensor_tensor", reads=[d_pb[1], d_gate], writes=[d_gate, d_bank[1]], out=edec[:], in0=pg[1], in1=gc[:], op=ALU.subtract)
            S.op("act", "activation", reads=[d_gate], writes=[d_gate], out=edec[:], in_=edec[:], func=AF.Exp)
            S.op("act", "activation", reads=[d_pb[2]], writes=[d_gate, d_bank[2]], out=sdec[0][:], in_=pg[2], func=AF.Exp)
            S.op("act", "activation", reads=[d_pb[3]], writes=[d_gate, d_bank[3]], out=sdec[1][:], in_=pg[3], func=AF.Exp)
            S.op("dve", "tensor_tensor", reads=[d_gate, d_g], writes=[d_g], out=r1[:], in0=gc[:], in1=lnb[:], op=ALU.add)
            d_rowst = Dep()
            prow = [(bank[4][0:nb, 0:128], Dep()), (bank[4][0:nb, 128:256], Dep()), (bank[4][0:nb, 256:384], Dep()), (bank[4][0:nb, 384:512], Dep())]
            for q in range(32):
                dh, which = q // 2, q % 2
                src = (r1 if which == 0 else gc)[:, :, dh]
                pr, d_pr = prow[q % 4]
                S.op("pe", "transpose", reads=[d_g, d_gate, d_const], writes=[d_pr, d_bank[4]], out=pr, in_=src, identity=identf[:])
                if q % 2 == 0:
                    S.op("act", "activation", reads=[d_pr], writes=[d_rowst, d_bank[4]], out=rowst[0:nb, q, :], in_=pr, func=AF.Copy)
                else:
                    S.op("dve", "tensor_copy", reads=[d_pr], writes=[d_rowst, d_bank[4]], out=rowst[0:nb, q, :], in_=pr)
            d_rows = Dep()
            S.dma("sp", ROWS[:, off:off + slen].rearrange("q (b i) -> b q i", i=128), rowst[0:nb], reads=[d_rowst], writes=[d_rows])
            S.phase_end()

        W = 7
        xin = T("cx", [128, slen + 4], BF16)
        d_xin = Dep()
        S.op("pool", "memset", writes=[d_xin], ap=xin[:, 0:2], constant=0.0)
        S.op("pool", "memset", writes=[d_xin], ap=xin[:, slen + 2:slen + 4], constant=0.0)
        CH = min(512, slen)
        caccs = Rot([(T("cacc%d" % i, [128, CH], F32), Dep()) for i in range(2)])
        ybf = T("cy", [128, slen], BF16)
        d_ybf = Dep()
        ytms = [(T("ytm%d" % i, [128, nb, 3, 128], BF16), Dep()) for i in range(2)]
        sss = [(T("css%d" % i, [128, nb, 2], F32), Dep()) for i in range(2)]
        junkc = T("cjunk", [128, 128], BF16)
        d_junk = Dep()
        oacc = T("oacc", [128, nb, 128], F32)
        d_oacc = Dep()
        oss = T("coss", [128, 2, nb], F32)
        d_oss = Dep()
        sgT = T("csg", [128, slen], BF16)
        d_sgT = Dep()
        ontm = Rot([(T("con%d" % i, [128, 128], BF16), Dep()) for i in range(2)])
        vnew = [[(T("cvn%d_%d" % (i, c), [128, 128], BF16), Dep()) for c in range(2)] for i in range(2)]
        Sf = [(T("cSf%d" % i, [128, 128], F32), Dep()) for i in range(2)]
        Sbf = [(T("cSb%d" % i, [128, 128], BF16), Dep()) for i in range(2)]

        class Slot:
            pass

        slots = []
        for si in range(W):
            sl = Slot()
            mk = lambda name, shape, dt: (T("%s_s%d" % (name, si), shape, dt), Dep())
            sl.arg = mk("carg", [128, 256], F32)
            sl.at = mk("cat", [128, 128], F32)
            sl.it = mk("cit", [128, 128], BF16)
            sl.asb = mk("casb", [128, 128], F32)
            sl.p0 = mk("cp0", [128, 128], F32)
            sl.bp = [mk("cbp%d" % i, [128, 256], F32) for i in range(2)]
            sl.bt = [mk("cbt%d" % i, [128, 128], F32) for i in range(2)]
            sl.p4 = mk("cp4", [128, 128], F32)
            sl.pbf = mk("cpbf", [128, 128], BF16)
            sl.kbg = mk("ckbg", [128, 128], BF16)
            sl.vb = mk("cvb", [128, 128], BF16)
            sl.kd = mk("ckd", [128, 128], BF16)
            sl.yqg = mk("cyqg", [128, 128], BF16)
            sl.us = mk("cus", [128, 128], F32)
            sl.wt = mk("cwt", [128, 128], BF16)
            sl.qg = mk("cqg", [128, 256], BF16)
            slots.append(sl)

        class Reg:
            def __init__(self, b):
                self.f = bank[b]
                self.h = bankb[b]
                self.d = d_bank[b]

            def W(self):
                return [self.d]

        p_o = [Reg(2), Reg(3)]
        pany = Rot([Reg(b) for b in (0, 1, 4, 5, 6, 7)])
        kkalls = [(T("kkall%d" % i, [128, nb, 256], BF16), Dep()) for i in range(2)]
        kqs = Rot([(T("ckq%d" % i, [128, 256], BF16), Dep()) for i in range(1)])
        chain_done = [0, 0]

        def blockdir(sl, h, dr, it, ytm, d_ytm, kkall, d_kkall):
            blk = it if dr == 0 else nb - 1 - it
            dh = dr * 8 + h
            arg, d_arg = sl.arg
            S.dma("sp", arg[:].rearrange("p (a b) -> p a b", b=128), bass.AP(ROWS.tensor, (2 * dh) * NTOK + off + blk * 128, [[0, 128], [NTOK, 2], [1, 128]]),
                  reads=[d_rows], writes=[d_arg])
            kbg, d_kbg = sl.kbg
            vb, d_vb = sl.vb
            kd, d_kd = sl.kd
            yqg, d_yqg = sl.yqg
            S.op("act", "activation", reads=[d_ytm, d_gate], writes=[d_yqg], out=yqg[:], in_=ytm[:, blk, 0, :], func=AF.Copy, scale=egc[:, blk, dh:dh + 1])
            S.op("dve", "tensor_scalar", reads=[d_ytm, d_gate], writes=[d_kbg], out=kbg[:], in0=ytm[:, blk, 1, :], scalar1=begc[:, blk, dh:dh + 1], scalar2=None, op0=ALU.mult)
            yield
            gg = arg
            d_gg = d_arg
            pq = pany.next()
            S.op("pe", "matmul", reads=[d_yqg, d_c], writes=pq.W(), signal=True, out=pq.f[:, 0:256], lhsT=yqg[:], rhs=ident2b[:], start=True, stop=True)
            qg, d_qg = sl.qg
            S.op("act", "activation", reads=[pq.d], writes=[d_qg, pq.d], out=qg[:], in_=pq.f[:, 0:256], func=AF.Copy)
            S.op("act", "activation", reads=[d_ytm, d_gate], writes=[d_vb], out=vb[:], in_=ytm[:, blk, 2, :], func=AF.Copy, scale=beta[:, blk, dh:dh + 1])
            S.op("dve", "tensor_scalar", reads=[d_ytm, d_gate], writes=[d_kd], out=kd[:], in0=ytm[:, blk, 1, :], scalar1=edec[:, blk, dh:dh + 1], scalar2=None, op0=ALU.mult)
            yield
            S.op("dve", "scalar_tensor_tensor", reads=[d_arg, d_gate, d_c], writes=[d_arg], out=arg[:], in0=arg[:],
                 scalar=gc[:, blk, dh:dh + 1], in1=negm[:, dr, :], op0=ALU.subtract, op1=ALU.add)
            S.op("act", "activation", reads=[d_arg], writes=[d_arg], out=arg[:], in_=arg[:], func=AF.Exp)
            yield
            at, d_at = sl.at
            itt, d_it = sl.it
            S.op("dve", "tensor_tensor", reads=[d_kkall, d_gg], writes=[d_at], out=at[:], in0=kkall[:, blk, 0:128], in1=gg[:, 0:128], op=ALU.mult)
            S.op("pool", "tensor_tensor", reads=[d_kkall, d_gg], writes=[d_it], out=itt[:], in0=kkall[:, blk, 128:256], in1=gg[:, 128:256], op=ALU.mult)
            pA = pany.next()
            S.op("pe", "transpose", reads=[d_at, d_const], writes=pA.W(), signal=True, out=pA.f[:, 0:128], in_=at[:], identity=identf[:])
            asb, d_asb = sl.asb
            S.op("act", "activation", reads=[pA.d], writes=[d_asb, pA.d], out=asb[:], in_=pA.f[:, 0:128], func=AF.Copy)
            p0, d_p0 = sl.p0
            S.op("pool", "tensor_tensor", reads=[d_at, d_const], writes=[d_p0], out=p0[:], in0=identf[:], in1=at[:], op=ALU.subtract)
            yield
            bp1, d_bp1 = sl.bp[1]
            px = pany.next()
            S.op("pe", "matmul", reads=[d_asb, d_at], writes=px.W(), signal=True, out=px.f[:, 0:128], lhsT=asb[:], rhs=at[:], start=True, stop=True)
            py = pany.next()
            S.op("pe", "matmul", reads=[d_asb, d_at], writes=py.W(), signal=True, out=py.f[:, 0:128], lhsT=at[:], rhs=asb[:], start=True, stop=True)
            S.op("dve", "tensor_copy", reads=[px.d], writes=[d_bp1, px.d], out=bp1[:, 0:128], in_=px.f[:, 0:128])
            S.op("pool", "tensor_copy", reads=[d_p0], writes=[d_bp1], out=bp1[:, 128:256], in_=p0[:])
            bt, d_bt = sl.bt[1]
            S.op("act", "activation", reads=[py.d], writes=[d_bt, py.d], out=bt[:], in_=py.f[:, 0:128], func=AF.Copy)
            yield
            cur, d_cur = bp1, d_bp1
            for k in range(1, 4):
                nxt, d_nxt = sl.bp[(k + 1) % 2]
                px = pany.next()
                S.op("pe", "matmul", reads=[d_bt, d_cur], writes=px.W(), signal=True, out=px.f[:, 0:256], lhsT=bt[:], rhs=cur[:], start=True, stop=True)
                if k % 2 == 1:
                    S.op("dve", "tensor_copy", reads=[px.d], writes=[d_nxt, px.d], out=nxt[:], in_=px.f[:, 0:256])
                else:
                    S.op("act", "activation", reads=[px.d], writes=[d_nxt, px.d], out=nxt[:], in_=px.f[:, 0:256], func=AF.Copy)
                S.op("pool", "tensor_tensor", reads=[d_cur, d_nxt], writes=[d_nxt], out=nxt[:, 128:256], in0=nxt[:, 128:256], in1=cur[:, 128:256], op=ALU.add)
                yield
                py = pany.next()
                S.op("pe", "matmul", reads=[d_bt, d_cur], writes=py.W(), signal=True, out=py.f[:, 0:128], lhsT=cur[:, 0:128], rhs=bt[:], start=True, stop=True)
                nbt, d_nbt = sl.bt[(k + 1) % 2]
                if k % 2 == 1:
                    S.op("act", "activation", reads=[py.d], writes=[d_nbt, py.d], out=nbt[:], in_=py.f[:, 0:128], func=AF.Copy)
                else:
                    S.op("dve", "tensor_copy", reads=[py.d], writes=[d_nbt, py.d], out=nbt[:], in_=py.f[:, 0:128])
                cur, d_cur = nxt, d_nxt
                bt, d_bt = nbt, d_nbt
                yield
            px = pany.next()
            S.op("pe", "matmul", reads=[d_bt, d_cur], writes=px.W(), signal=True, out=px.f[:, 0:128], lhsT=bt[:], rhs=cur[:, 128:256], start=True, stop=True)
            py = pany.next()
            S.op("pe", "matmul", reads=[d_bt, d_cur], writes=py.W(), signal=True, out=py.f[:, 0:128], lhsT=cur[:, 0:128], rhs=bt[:], start=True, stop=True)
            p4, d_p4 = sl.p4
            S.op("dve", "tensor_tensor", reads=[px.d, d_cur], writes=[d_p4, px.d], out=p4[:], in0=px.f[:, 0:128], in1=cur[:, 128:256], op=ALU.add)
            bt5, d_bt5 = sl.bt[1]
            S.op("act", "activation", reads=[py.d], writes=[d_bt5, py.d], out=bt5[:], in_=py.f[:, 0:128], func=AF.Copy)
            yield
            px = pany.next()
            S.op("pe", "matmul", reads=[d_bt5, d_p4], writes=px.W(), signal=True, out=px.f[:, 0:128], lhsT=bt5[:], rhs=p4[:], start=True, stop=True)
            pbf, d_pbf = sl.pbf
            S.op("dve", "tensor_tensor", reads=[px.d, d_p4], writes=[d_pbf, px.d], out=pbf[:], in0=px.f[:, 0:128], in1=p4[:], op=ALU.add)
            yield
            pu = pany.next()
            S.op("pe", "matmul", reads=[d_pbf, d_vb], writes=pu.W(), signal=True, out=pu.f[:, 0:128], lhsT=pbf[:], rhs=vb[:], start=True, stop=True)
            us, d_us = sl.us
            S.op("act", "activation", reads=[pu.d], writes=[d_us, pu.d], out=us[:], in_=pu.f[:, 0:128], func=AF.Copy)
            pw = pany.next()
            S.op("pe", "matmul", reads=[d_pbf, d_kbg], writes=pw.W(), signal=True, out=pw.f[:, 0:128], lhsT=kbg[:], rhs=pbf[:], start=True, stop=True)
            wt, d_wt = sl.wt
            S.op("dve", "tensor_copy", reads=[pw.d], writes=[d_wt, pw.d], out=wt[:], in_=pw.f[:, 0:128])
            yield
            while chain_done[dr] < it:
                yield
            sf, d_sf = Sf[dr]
            sb, d_sb = Sbf[dr]
            po = p_o[dr]
            order = (0, 1) if dr == 0 else (1, 0)
            for n, c in enumerate(order):
                vn, d_vn = vnew[dr][c]
                pws = pany.next()
                rows = slice(64 * c, 64 * c + 64)
                S.op("pe", "matmul", reads=[d_wt, d_sb], writes=pws.W(), signal=True, out=pws.f[:, 0:128], lhsT=wt[:], rhs=sb[:], start=True, stop=True)
                S.op("pe", "matmul", reads=[d_qg, d_sb], writes=po.W(), signal=False, out=po.f[:, 0:128], lhsT=qg[:, c * 128:(c + 1) * 128], rhs=sb[:], start=(n == 0), stop=False)
                S.op("dve", "tensor_tensor", reads=[d_us, pws.d], writes=[d_vn, pws.d], out=vn[rows, :], in0=us[rows, :], in1=pws.f[rows, 0:128], op=ALU.subtract)
                yield
                psu = pany.next()
                S.op("pe", "matmul", reads=[d_kd, d_vn], writes=psu.W(), signal=True, out=psu.f[:, 0:128], lhsT=kd[:], rhs=vn[:], start=True, stop=True)
                S.op("pe", "matmul", reads=[d_it, d_vn], writes=po.W(), signal=(n == 1), out=po.f[:, 0:128], lhsT=itt[:], rhs=vn[:], start=False, stop=(n == 1))
                S.op("dve", "scalar_tensor_tensor", reads=[psu.d, d_sf, d_gate], writes=[d_sf, psu.d], out=sf[:], in0=sf[:], scalar=sdec[c][:, blk, dh:dh + 1], in1=psu.f[:, 0:128], op0=ALU.mult, op1=ALU.add)
                S.op("act", "activation", reads=[d_sf], writes=[d_sb], out=sb[:], in_=sf[:], func=AF.Copy)
                yield
            S.op("dve", "tensor_tensor", reads=[po.d, d_oacc], writes=[d_oacc, po.d], out=oacc[:, blk, :], in0=po.f[:, 0:128], in1=oacc[:, blk, :], op=ALU.add)
            chain_done[dr] = it + 1

        def prologue(h, buf):
            ytm, d_ytm = ytms[buf]
            ss, d_ss = sss[buf]
            for i in range(3):
                S.dma("sp", xin[:, 2:slen + 2], FT[24 + 8 * i + h, :, off:off + slen], writes=[d_xin])
                wcol = lambda j: cw[:, 8 * i + h, j:j + 1]
                for ch in range(slen // CH):
                    c0 = ch * CH
                    cacc, d_cacc = caccs.next()
                    S.op("dve", "tensor_scalar", reads=[d_xin, d_c], writes=[d_cacc], out=cacc[:], in0=xin[:, c0:c0 + CH], scalar1=wcol(0), scalar2=None, op0=ALU.mult)
                    yield
                    for j in range(1, 5):
                        S.op("dve", "scalar_tensor_tensor", reads=[d_xin, d_c, d_cacc], writes=[d_cacc], out=cacc[:], in0=xin[:, c0 + j:c0 + j + CH], scalar=wcol(j), in1=cacc[:], op0=ALU.mult, op1=ALU.add)
                        yield
                    S.op("act", "activation", reads=[d_cacc], writes=[d_ybf], out=ybf[:, c0:c0 + CH], in_=cacc[:], func=AF.Silu)
                    yield
                for b4 in range(nb // 4):
                    pt = pany.next()
                    ptv = pt.h[:, 0:512].rearrange("p (a b) -> p a b", b=128)
                    for j in range(4):
                        blk = b4 * 4 + j
                        S.op("pe", "transpose", reads=[d_ybf, d_const], writes=pt.W(), signal=(j == 3),
                             out=ptv[:, j, :], in_=ybf[:, blk * 128:(blk + 1) * 128], identity=identb[:])
                    if b4 % 2 == 0:
                        S.op("dve", "tensor_copy", reads=[pt.d], writes=[d_ytm, pt.d], out=ytm[:, b4 * 4:b4 * 4 + 4, i, :], in_=ptv)
                    else:
                        S.op("act", "activation", reads=[pt.d], writes=[d_ytm, pt.d], out=ytm[:, b4 * 4:b4 * 4 + 4, i, :], in_=ptv, func=AF.Copy)
                    yield
            for blk in range(nb):
                for i in range(2):
                    S.op("act", "activation", reads=[d_ytm], writes=[d_junk, d_ss], out=junkc[:], in_=ytm[:, blk, i, :], func=AF.Square, accum_out=ss[:, blk, i:i + 1])
                yield
            S.op("dve", "tensor_scalar", reads=[d_ss], writes=[d_ss], out=ss[:], in0=ss[:], scalar1=EPS, scalar2=None, op0=ALU.add)
            S.op("act", "activation", reads=[d_ss], writes=[d_ss], out=ss[:], in_=ss[:], func=AF.Sqrt)
            S.op("dve", "reciprocal", reads=[d_ss], writes=[d_ss], out=ss[:], in_=ss[:])
            S.op("dve", "tensor_scalar", reads=[d_ss], writes=[d_ss], out=ss[:, :, 0], in0=ss[:, :, 0], scalar1=qscale, scalar2=None, op0=ALU.mult)
            yield
            for blk in range(nb):
                S.op("dve", "tensor_scalar", reads=[d_ss, d_ytm], writes=[d_ytm], out=ytm[:, blk, 0, :], in0=ytm[:, blk, 0, :], scalar1=ss[:, blk, 0:1], scalar2=None, op0=ALU.mult)
                S.op("act", "activation", reads=[d_ss, d_ytm], writes=[d_ytm], out=ytm[:, blk, 1, :], in_=ytm[:, blk, 1, :], func=AF.Copy, scale=ss[:, blk, 1:2])
                yield
            kkall, d_kkall = kkalls[buf]
            for blk in range(nb):
                pk = pany.next()
                S.op("pe", "transpose", reads=[d_ytm, d_const], writes=pk.W(), signal=False, out=pk.h[:, 0:128], in_=ytm[:, blk, 1, :], identity=identb[:])
                S.op("pe", "transpose", reads=[d_ytm, d_const], writes=pk.W(), signal=True, out=pk.h[:, 128:256], in_=ytm[:, blk, 0, :], identity=identb[:])
                kq, d_kq = kqs.next()
                S.op("act", "activation", reads=[pk.d], writes=[d_kq, pk.d], out=kq[:], in_=pk.h[:, 0:256], func=AF.Copy)
                yield
                pkk = pany.next()
                S.op("pe", "matmul", reads=[d_kq], writes=pkk.W(), signal=True, out=pkk.f[:, 0:256], lhsT=kq[:, 0:128], rhs=kq[:], start=True, stop=True)
                S.op("dve", "tensor_copy", reads=[pkk.d], writes=[d_kkall, pkk.d], out=kkall[:, blk, :], in_=pkk.f[:, 0:256])
                yield

        hl = list(heads)
        for _ in prologue(hl[0], 0):
            pass
        for hi, h in enumerate(hl):
            ytm, d_ytm = ytms[hi % 2]
            kkall, d_kkall = kkalls[hi % 2]
            bg = prologue(hl[hi + 1], (hi + 1) % 2) if hi + 1 < len(hl) else None
            S.op("pool", "memset", reads=[], writes=[d_oacc], ap=oacc[:].rearrange("p a b -> p (a b)"), constant=0.0)
            for dr in range(2):
                S.op("pool", "memset", writes=[Sf[dr][1]], ap=Sf[dr][0][:], constant=0.0)
                S.op("pool", "memset", writes=[Sbf[dr][1]], ap=Sbf[dr][0][:], constant=0.0)
                for c in range(2):
                    S.op("pool", "memset", writes=[vnew[dr][c][1]], ap=vnew[dr][c][0][:], constant=0.0)
            chain_done[0] = 0
            chain_done[1] = 0
            pending = []
            for it in range(nb):
                pending.append((0, it))
                pending.append((1, it))
            pending.reverse()
            active = []
            free = list(range(W))
            rnd = 0
            while pending or active:
                if pending and free:
                    dr, it = pending.pop()
                    si = free.pop()
                    active.append((si, blockdir(slots[si], h, dr, it, ytm, d_ytm, kkall, d_kkall)))
                for ent in list(active):
                    try:
                        next(ent[1])
                    except StopIteration:
                        active.remove(ent)
                        free.append(ent[0])
                rnd += 1
                if bg is not None:
                    for _ in range(1 + (rnd % 2)):
                        try:
                            next(bg)
                        except StopIteration:
                            bg = None
                            break
            if bg is not None:
                for _ in bg:
                    pass
            S.dma("sp", sgT[:], FT[48 + h, :, off:off + slen], writes=[d_sgT])
            S.op("act", "activation", reads=[d_sgT], writes=[d_sgT], out=sgT[:], in_=sgT[:], func=AF.Silu)
            for blk in range(nb):
                S.op("act", "activation", reads=[d_oacc], writes=[d_junk, d_oss], out=junkc[:], in_=oacc[:, blk, :], func=AF.Square, accum_out=oss[:, 0, blk:blk + 1])
            S.op("dve", "tensor_scalar", reads=[d_oss], writes=[d_oss], out=oss[:, 1, :], in0=oss[:, 0, :], scalar1=1.0 / 128, scalar2=EPS, op0=ALU.mult, op1=ALU.add)
            S.op("act", "activation", reads=[d_oss], writes=[d_oss], out=oss[:, 1, :], in_=oss[:, 1, :], func=AF.Sqrt)
            S.op("dve", "reciprocal", reads=[d_oss], writes=[d_oss], out=oss[:, 1, :], in_=oss[:, 1, :])
            for blk in range(nb):
                on, d_on = ontm.next()
                S.op("act", "activation", reads=[d_oacc, d_oss], writes=[d_on], out=on[:], in_=oacc[:, blk, :], func=AF.Copy, scale=oss[:, 1, blk:blk + 1])
                pt = pany.next()
                S.op("pe", "transpose", reads=[d_on, d_const], writes=pt.W(), signal=True, out=pt.h[:, 0:128], in_=on[:], identity=identb[:])
                S.op("dve", "scalar_tensor_tensor", reads=[pt.d, d_c, d_sgT], writes=[d_sgT, pt.d], out=sgT[:, blk * 128:(blk + 1) * 128], in0=pt.h[:, 0:128], scalar=dng[:, 0:1],
                     in1=sgT[:, blk * 128:(blk + 1) * 128], op0=ALU.mult, op1=ALU.mult)
            S.dma("pool", MT[8 + h, :, off:off + slen], sgT[:], reads=[d_sgT])
        S.phase_end()


def _host_maps(xs_per_core, w_in, w_out, norm_in_gain, conv_w, a_log, dt_bias, dn_gain, final_gain):
    c = _consts()
    gin = np.ascontiguousarray(norm_in_gain.reshape(16, 128).T)
    cw = np.ascontiguousarray(conv_w.reshape(5, 24, 128).transpose(2, 1, 0).reshape(128, 120))
    al = np.ascontiguousarray(np.broadcast_to(a_log.reshape(1, 1, 16), (128, 32, 16)).reshape(128, 512))
    db = np.ascontiguousarray(np.broadcast_to(dt_bias.reshape(1, 1, 16), (128, 32, 16)).reshape(128, 512))
    dg = np.ascontiguousarray(dn_gain.reshape(128, 1))
    fg = np.ascontiguousarray(np.broadcast_to(final_gain.reshape(1, D), (128, D)))
    maps = []
    for xc in xs_per_core:
        m = {"x": xc, "w_in": w_in, "w_out": w_out, "gin": gin, "convw": cw, "alog": al, "dtb": db,
             "dngain": dg, "fgain": fg}
        m.update(c)
        maps.append(m)
    return maps


PHASES = "0ABCD"


def kernel(x_prompt, x_sample, norm_in_gain, w_in, conv_w, a_log, dt_bias, delta_norm_gain, w_out, final_norm_gain):
    f = lambda a: np.ascontiguousarray(np.asarray(a, dtype=np.float32))
    x_prompt, x_sample = f(x_prompt), f(x_sample)
    seqs = [4096, 2048, 2048]
    xs = []
    for c in range(8):
        xs.append(np.ascontiguousarray(np.concatenate(
            [x_prompt[c], x_sample[2 * c], x_sample[2 * c + 1]], axis=0)))
    maps = _host_maps(xs, f(w_in)[0], f(w_out)[0], f(norm_in_gain)[0], f(conv_w)[0], f(a_log)[0],
                      f(dt_bias)[0], f(delta_norm_gain)[0], f(final_norm_gain))
    nc = build(seqs, phases=PHASES)
    res = run_bass_kernel_spmd(nc, maps, core_ids=list(range(8)))
    yp = np.empty((8, 4096, D), np.float32)
    ys = np.empty((16, 2048, D), np.float32)
    for c in range(8):
        yc = res.results[c]["y"]
        yp[c] = yc[0:4096]
        ys[2 * c] = yc[4096:6144]
        ys[2 * c + 1] = yc[6144:8192]
    return (yp, ys)
```

```python
import contextlib
import numpy as np
import concourse.bass as bass
import concourse.mybir as mybir
from concourse.bass_utils import run_bass_kernel_spmd

F32 = mybir.dt.float32
BF16 = mybir.dt.bfloat16
AF = mybir.ActivationFunctionType
ALU = mybir.AluOpType

D = 2048
PW = 8224
NDS = 40
EPS = 1e-6
NEG = -30000.0
PATTERNS = (1, 4, 16)


class Dep:
    __slots__ = ("w", "r", "const")

    def __init__(self, const=False):
        self.w = None
        self.r = {}
        self.const = const


class Sched:
    def __init__(self, nc, stack):
        self.nc = nc
        self.names = ["pe", "act", "dve", "pool", "sp"]
        self.sem = {}
        self.cnt = {}
        self.waited = {}
        for e in self.names:
            self.sem[e] = stack.enter_context(nc.semaphore("s_" + e))
            self.cnt[e] = 0
            self.waited[e] = {}
        self.dsem = [stack.enter_context(nc.semaphore("d%d" % i)) for i in range(NDS)]
        self.dcnt = [0] * NDS
        self.ndma = 0
        self.ndma_sw = 0
        self.allsems = {}
        for e in self.names:
            self.allsems[id(self.sem[e])] = self.sem[e]
        for s in self.dsem:
            self.allsems[id(s)] = s
        self.nins = 0
        self.q = {e: [] for e in self.names}

    def flush(self):
        nc = self.nc
        q = self.q

        def replay(e, items):
            for it in items:
                if it[0] == "w":
                    e.wait_ge(it[1], it[2])
                else:
                    _, name, kw, inc = it
                    ins = getattr(e, name)(**kw)
                    if inc is not None:
                        ins.then_inc(inc[0], inc[1])

        with nc.Block() as block:
            @block.tensor
            def _(e):
                replay(e, q["pe"])

            @block.scalar
            def _(e):
                replay(e, q["act"])

            @block.vector
            def _(e):
                replay(e, q["dve"])

            @block.gpsimd
            def _(e):
                replay(e, q["pool"])

            @block.sync
            def _(e):
                replay(e, q["sp"])
        self.q = {e: [] for e in self.names}

    def _collect(self, eng, reads, writes, extra=()):
        waits = {}

        def need(ev):
            if ev is None:
                return
            sem, val, src = ev
            if src == "pe" and eng == "pe":
                return
            k = id(sem)
            if waits.get(k, 0) < val:
                waits[k] = val

        for d in reads:
            need(d.w)
        for d in writes:
            need(d.w)
            for ev in d.r.values():
                need(ev)
        for ev in extra:
            need(ev)
        out = []
        wd = self.waited[eng]
        for k, val in waits.items():
            if wd.get(k, 0) >= val:
                continue
            wd[k] = val
            out.append((self.allsems[k], val))
        return out

    def _record(self, ev, reads, writes):
        k = id(ev[0])
        for d in reads:
            if d.const:
                continue
            old = d.r.get(k)
            if old is None or old[1] < ev[1]:
                d.r[k] = ev
        for d in writes:
            d.w = ev
            d.r = {}

    def op(self, eng, name, reads=(), writes=(), signal=True, **kw):
        q = self.q[eng]
        for sem, val in self._collect(eng, reads, writes):
            q.append(("w", sem, val))
        self.nins += 1
        if signal:
            self.cnt[eng] += 1
            q.append(("i", name, kw, (self.sem[eng], 1)))
            ev = (self.sem[eng], self.cnt[eng], eng)
        else:
            assert eng == "pe"
            q.append(("i", name, kw, None))
            ev = (self.sem[eng], self.cnt[eng] + 1, eng)
        self._record(ev, reads, writes)
        return ev

    def dma(self, q, out, in_, reads=(), writes=(), **dkw):
        if q == "sp":
            s = self.ndma % (NDS - 8)
            self.ndma += 1
        else:
            s = (NDS - 8) + self.ndma_sw % 8
            self.ndma_sw += 1
        sem = self.dsem[s]
        prev = self.dcnt[s]
        self.dcnt[s] += 16
        val = self.dcnt[s]
        extra = [(sem, prev, "dma")] if prev > 0 else []
        for sm, v in self._collect(q, reads, writes, extra):
            self.q[q].append(("w", sm, v))
        self.q[q].append(("i", "dma_start", dict(out=out, in_=in_, **dkw), (sem, 16)))
        self.nins += 1
        ev = (sem, val, "dma")
        self._record(ev, reads, writes)
        return ev

    def barrier(self):
        for eng in self.names:
            wd = self.waited[eng]
            for src in self.names:
                if src == eng or self.cnt[src] == 0:
                    continue
                k = id(self.sem[src])
                if wd.get(k, 0) < self.cnt[src]:
                    wd[k] = self.cnt[src]
                    self.q[eng].append(("w", self.sem[src], self.cnt[src]))
            for s in range(NDS):
                if self.dcnt[s] == 0:
                    continue
                k = id(self.dsem[s])
                if wd.get(k, 0) < self.dcnt[s]:
                    wd[k] = self.dcnt[s]
                    self.q[eng].append(("w", self.dsem[s], self.dcnt[s]))

    def phase_end(self):
        self.barrier()
        self.flush()


_UID = [0]


def _uniq(name):
    _UID[0] += 1
    return "%s_u%d" % (name, _UID[0])


class Rot:
    def __init__(self, items):
        self.items = items
        self.i = 0

    def next(self):
        it = self.items[self.i % len(self.items)]
        self.i += 1
        return it


def _consts():
    c = {}
    c["ident"] = np.eye(128, dtype=np.float32)
    k = np.arange(128)[:, None]
    q = np.arange(128)[None, :]
    slopes = np.array([2.0 ** (-8.0 * (h + 1) / 8) for h in range(8)], np.float64)
    et = np.zeros((128, 8, 3, 2, 128), np.float32)
    for h in range(8):
        for p, d in enumerate(PATTERNS):
            sc_ = 128.0 ** -0.5
            lo = np.where(k >= q, -slopes[h] * d * np.abs(64 + q - k) / sc_, NEG)
            hi = np.where(k <= q, -slopes[h] * d * np.abs(q - k - 64) / sc_, NEG)
            et[:, h, p, 0, :] = lo
            et[:, h, p, 1, :] = hi
    c["etab"] = et.reshape(128, 8 * 3 * 2 * 128)
    j = np.arange(128)[:, None]
    i = np.arange(128)[None, :]
    same = (j // 64) == (i // 64)
    nm = np.zeros((128, 2, 2, 128), np.float32)
    nm[:, 0, 0, :] = np.where(same & (j < i), 0.0, NEG)
    nm[:, 0, 1, :] = np.where(same & (j <= i), 0.0, NEG)
    nm[:, 1, 0, :] = np.where(same & (j > i), 0.0, NEG)
    nm[:, 1, 1, :] = np.where(same & (j >= i), 0.0, NEG)
    c["negmask"] = nm.reshape(128, 512)
    cm = np.zeros((128, 5, 128), np.float32)
    cm[:, 0, :] = (same & (j <= i))
    cm[:, 1, :] = (same & (j >= i))
    cm[:, 2, :] = same
    cm[:, 3, :] = (j < 64)
    cm[:, 4, :] = (j >= 64)
    c["cmask"] = cm.reshape(128, 640)
    i2 = np.zeros((128, 2, 128), np.float32)
    for t in range(128):
        i2[t, t // 64, t] = 1.0
    c["ident2"] = i2.reshape(128, 256)
    hm = np.zeros((128, 2), np.float32)
    hm[:64, 0] = 1.0
    hm[64:, 1] = 1.0
    c["hmask"] = hm
    return c


def build(seqs, phases="0ABCD", dbg=False, ext_in=False):
    NTOK = sum(seqs)
    assert ext_in or (NTOK % 2048 == 0 and all(s % 2048 == 0 for s in seqs))
    NSB = NTOK // 2048
    nc = bass.Bass("TRN2", target_bir_lowering=False)
    inp = lambda name, shape: nc.dram_tensor(name, shape, F32, kind="ExternalInput").ap()
    x = inp("x", [NTOK, D])
    w_in = inp("w_in", [D, PW])
    w_out = inp("w_out", [D, D])
    gin = inp("gin", [128, 16])
    convw = inp("convw", [128, 24 * 5])
    alog = inp("alog", [128, 32 * 16])
    dtb = inp("dtb", [128, 32 * 16])
    dngain = inp("dngain", [128, 1])
    fgain = inp("fgain", [128, D])
    c_ident = inp("ident", [128, 128])
    c_etab = inp("etab", [128, 8 * 3 * 2 * 128])
    c_negmask = inp("negmask", [128, 512])
    c_cmask = inp("cmask", [128, 640])
    c_ident2 = inp("ident2", [128, 256])
    c_hmask = inp("hmask", [128, 2])
    okind = "ExternalOutput"
    y = nc.dram_tensor("y", [NTOK, D], F32, kind=okind).ap()
    skind = "ExternalOutput" if dbg else "Internal"
    winb = nc.dram_tensor("winb", [128, 16, PW], BF16).ap()
    ikind = "ExternalInput" if ext_in else skind
    FT = nc.dram_tensor("FT", [56, 128, NTOK], BF16, kind=ikind).ap()
    AV = nc.dram_tensor("AV", [NTOK, 1024], BF16, kind=ikind).ap()
    GB = nc.dram_tensor("GB", [NTOK, 32], F32, kind=ikind).ap()
    ROWS = nc.dram_tensor("ROWS", [32, NTOK], F32, kind=skind).ap()
    MT = nc.dram_tensor("MT", [16, 128, NTOK], BF16, kind=skind).ap()

    with contextlib.ExitStack() as gst:
        S = Sched(nc, gst)
        GT = lambda name, shape, dt: gst.enter_context(nc.sbuf_tensor(name, shape, dt))
        identf = GT("identf", [128, 128], F32)
        identb = GT("identb", [128, 128], BF16)
        onesb = GT("onesb", [128, 128], BF16)
        d_const = Dep(const=True)
        S.dma("sp", identf[:], c_ident, writes=[d_const])
        S.op("dve", "tensor_copy", reads=[d_const], writes=[d_const], out=identb[:], in_=identf[:])
        S.op("dve", "memset", writes=[d_const], ap=onesb[:], constant=1.0)
        S.phase_end()

        if "0" in phases:
            _phase0(nc, S, w_in, gin, winb)
        if "A" in phases:
            _phaseA(nc, S, x, winb, FT, AV, GB, NSB, identb, d_const)
        off = 0
        for si, slen in enumerate(seqs):
            if "B" in phases:
                _phaseB(nc, S, FT, AV, MT, off, slen, c_etab, onesb, identb, d_const)
            if "C" in phases:
                _phaseC(nc, S, FT, GB, ROWS, MT, off, slen, convw, alog, dtb, dngain,
                        c_negmask, c_cmask, c_ident2, c_hmask, identf, identb, d_const)
            off += slen
        if "D" in phases:
            _phaseD(nc, S, x, w_out, MT, fgain, y, NTOK, use_mixed=("B" in phases or "C" in phases),
                    heads=(range(16) if ("B" in phases and "C" in phases) else (range(8) if "B" in phases else range(8, 16))))
        S.phase_end()
    return nc


def _phase0(nc, S, w_in, gin, winb):
    with contextlib.ExitStack() as st:
        T = lambda name, shape, dt: st.enter_context(nc.sbuf_tensor(_uniq(name), shape, dt))
        gint = T("gint", [128, 16], F32)
        d_g = Dep()
        S.dma("sp", gint[:], gin, writes=[d_g])
        wf = Rot([(T("p0wf%d" % i, [128, 2056], F32), Dep()) for i in range(3)])
        wb = Rot([(T("p0wb%d" % i, [128, 2056], BF16), Dep()) for i in range(3)])
        n = 0
        for kc in range(16):
            for sl in range(4):
                f, d_f = wf.next()
                b, d_b = wb.next()
                S.dma("sp", f[:], w_in[kc * 128:(kc + 1) * 128, sl * 2056:(sl + 1) * 2056], writes=[d_f])
                if n % 2 == 0:
                    S.op("act", "activation", reads=[d_f, d_g], writes=[d_b], out=b[:], in_=f[:], func=AF.Copy, scale=gint[:, kc:kc + 1])
                else:
                    S.op("dve", "tensor_scalar", reads=[d_f, d_g], writes=[d_b], out=b[:], in0=f[:], scalar1=gint[:, kc:kc + 1], scalar2=None, op0=ALU.mult)
                S.dma("pool", winb[:, kc, sl * 2056:(sl + 1) * 2056], b[:], reads=[d_b])
                n += 1
        S.phase_end()


def _ft_index(cb):
    if cb < 16:
        return cb
    if cb < 24:
        return None
    return cb - 8


def _phaseA(nc, S, x, winb, FT, AV, GB, NSB, identb, d_const):
    with contextlib.ExitStack() as st:
        T = lambda name, shape, dt: st.enter_context(nc.sbuf_tensor(_uniq(name), shape, dt))
        P = lambda name, shape, dt: st.enter_context(nc.psum_tensor(_uniq(name), shape, dt))
        hTs = [(T("hT%d" % i, [128, 16, 2048], BF16), Dep()) for i in range(2)]
        xts = Rot([(T("xt%d" % i, [128, D], F32), Dep()) for i in range(2)])
        hbs = Rot([(T("hb%d" % i, [128, D], BF16), Dep()) for i in range(2)])
        junk = T("junkA", [128, D], BF16)
        d_junk = Dep()
        sms = Rot([(T("smA%d" % i, [128, 2], F32), Dep()) for i in range(2)])
        wsl = Rot([(T("wsl%d" % i, [128, 16, 512], BF16), Dep()) for i in range(2)])
        stg = Rot([(T("stg%d" % i, [128, 2048], BF16), Dep()) for i in range(2)])
        stv = Rot([(T("stv%d" % i, [128, 512], BF16), Dep()) for i in range(3)])
        gbt = T("gbt", [128, 16, 32], F32)
        d_gbt = Dep()
        wtail = T("wtail", [128, 16, 32], BF16)
        d_wtail = Dep()
        ptr = Rot([(P("ptr%d" % i, [128, 4, 128], BF16), Dep()) for i in range(2)])
        pac = Rot([(P("pac%d" % i, [128, 512], F32), Dep()) for i in range(6)])
        S.dma("sp", wtail[:], winb[:, :, 8192:8224], writes=[d_wtail])
        ne = 0

        def a1(sb):
            t0 = sb * 2048
            hT, d_hT = hTs[sb % 2]
            for tt in range(16):
                xt, d_xt = xts.next()
                hb, d_hb = hbs.next()
                sm, d_sm = sms.next()
                S.dma("sp", xt[:], x[t0 + tt * 128:t0 + (tt + 1) * 128, :], writes=[d_xt])
                S.op("act", "activation", reads=[d_xt], writes=[d_junk, d_sm], out=junk[:], in_=xt[:], func=AF.Square, accum_out=sm[:, 0:1])
                S.op("dve", "tensor_scalar", reads=[d_sm], writes=[d_sm], out=sm[:, 1:2], in0=sm[:, 0:1], scalar1=1.0 / D, scalar2=EPS, op0=ALU.mult, op1=ALU.add)
                S.op("act", "activation", reads=[d_sm], writes=[d_sm], out=sm[:, 1:2], in_=sm[:, 1:2], func=AF.Sqrt)
                S.op("dve", "reciprocal", reads=[d_sm], writes=[d_sm], out=sm[:, 1:2], in_=sm[:, 1:2])
                S.op("act", "activation", reads=[d_xt, d_sm], writes=[d_hb], out=hb[:], in_=xt[:], func=AF.Copy, scale=sm[:, 1:2])
                for g in range(4):
                    pt, d_pt = ptr.next()
                    for j in range(4):
                        kc = g * 4 + j
                        S.op("pe", "transpose", reads=[d_hb, d_const], writes=[d_pt], signal=(j == 3),
                             out=pt[:, j, :], in_=hb[:, kc * 128:(kc + 1) * 128], identity=identb[:])
                    S.op("dve", "tensor_copy", reads=[d_pt], writes=[d_hT],
                         out=hT[:, g * 4:(g + 1) * 4, tt * 128:(tt + 1) * 128], in_=pt[:])
                yield

        for _ in a1(0):
            pass
        for sb in range(NSB):
            t0 = sb * 2048
            hT, d_hT = hTs[sb % 2]
            bg = a1(sb + 1) if sb + 1 < NSB else None
            for slab in range(16):
                if bg is not None:
                    try:
                        next(bg)
                    except StopIteration:
                        bg = None
                ws, d_ws = wsl.next()
                S.dma("sp", ws[:], winb[:, :, slab * 512:(slab + 1) * 512], writes=[d_ws])
                if 4 <= slab < 6:
                    for tt in range(16):
                        pa, d_pa = pac.next()
                        for kc in range(16):
                            S.op("pe", "matmul", reads=[d_hT, d_ws], writes=[d_pa], signal=(kc == 15),
                                 out=pa[:], lhsT=hT[:, kc, tt * 128:(tt + 1) * 128], rhs=ws[:, kc, :], start=(kc == 0), stop=(kc == 15))
                        sv, d_sv = stv.next()
                        if ne % 2 == 0:
                            S.op("act", "activation", reads=[d_pa], writes=[d_sv], out=sv[:], in_=pa[:], func=AF.Copy)
                        else:
                            S.op("dve", "tensor_copy", reads=[d_pa], writes=[d_sv], out=sv[:], in_=pa[:])
                        ne += 1
                        S.dma("pool", AV[t0 + tt * 128:t0 + (tt + 1) * 128, (slab - 4) * 512:(slab - 3) * 512], sv[:], reads=[d_sv])
                    continue
                for cbi in range(4):
                    cb = slab * 4 + cbi
                    pas = [pac.next() for _ in range(4)]
                    for kc in range(16):
                        for tb in range(4):
                            pa, d_pa = pas[tb]
                            S.op("pe", "matmul", reads=[d_hT, d_ws], writes=[d_pa], signal=(kc == 15),
                                 out=pa[:], lhsT=ws[:, kc, cbi * 128:(cbi + 1) * 128], rhs=hT[:, kc, tb * 512:(tb + 1) * 512],
                                 start=(kc == 0), stop=(kc == 15))
                    sg, d_sg = stg.next()
                    for tb in range(4):
                        pa, d_pa = pas[tb]
                        if ne % 2 == 0:
                            S.op("act", "activation", reads=[d_pa], writes=[d_sg], out=sg[:, tb * 512:(tb + 1) * 512], in_=pa[:], func=AF.Copy)
                        else:
                            S.op("dve", "tensor_copy", reads=[d_pa], writes=[d_sg], out=sg[:, tb * 512:(tb + 1) * 512], in_=pa[:])
                        ne += 1
                    S.dma("pool", FT[_ft_index(cb), :, t0:t0 + 2048], sg[:], reads=[d_sg])
            if bg is not None:
                for _ in bg:
                    pass
            for tt in range(16):
                pa, d_pa = pac.next()
                for kc in range(16):
                    S.op("pe", "matmul", reads=[d_hT, d_wtail], writes=[d_pa], signal=(kc == 15),
                         out=pa[:, 0:32], lhsT=hT[:, kc, tt * 128:(tt + 1) * 128], rhs=wtail[:, kc, :], start=(kc == 0), stop=(kc == 15))
                S.op("dve", "tensor_copy", reads=[d_pa], writes=[d_gbt], out=gbt[:, tt, :], in_=pa[:, 0:32])
            for t8 in range(2):
                S.dma("pool", GB[t0 + t8 * 1024:t0 + (t8 + 1) * 1024, :].rearrange("(t p) c -> p t c", p=128), gbt[:, t8 * 8:(t8 + 1) * 8, :], reads=[d_gbt])
        S.phase_end()


def _phaseD(nc, S, x, w_out, MT, fgain, y, NTOK, use_mixed, heads):
    heads = list(heads)
    with contextlib.ExitStack() as st:
        T = lambda name, shape, dt: st.enter_context(nc.sbuf_tensor(_uniq(name), shape, dt))
        P = lambda name, shape, dt: st.enter_context(nc.psum_tensor(_uniq(name), shape, dt))
        fg = T("fg", [128, D], F32)
        d_fg = Dep()
        S.dma("sp", fg[:], fgain, writes=[d_fg])
        wo = T("wo", [128, 16, D], BF16)
        d_wo = Dep()
        if use_mixed:
            wf = Rot([(T("dwf%d" % i, [128, D], F32), Dep()) for i in range(2)])
            for kc in range(16):
                f, d_f = wf.next()
                S.dma("sp", f[:], w_out[kc * 128:(kc + 1) * 128, :], writes=[d_f])
                if kc % 2 == 0:
                    S.op("act", "activation", reads=[d_f], writes=[d_wo], out=wo[:, kc, :], in_=f[:], func=AF.Copy)
                else:
                    S.op("dve", "tensor_copy", reads=[d_f], writes=[d_wo], out=wo[:, kc, :], in_=f[:])
        xts = Rot([(T("dxt%d" % i, [128, D], F32), Dep()) for i in range(2)])
        yts = Rot([(T("dyt%d" % i, [128, D], F32), Dep()) for i in range(2)])
        mts = Rot([(T("dmt%d" % i, [128, 16, 512], BF16), Dep()) for i in range(2)])
        junk = T("junkD", [128, D], BF16)
        d_junk = Dep()
        sms = Rot([(T("smD%d" % i, [128, 2], F32), Dep()) for i in range(2)])
        pac = Rot([(P("dpac%d" % i, [128, 512], F32), Dep()) for i in range(6)])
        for t4 in range(NTOK // 512):
            if use_mixed:
                mt, d_mt = mts.next()
                for h in heads:
                    S.dma("sp", mt[:, h, :], MT[h, :, t4 * 512:(t4 + 1) * 512], writes=[d_mt])
            for ti in range(4):
                tt = t4 * 4 + ti
                xt, d_xt = xts.next()
                yt, d_yt = yts.next()
                sm, d_sm = sms.next()
                S.dma("sp", xt[:], x[tt * 128:(tt + 1) * 128, :], writes=[d_xt])
                if use_mixed:
                    for cb in range(4):
                        pa, d_pa = pac.next()
                        for n, h in enumerate(heads):
                            S.op("pe", "matmul", reads=[d_mt, d_wo], writes=[d_pa], signal=(n == len(heads) - 1),
                                 out=pa[:], lhsT=mt[:, h, ti * 128:(ti + 1) * 128], rhs=wo[:, h, cb * 512:(cb + 1) * 512],
                                 start=(n == 0), stop=(n == len(heads) - 1))
                        S.op("dve", "tensor_tensor", reads=[d_pa, d_xt], writes=[d_yt],
                             out=yt[:, cb * 512:(cb + 1) * 512], in0=pa[:], in1=xt[:, cb * 512:(cb + 1) * 512], op=ALU.add)
                    src, d_src = yt, d_yt
                else:
                    src, d_src = xt, d_xt
                S.op("act", "activation", reads=[d_src], writes=[d_junk, d_sm], out=junk[:], in_=src[:], func=AF.Square, accum_out=sm[:, 0:1])
                S.op("dve", "tensor_scalar", reads=[d_sm], writes=[d_sm], out=sm[:, 1:2], in0=sm[:, 0:1], scalar1=1.0 / D, scalar2=EPS, op0=ALU.mult, op1=ALU.add)
                S.op("act", "activation", reads=[d_sm], writes=[d_sm], out=sm[:, 1:2], in_=sm[:, 1:2], func=AF.Sqrt)
                S.op("dve", "reciprocal", reads=[d_sm], writes=[d_sm], out=sm[:, 1:2], in_=sm[:, 1:2])
                S.op("dve", "scalar_tensor_tensor", reads=[d_src, d_sm, d_fg], writes=[d_yt],
                     out=yt[:], in0=src[:], scalar=sm[:, 1:2], in1=fg[:], op0=ALU.mult, op1=ALU.mult)
                S.dma("pool", y[tt * 128:(tt + 1) * 128, :], yt[:], reads=[d_yt])
        S.phase_end()


def _sl(r, d, j0, j1):
    return slice(r + d * j0, r + d * (j1 - 1) + 1, d)


def _phaseB(nc, S, FT, AV, MT, off, slen, c_etab, onesb, identb, d_const, heads=range(8)):
    nb = slen // 128
    scale = 128.0 ** -0.5
    with contextlib.ExitStack() as st:
        T = lambda name, shape, dt: st.enter_context(nc.sbuf_tensor(_uniq(name), shape, dt))
        P = lambda name, shape, dt: st.enter_context(nc.psum_tensor(_uniq(name), shape, dt))
        etf = T("etf", [128, 768], F32)
        d_etf = Dep()
        etb = T("etb", [128, 8, 3, 2, 128], BF16)
        d_etb = Dep()
        for h in range(8):
            S.dma("sp", etf[:], c_etab[:, h * 768:(h + 1) * 768], writes=[d_etf])
            S.op("dve", "tensor_copy", reads=[d_etf], writes=[d_etb], out=etb[:, h].rearrange("p a b c -> p (a b c)"), in_=etf[:])
        qkg = Rot([([T("bq%d" % i, [128, slen], BF16), T("bk%d" % i, [128, slen], BF16), T("bg%d" % i, [128, slen], BF16)], Dep()) for i in range(2)])
        vts = Rot([([T("bv%d_%d" % (i, d), [128, d, nb // d, 128], BF16) for d in PATTERNS], Dep()) for i in range(2)])
        acc = T("bacc", [128, 2, slen], F32)
        d_acc = Dep()
        pes = Rot([(T("bpe%d" % i, [128, 2, 128], BF16), Dep()) for i in range(6)])
        tmps = Rot([(T("btm%d" % i, [128, 2, 128], F32), Dep()) for i in range(3)])
        nst = [0]
        sgt = T("bsg", [128, slen], BF16)
        d_sgt = Dep()
        outt = T("bout", [128, slen], BF16)
        d_out = Dep()
        pst = Rot([(P("bst%d" % i, [128, 2, 128], F32), Dep()) for i in range(3)])
        pol = Rot([(P("bol%d" % i, [128, 2, 256], F32), Dep()) for i in range(3)])
        for h in heads:
            (qt_, kt_, gt_), d_qkg = qkg.next()
            vt, d_v = vts.next()
            S.dma("sp", qt_[:], FT[h, :, off:off + slen], writes=[d_qkg])
            S.dma("sp", kt_[:], FT[8 + h, :, off:off + slen], writes=[d_qkg])
            S.dma("sp", gt_[:], FT[16 + h, :, off:off + slen], writes=[d_qkg])
            for pi, d in enumerate(PATTERNS):
                njh = nb // d
                for r in range(d):
                    src = AV[off:off + slen, h * 128:(h + 1) * 128].rearrange("(jh p r) c -> p r jh c", p=128, r=d)[:, r]
                    for j0 in range(0, njh, 8):
                        j1 = min(njh, j0 + 8)
                        S.dma("sp", vt[pi][:, r, j0:j1], src[:, j0:j1], writes=[d_v])
            tiles = []
            for pi, d in enumerate(PATTERNS):
                L = slen // d
                nkb = L // 128
                for r in range(d):
                    for qt in range(nkb + 1):
                        tiles.append((pi, d, L, nkb, r, qt))
            LAG = 3
            inflight = []
            groups = []
            for ti, (pi, d, L, nkb, r, qt) in enumerate(tiles):
                q0 = max(0, 128 * qt - 64)
                q1 = min(L, 128 * qt + 64)
                if groups and groups[-1]["key"] == (pi, r) and groups[-1]["q1"] == q0 and (q1 - groups[-1]["q0"]) <= 256:
                    groups[-1]["q1"] = q1
                    groups[-1]["last"] = ti
                else:
                    groups.append(dict(key=(pi, r), q0=q0, q1=q1, last=ti, po=None))
                tiles[ti] = (pi, d, L, nkb, r, qt, groups[-1])

            def stage2(ent):
                (ti, pi, d, r, q0, q1, nq, blocks, pe, d_pe, grp) = ent
                if grp["po"] is None:
                    grp["po"] = pol.next()
                po, d_po = grp["po"]
                go = q0 - grp["q0"]
                for bi, (b, kb) in enumerate(blocks):
                    S.op("pe", "matmul", reads=[d_pe, d_v], writes=[d_po], signal=False,
                         out=po[:, 0, go:go + nq], lhsT=vt[pi][:, r, kb, :], rhs=pe[:, b, 0:nq], start=(bi == 0), stop=(bi == len(blocks) - 1))
                for bi, (b, kb) in enumerate(blocks):
                    S.op("pe", "matmul", reads=[d_pe, d_const], writes=[d_po], signal=(bi == len(blocks) - 1),
                         out=po[:, 1, go:go + nq], lhsT=onesb[:], rhs=pe[:, b, 0:nq], start=(bi == 0), stop=(bi == len(blocks) - 1))
                if ti != grp["last"]:
                    return
                gn = grp["q1"] - grp["q0"]
                aap = acc[:, :, _sl(r, d, grp["q0"], grp["q1"])]
                if pi == 0:
                    S.op("dve", "tensor_copy", reads=[d_po], writes=[d_acc], out=aap, in_=po[:, :, 0:gn])
                else:
                    S.op("dve", "tensor_tensor", reads=[d_po, d_acc], writes=[d_acc], out=aap, in0=po[:, :, 0:gn], in1=aap, op=ALU.add)

            for ti, (pi, d, L, nkb, r, qt, grp) in enumerate(tiles):
                q0 = max(0, 128 * qt - 64)
                q1 = min(L, 128 * qt + 64)
                nq = q1 - q0
                qoff = 64 if qt == 0 else 0
                blocks = []
                if qt >= 1:
                    blocks.append((0, qt - 1))
                if qt < nkb:
                    blocks.append((1, qt))
                ps, d_ps = pst.next()
                pe, d_pe = pes.next()
                qap = qt_[:, _sl(r, d, q0, q1)]
                for bi, (b, kb) in enumerate(blocks):
                    S.op("pe", "matmul", reads=[d_qkg], writes=[d_ps], signal=False,
                         out=ps[:, b, 0:nq], lhsT=kt_[:, _sl(r, d, 128 * kb, 128 * (kb + 1))], rhs=qap, start=True, stop=False)
                    S.op("pe", "matmul", reads=[d_etb, d_const], writes=[d_ps], signal=(bi == len(blocks) - 1),
                         out=ps[:, b, 0:nq], lhsT=identb[:], rhs=etb[:, h, pi, b, qoff:qoff + nq], start=False, stop=True)
                b0 = blocks[0][0]
                b1 = blocks[-1][0] + 1
                S.op("act", "activation", reads=[d_ps], writes=[d_pe], out=pe[:, b0:b1, 0:nq], in_=ps[:, b0:b1, 0:nq], func=AF.Exp, scale=scale)
                inflight.append((ti, pi, d, r, q0, q1, nq, blocks, pe, d_pe, grp))
                if len(inflight) > LAG:
                    stage2(inflight.pop(0))
            while inflight:
                stage2(inflight.pop(0))
            S.op("act", "activation", reads=[d_acc], writes=[d_acc], out=acc[:, 1, :], in_=acc[:, 1, :], func=AF.Ln)
            S.op("act", "activation", reads=[d_acc], writes=[d_acc], out=acc[:, 1, :], in_=acc[:, 1, :], func=AF.Exp, scale=-1.0)
            S.op("act", "activation", reads=[d_qkg], writes=[d_sgt], out=sgt[:], in_=gt_[:], func=AF.Silu)
            S.op("dve", "tensor_tensor", reads=[d_acc], writes=[d_acc], out=acc[:, 0, :], in0=acc[:, 0, :], in1=acc[:, 1, :], op=ALU.mult)
            S.op("dve", "tensor_tensor", reads=[d_acc, d_sgt], writes=[d_out], out=outt[:], in0=acc[:, 0, :], in1=sgt[:], op=ALU.mult)
            S.dma("pool", MT[h, :, off:off + slen], outt[:], reads=[d_out])
        S.phase_end()


def _phaseC(nc, S, FT, GB, ROWS, MT, off, slen, convw, alog, dtb, dngain,
            c_negmask, c_cmask, c_ident2, c_hmask, identf, identb, d_const, heads=range(8)):
    nb = slen // 128
    NTOK = ROWS.shape[1]
    qscale = 128.0 ** -0.5
    with contextlib.ExitStack() as st:
        T = lambda name, shape, dt: st.enter_context(nc.sbuf_tensor(_uniq(name), shape, dt))
        P = lambda name, shape, dt: st.enter_context(nc.psum_tensor(_uniq(name), shape, dt))
        d_c = Dep()
        negm = T("negm", [128, 2, 256], F32)
        cmask = T("cmask", [128, 5, 128], F32)
        id2f = T("id2f", [128, 256], F32)
        ident2b = T("ident2b", [128, 256], BF16)
        hmask = T("hmask", [128, 2], F32)
        cw = T("cw", [128, 24, 5], F32)
        dng = T("dng", [128, 1], F32)
        S.dma("sp", negm[:].rearrange("p a b -> p (a b)"), c_negmask, writes=[d_c])
        S.dma("sp", cmask[:].rearrange("p a b -> p (a b)"), c_cmask, writes=[d_c])
        S.dma("sp", id2f[:], c_ident2, writes=[d_c])
        S.dma("sp", hmask[:], c_hmask, writes=[d_c])
        S.dma("sp", cw[:].rearrange("p a b -> p (a b)"), convw, writes=[d_c])
        S.dma("sp", dng[:], dngain, writes=[d_c])
        S.op("dve", "tensor_copy", reads=[d_c], writes=[d_c], out=ident2b[:], in_=id2f[:])
        gc = T("gc", [128, nb, 16], F32)
        egc = T("egc", [128, nb, 16], F32)
        edec = T("edec", [128, nb, 16], F32)
        beta = T("beta", [128, nb, 16], F32)
        sdec = [T("sdec0", [128, nb, 16], F32), T("sdec1", [128, nb, 16], F32)]
        begc = T("begc", [128, nb, 16], F32)
        d_gate = Dep()
        bank = [P("cbank%d" % i, [128, 512], F32) for i in range(8)]
        d_bank = [Dep() for _ in range(8)]
        bankb = [bank[i][:].bitcast(BF16) for i in range(8)]
        with contextlib.ExitStack() as st2:
            T2 = lambda name, shape, dt: st2.enter_context(nc.sbuf_tensor(_uniq(name), shape, dt))
            gbt = T2("gbt", [128, nb, 32], F32)
            al = T2("al", [128, nb, 16], F32)
            db = T2("db", [128, nb, 16], F32)
            t1 = T2("t1", [128, nb, 16], F32)
            t2 = T2("t2", [128, nb, 16], F32)
            z = T2("z", [128, nb, 16], F32)
            lnb = T2("lnb", [128, nb, 16], F32)
            g = T2("g", [128, nb, 16], F32)
            r1 = T2("r1", [128, nb, 16], F32)
            rowst = T2("rowst", [32, 32, 128], F32)
            d_g = Dep()
            S.dma("sp", gbt[:], GB[off:off + slen, :].rearrange("(t p) c -> p t c", p=128), writes=[d_g])
            S.dma("sp", al[:].rearrange("p a b -> p (a b)"), alog[:, 0:nb * 16], writes=[d_g])
            S.dma("sp", db[:].rearrange("p a b -> p (a b)"), dtb[:, 0:nb * 16], writes=[d_g])
            braw = gbt[:, :, 0:16]
            araw = gbt[:, :, 16:32]
            G = dict(reads=[d_g], writes=[d_g])
            S.op("dve", "scalar_tensor_tensor", out=t1[:], in0=braw, scalar=-1.0, in1=braw, op0=ALU.mult, op1=ALU.max, **G)
            S.op("act", "activation", out=t1[:], in_=t1[:], func=AF.Exp, scale=-1.0, **G)
            S.op("dve", "tensor_scalar", out=t1[:], in0=t1[:], scalar1=1.0, scalar2=None, op0=ALU.add, **G)
            S.op("act", "activation", out=t1[:], in_=t1[:], func=AF.Ln, **G)
            S.op("dve", "scalar_tensor_tensor", out=lnb[:], in0=braw, scalar=0.0, in1=t1[:], op0=ALU.min, op1=ALU.subtract, **G)
            S.op("act", "activation", reads=[d_g], writes=[d_gate], out=beta[:], in_=lnb[:], func=AF.Exp)
            S.op("dve", "tensor_tensor", out=z[:], in0=araw, in1=db[:], op=ALU.add, **G)
            S.op("dve", "scalar_tensor_tensor", out=t2[:], in0=z[:], scalar=-1.0, in1=z[:], op0=ALU.mult, op1=ALU.max, **G)
            S.op("act", "activation", out=t2[:], in_=t2[:], func=AF.Exp, scale=-1.0, **G)
            S.op("dve", "tensor_scalar", out=t2[:], in0=t2[:], scalar1=1.0, scalar2=None, op0=ALU.add, **G)
            S.op("act", "activation", out=t2[:], in_=t2[:], func=AF.Ln, **G)
            S.op("dve", "scalar_tensor_tensor", out=t2[:], in0=z[:], scalar=0.0, in1=t2[:], op0=ALU.max, op1=ALU.add, **G)
            S.op("act", "activation", out=al[:], in_=al[:], func=AF.Exp, **G)
            S.op("dve", "scalar_tensor_tensor", out=g[:], in0=t2[:], scalar=-1.0, in1=al[:], op0=ALU.mult, op1=ALU.mult, **G)
            d_pb = [Dep() for _ in range(5)]
            pgi = [0, 1, 2, 3, 5]
            pg2 = [bank[i][:, 0:nb * 16] for i in pgi]
            pg = [bank[i][:, 0:nb * 16].rearrange("p (a b) -> p a b", b=16) for i in pgi]
            g2 = g[:].rearrange("p a b -> p (a b)")
            S.op("pe", "matmul", reads=[d_g, d_c], writes=[d_pb[0], d_bank[0]], signal=True, out=pg2[0], lhsT=cmask[:, 0, :], rhs=g2, start=True, stop=True)
            S.op("pe", "matmul", reads=[d_g, d_c], writes=[d_pb[4], d_bank[5]], signal=True, out=pg2[4], lhsT=cmask[:, 1, :], rhs=g2, start=True, stop=True)
            for i in range(1, 4):
                S.op("pe", "matmul", reads=[d_g, d_c], writes=[d_pb[i], d_bank[i]], signal=True,
                     out=pg2[i], lhsT=cmask[:, 1 + i, :], rhs=g2, start=True, stop=True)
            S.op("dve", "tensor_copy", reads=[d_pb[0]], writes=[d_gate, d_bank[0]], out=gc[:, :, 0:8], in_=pg[0][:, :, 0:8])
            S.op("dve", "tensor_copy", reads=[d_pb[4]], writes=[d_gate, d_bank[5]], out=gc[:, :, 8:16], in_=pg[4][:, :, 8:16])
            S.op("act", "activation", reads=[d_gate], writes=[d_gate], out=egc[:], in_=gc[:], func=AF.Exp)
            S.op("dve", "tensor_tensor", reads=[d_gate], writes=[d_gate], out=begc[:], in0=egc[:], in1=beta[:], op=ALU.mult)
            S.op("dve", "tensor_tensor", reads=[d_pb[1], d_gate], writes=[d_gate, d_bank[1]], out=edec[:], in0=pg[1], in1=gc[:], op=ALU.subtract)
            S.op("act", "activation", reads=[d_gate], writes=[d_gate], out=edec[:], in_=edec[:], func=AF.Exp)
            S.op("act", "activation", reads=[d_pb[2]], writes=[d_gate, d_bank[2]], out=sdec[0][:], in_=pg[2], func=AF.Exp)
            S.op("act", "activation", reads=[d_pb[3]], writes=[d_gate, d_bank[3]], out=sdec[1][:], in_=pg[3], func=AF.Exp)
            S.op("dve", "tensor_tensor", reads=[d_gate, d_g], writes=[d_g], out=r1[:], in0=gc[:], in1=lnb[:], op=ALU.add)
            d_rowst = Dep()
            prow = [(bank[4][0:nb, 0:128], Dep()), (bank[4][0:nb, 128:256], Dep()), (bank[4][0:nb, 256:384], Dep()), (bank[4][0:nb, 384:512], Dep())]
            for q in range(32):
                dh, which = q // 2, q % 2
                src = (r1 if which == 0 else gc)[:, :, dh]
                pr, d_pr = prow[q % 4]
                S.op("pe", "transpose", reads=[d_g, d_gate, d_const], writes=[d_pr, d_bank[4]], out=pr, in_=src, identity=identf[:])
                if q % 2 == 0:
                    S.op("act", "activation", reads=[d_pr], writes=[d_rowst, d_bank[4]], out=rowst[0:nb, q, :], in_=pr, func=AF.Copy)
                else:
                    S.op("dve", "tensor_copy", reads=[d_pr], writes=[d_rowst, d_bank[4]], out=rowst[0:nb, q, :], in_=pr)
            d_rows = Dep()
            S.dma("sp", ROWS[:, off:off + slen].rearrange("q (b i) -> b q i", i=128), rowst[0:nb], reads=[d_rowst], writes=[d_rows])
            S.phase_end()

        W = 6
        xin = T("cx", [128, slen + 4], BF16)
        d_xin = Dep()
        S.op("pool", "memset", writes=[d_xin], ap=xin[:, 0:2], constant=0.0)
        S.op("pool", "memset", writes=[d_xin], ap=xin[:, slen + 2:slen + 4], constant=0.0)
        CH = min(1024, slen)
        caccs = Rot([(T("cacc%d" % i, [128, CH], F32), Dep()) for i in range(2)])
        ybf = T("cy", [128, slen], BF16)
        d_ybf = Dep()
        ytms = [(T("ytm%d" % i, [128, nb, 3, 128], BF16), Dep()) for i in range(2)]
        sss = [(T("css%d" % i, [128, nb, 2], F32), Dep()) for i in range(2)]
        junkc = T("cjunk", [128, 128], BF16)
        d_junk = Dep()
        oacc = T("oacc", [128, nb, 128], F32)
        d_oacc = Dep()
        oss = T("coss", [128, 2, nb], F32)
        d_oss = Dep()
        sgT = T("csg", [128, slen], BF16)
        d_sgT = Dep()
        ontm = Rot([(T("con%d" % i, [128, 128], BF16), Dep()) for i in range(2)])
        vnew = [[(T("cvn%d_%d" % (i, c), [128, 128], BF16), Dep()) for c in range(2)] for i in range(2)]
        Sf = [(T("cSf%d" % i, [128, 128], F32), Dep()) for i in range(2)]
        Sbf = [(T("cSb%d" % i, [128, 128], BF16), Dep()) for i in range(2)]

        class Slot:
            pass

        slots = []
        for si in range(W):
            sl = Slot()
            mk = lambda name, shape, dt: (T("%s_s%d" % (name, si), shape, dt), Dep())
            sl.arg = mk("carg", [128, 256], F32)
            sl.at = mk("cat", [128, 128], F32)
            sl.it = mk("cit", [128, 128], BF16)
            sl.asb = mk("casb", [128, 128], F32)
            sl.p0 = mk("cp0", [128, 128], F32)
            sl.bp = [mk("cbp%d" % i, [128, 256], F32) for i in range(2)]
            sl.bt = [mk("cbt%d" % i, [128, 128], F32) for i in range(2)]
            sl.p4 = mk("cp4", [128, 128], F32)
            sl.pbf = mk("cpbf", [128, 128], BF16)
            sl.kbg = mk("ckbg", [128, 128], BF16)
            sl.vb = mk("cvb", [128, 128], BF16)
            sl.kd = mk("ckd", [128, 128], BF16)
            sl.yqg = mk("cyqg", [128, 128], BF16)
            sl.us = mk("cus", [128, 128], F32)
            sl.wt = mk("cwt", [128, 128], BF16)
            sl.qg = mk("cqg", [128, 256], BF16)
            slots.append(sl)

        class Reg:
            def __init__(self, b):
                self.f = bank[b]
                self.h = bankb[b]
                self.d = d_bank[b]

            def W(self):
                return [self.d]

        p_o = [Reg(2), Reg(3)]
        pany = Rot([Reg(b) for b in (0, 1, 4, 5, 6, 7)])
        kkalls = [(T("kkall%d" % i, [128, nb, 256], BF16), Dep()) for i in range(2)]
        kqs = Rot([(T("ckq%d" % i, [128, 256], BF16), Dep()) for i in range(2)])
        chain_done = [0, 0]

        def blockdir(sl, h, dr, it, ytm, d_ytm, kkall, d_kkall):
            blk = it if dr == 0 else nb - 1 - it
            dh = dr * 8 + h
            arg, d_arg = sl.arg
            S.dma("sp", arg[:].rearrange("p (a b) -> p a b", b=128), bass.AP(ROWS.tensor, (2 * dh) * NTOK + off + blk * 128, [[0, 128], [NTOK, 2], [1, 128]]),
                  reads=[d_rows], writes=[d_arg])
            kbg, d_kbg = sl.kbg
            vb, d_vb = sl.vb
            kd, d_kd = sl.kd
            yqg, d_yqg = sl.yqg
            S.op("act", "activation", reads=[d_ytm, d_gate], writes=[d_yqg], out=yqg[:], in_=ytm[:, blk, 0, :], func=AF.Copy, scale=egc[:, blk, dh:dh + 1])
            S.op("dve", "tensor_scalar", reads=[d_ytm, d_gate], writes=[d_kbg], out=kbg[:], in0=ytm[:, blk, 1, :], scalar1=begc[:, blk, dh:dh + 1], scalar2=None, op0=ALU.mult)
            yield
            gg = arg
            d_gg = d_arg
            pq = pany.next()
            S.op("pe", "matmul", reads=[d_yqg, d_c], writes=pq.W(), signal=True, out=pq.f[:, 0:256], lhsT=yqg[:], rhs=ident2b[:], start=True, stop=True)
            qg, d_qg = sl.qg
            S.op("act", "activation", reads=[pq.d], writes=[d_qg, pq.d], out=qg[:], in_=pq.f[:, 0:256], func=AF.Copy)
            S.op("act", "activation", reads=[d_ytm, d_gate], writes=[d_vb], out=vb[:], in_=ytm[:, blk, 2, :], func=AF.Copy, scale=beta[:, blk, dh:dh + 1])
            S.op("dve", "tensor_scalar", reads=[d_ytm, d_gate], writes=[d_kd], out=kd[:], in0=ytm[:, blk, 1, :], scalar1=edec[:, blk, dh:dh + 1], scalar2=None, op0=ALU.mult)
            yield
            S.op("dve", "scalar_tensor_tensor", reads=[d_arg, d_gate, d_c], writes=[d_arg], out=arg[:], in0=arg[:],
                 scalar=gc[:, blk, dh:dh + 1], in1=negm[:, dr, :], op0=ALU.subtract, op1=ALU.add)
            S.op("act", "activation", reads=[d_arg], writes=[d_arg], out=arg[:], in_=arg[:], func=AF.Exp)
            yield
            at, d_at = sl.at
            itt, d_it = sl.it
            S.op("dve", "tensor_tensor", reads=[d_kkall, d_gg], writes=[d_at], out=at[:], in0=kkall[:, blk, 0:128], in1=gg[:, 0:128], op=ALU.mult)
            S.op("pool", "tensor_tensor", reads=[d_kkall, d_gg], writes=[d_it], out=itt[:], in0=kkall[:, blk, 128:256], in1=gg[:, 128:256], op=ALU.mult)
            pA = pany.next()
            S.op("pe", "transpose", reads=[d_at, d_const], writes=pA.W(), signal=True, out=pA.f[:, 0:128], in_=at[:], identity=identf[:])
            asb, d_asb = sl.asb
            S.op("act", "activation", reads=[pA.d], writes=[d_asb, pA.d], out=asb[:], in_=pA.f[:, 0:128], func=AF.Copy)
            p0, d_p0 = sl.p0
            S.op("pool", "tensor_tensor", reads=[d_at, d_const], writes=[d_p0], out=p0[:], in0=identf[:], in1=at[:], op=ALU.subtract)
            yield
            bp1, d_bp1 = sl.bp[1]
            px = pany.next()
            S.op("pe", "matmul", reads=[d_asb, d_at], writes=px.W(), signal=True, out=px.f[:, 0:128], lhsT=asb[:], rhs=at[:], start=True, stop=True)
            py = pany.next()
            S.op("pe", "matmul", reads=[d_asb, d_at], writes=py.W(), signal=True, out=py.f[:, 0:128], lhsT=at[:], rhs=asb[:], start=True, stop=True)
            S.op("dve", "tensor_copy", reads=[px.d], writes=[d_bp1, px.d], out=bp1[:, 0:128], in_=px.f[:, 0:128])
            S.op("pool", "tensor_copy", reads=[d_p0], writes=[d_bp1], out=bp1[:, 128:256], in_=p0[:])
            bt, d_bt = sl.bt[1]
            S.op("act", "activation", reads=[py.d], writes=[d_bt, py.d], out=bt[:], in_=py.f[:, 0:128], func=AF.Copy)
            yield
            cur, d_cur = bp1, d_bp1
            for k in range(1, 4):
                nxt, d_nxt = sl.bp[(k + 1) % 2]
                px = pany.next()
                S.op("pe", "matmul", reads=[d_bt, d_cur], writes=px.W(), signal=True, out=px.f[:, 0:256], lhsT=bt[:], rhs=cur[:], start=True, stop=True)
                if k % 2 == 1:
                    S.op("dve", "tensor_copy", reads=[px.d], writes=[d_nxt, px.d], out=nxt[:], in_=px.f[:, 0:256])
                else:
                    S.op("act", "activation", reads=[px.d], writes=[d_nxt, px.d], out=nxt[:], in_=px.f[:, 0:256], func=AF.Copy)
                S.op("pool", "tensor_tensor", reads=[d_cur, d_nxt], writes=[d_nxt], out=nxt[:, 128:256], in0=nxt[:, 128:256], in1=cur[:, 128:256], op=ALU.add)
                yield
                py = pany.next()
                S.op("pe", "matmul", reads=[d_bt, d_cur], writes=py.W(), signal=True, out=py.f[:, 0:128], lhsT=cur[:, 0:128], rhs=bt[:], start=True, stop=True)
                nbt, d_nbt = sl.bt[(k + 1) % 2]
                if k % 2 == 1:
                    S.op("act", "activation", reads=[py.d], writes=[d_nbt, py.d], out=nbt[:], in_=py.f[:, 0:128], func=AF.Copy)
                else:
                    S.op("dve", "tensor_copy", reads=[py.d], writes=[d_nbt, py.d], out=nbt[:], in_=py.f[:, 0:128])
                cur, d_cur = nxt, d_nxt
                bt, d_bt = nbt, d_nbt
                yield
            px = pany.next()
            S.op("pe", "matmul", reads=[d_bt, d_cur], writes=px.W(), signal=True, out=px.f[:, 0:128], lhsT=bt[:], rhs=cur[:, 128:256], start=True, stop=True)
            py = pany.next()
            S.op("pe", "matmul", reads=[d_bt, d_cur], writes=py.W(), signal=True, out=py.f[:, 0:128], lhsT=cur[:, 0:128], rhs=bt[:], start=True, stop=True)
            p4, d_p4 = sl.p4
            S.op("dve", "tensor_tensor", reads=[px.d, d_cur], writes=[d_p4, px.d], out=p4[:], in0=px.f[:, 0:128], in1=cur[:, 128:256], op=ALU.add)
            bt5, d_bt5 = sl.bt[1]
            S.op("act", "activation", reads=[py.d], writes=[d_bt5, py.d], out=bt5[:], in_=py.f[:, 0:128], func=AF.Copy)
            yield
            px = pany.next()
            S.op("pe", "matmul", reads=[d_bt5, d_p4], writes=px.W(), signal=True, out=px.f[:, 0:128], lhsT=bt5[:], rhs=p4[:], start=True, stop=True)
            pbf, d_pbf = sl.pbf
            S.op("dve", "tensor_tensor", reads=[px.d, d_p4], writes=[d_pbf, px.d], out=pbf[:], in0=px.f[:, 0:128], in1=p4[:], op=ALU.add)
            yield
            pu = pany.next()
            S.op("pe", "matmul", reads=[d_pbf, d_vb], writes=pu.W(), signal=True, out=pu.f[:, 0:128], lhsT=pbf[:], rhs=vb[:], start=True, stop=True)
            us, d_us = sl.us
            S.op("act", "activation", reads=[pu.d], writes=[d_us, pu.d], out=us[:], in_=pu.f[:, 0:128], func=AF.Copy)
            pw = pany.next()
            S.op("pe", "matmul", reads=[d_pbf, d_kbg], writes=pw.W(), signal=True, out=pw.f[:, 0:128], lhsT=kbg[:], rhs=pbf[:], start=True, stop=True)
            wt, d_wt = sl.wt
            S.op("dve", "tensor_copy", reads=[pw.d], writes=[d_wt, pw.d], out=wt[:], in_=pw.f[:, 0:128])
            yield
            while chain_done[dr] < it:
                yield
            sf, d_sf = Sf[dr]
            sb, d_sb = Sbf[dr]
            po = p_o[dr]
            order = (0, 1) if dr == 0 else (1, 0)
            for n, c in enumerate(order):
                vn, d_vn = vnew[dr][c]
                pws = pany.next()
                rows = slice(64 * c, 64 * c + 64)
                S.op("pe", "matmul", reads=[d_wt, d_sb], writes=pws.W(), signal=True, out=pws.f[:, 0:128], lhsT=wt[:], rhs=sb[:], start=True, stop=True)
                S.op("pe", "matmul", reads=[d_qg, d_sb], writes=po.W(), signal=False, out=po.f[:, 0:128], lhsT=qg[:, c * 128:(c + 1) * 128], rhs=sb[:], start=(n == 0), stop=False)
                S.op("dve", "tensor_tensor", reads=[d_us, pws.d], writes=[d_vn, pws.d], out=vn[rows, :], in0=us[rows, :], in1=pws.f[rows, 0:128], op=ALU.subtract)
                yield
                psu = pany.next()
                S.op("pe", "matmul", reads=[d_kd, d_vn], writes=psu.W(), signal=True, out=psu.f[:, 0:128], lhsT=kd[:], rhs=vn[:], start=True, stop=True)
                S.op("pe", "matmul", reads=[d_it, d_vn], writes=po.W(), signal=(n == 1), out=po.f[:, 0:128], lhsT=itt[:], rhs=vn[:], start=False, stop=(n == 1))
                S.op("dve", "scalar_tensor_tensor", reads=[psu.d, d_sf, d_gate], writes=[d_sf, psu.d], out=sf[:], in0=sf[:], scalar=sdec[c][:, blk, dh:dh + 1], in1=psu.f[:, 0:128], op0=ALU.mult, op1=ALU.add)
                S.op("act", "activation", reads=[d_sf], writes=[d_sb], out=sb[:], in_=sf[:], func=AF.Copy)
                yield
            S.op("dve", "tensor_tensor", reads=[po.d, d_oacc], writes=[d_oacc, po.d], out=oacc[:, blk, :], in0=po.f[:, 0:128], in1=oacc[:, blk, :], op=ALU.add)
            chain_done[dr] = it + 1

        def prologue(h, buf):
            ytm, d_ytm = ytms[buf]
            ss, d_ss = sss[buf]
            for i in range(3):
                S.dma("sp", xin[:, 2:slen + 2], FT[24 + 8 * i + h, :, off:off + slen], writes=[d_xin])
                wcol = lambda j: cw[:, 8 * i + h, j:j + 1]
                for ch in range(slen // CH):
                    c0 = ch * CH
                    cacc, d_cacc = caccs.next()
                    S.op("dve", "tensor_scalar", reads=[d_xin, d_c], writes=[d_cacc], out=cacc[:], in0=xin[:, c0:c0 + CH], scalar1=wcol(0), scalar2=None, op0=ALU.mult)
                    yield
                    for j in range(1, 5):
                        S.op("dve", "scalar_tensor_tensor", reads=[d_xin, d_c, d_cacc], writes=[d_cacc], out=cacc[:], in0=xin[:, c0 + j:c0 + j + CH], scalar=wcol(j), in1=cacc[:], op0=ALU.mult, op1=ALU.add)
                        yield
                    S.op("act", "activation", reads=[d_cacc], writes=[d_ybf], out=ybf[:, c0:c0 + CH], in_=cacc[:], func=AF.Silu)
                    yield
                for b4 in range(nb // 4):
                    pt = pany.next()
                    ptv = pt.h[:, 0:512].rearrange("p (a b) -> p a b", b=128)
                    for j in range(4):
                        blk = b4 * 4 + j
                        S.op("pe", "transpose", reads=[d_ybf, d_const], writes=pt.W(), signal=(j == 3),
                             out=ptv[:, j, :], in_=ybf[:, blk * 128:(blk + 1) * 128], identity=identb[:])
                    if b4 % 2 == 0:
                        S.op("dve", "tensor_copy", reads=[pt.d], writes=[d_ytm, pt.d], out=ytm[:, b4 * 4:b4 * 4 + 4, i, :], in_=ptv)
                    else:
                        S.op("act", "activation", reads=[pt.d], writes=[d_ytm, pt.d], out=ytm[:, b4 * 4:b4 * 4 + 4, i, :], in_=ptv, func=AF.Copy)
                    yield
            for blk in range(nb):
                for i in range(2):
                    S.op("act", "activation", reads=[d_ytm], writes=[d_junk, d_ss], out=junkc[:], in_=ytm[:, blk, i, :], func=AF.Square, accum_out=ss[:, blk, i:i + 1])
                yield
            S.op("dve", "tensor_scalar", reads=[d_ss], writes=[d_ss], out=ss[:], in0=ss[:], scalar1=EPS, scalar2=None, op0=ALU.add)
            S.op("act", "activation", reads=[d_ss], writes=[d_ss], out=ss[:], in_=ss[:], func=AF.Sqrt)
            S.op("dve", "reciprocal", reads=[d_ss], writes=[d_ss], out=ss[:], in_=ss[:])
            S.op("dve", "tensor_scalar", reads=[d_ss], writes=[d_ss], out=ss[:, :, 0], in0=ss[:, :, 0], scalar1=qscale, scalar2=None, op0=ALU.mult)
            yield
            for blk in range(nb):
                S.op("dve", "tensor_scalar", reads=[d_ss, d_ytm], writes=[d_ytm], out=ytm[:, blk, 0, :], in0=ytm[:, blk, 0, :], scalar1=ss[:, blk, 0:1], scalar2=None, op0=ALU.mult)
                S.op("act", "activation", reads=[d_ss, d_ytm], writes=[d_ytm], out=ytm[:, blk, 1, :], in_=ytm[:, blk, 1, :], func=AF.Copy, scale=ss[:, blk, 1:2])
                yield
            kkall, d_kkall = kkalls[buf]
            for blk in range(nb):
                pk = pany.next()
                S.op("pe", "transpose", reads=[d_ytm, d_const], writes=pk.W(), signal=False, out=pk.h[:, 0:128], in_=ytm[:, blk, 1, :], identity=identb[:])
                S.op("pe", "transpose", reads=[d_ytm, d_const], writes=pk.W(), signal=True, out=pk.h[:, 128:256], in_=ytm[:, blk, 0, :], identity=identb[:])
                kq, d_kq = kqs.next()
                S.op("act", "activation", reads=[pk.d], writes=[d_kq, pk.d], out=kq[:], in_=pk.h[:, 0:256], func=AF.Copy)
                yield
                pkk = pany.next()
                S.op("pe", "matmul", reads=[d_kq], writes=pkk.W(), signal=True, out=pkk.f[:, 0:256], lhsT=kq[:, 0:128], rhs=kq[:], start=True, stop=True)
                S.op("dve", "tensor_copy", reads=[pkk.d], writes=[d_kkall, pkk.d], out=kkall[:, blk, :], in_=pkk.f[:, 0:256])
                yield

        hl = list(heads)
        for _ in prologue(hl[0], 0):
            pass
        for hi, h in enumerate(hl):
            ytm, d_ytm = ytms[hi % 2]
            kkall, d_kkall = kkalls[hi % 2]
            bg = prologue(hl[hi + 1], (hi + 1) % 2) if hi + 1 < len(hl) else None
            S.op("pool", "memset", reads=[], writes=[d_oacc], ap=oacc[:].rearrange("p a b -> p (a b)"), constant=0.0)
            for dr in range(2):
                S.op("pool", "memset", writes=[Sf[dr][1]], ap=Sf[dr][0][:], constant=0.0)
                S.op("pool", "memset", writes=[Sbf[dr][1]], ap=Sbf[dr][0][:], constant=0.0)
                for c in range(2):
                    S.op("pool", "memset", writes=[vnew[dr][c][1]], ap=vnew[dr][c][0][:], constant=0.0)
            chain_done[0] = 0
            chain_done[1] = 0
            pending = []
            for it in range(nb):
                pending.append((0, it))
                pending.append((1, it))
            pending.reverse()
            active = []
            free = list(range(W))
            while pending or active:
                if pending and free:
                    dr, it = pending.pop()
                    si = free.pop()
                    active.append((si, blockdir(slots[si], h, dr, it, ytm, d_ytm, kkall, d_kkall)))
                for ent in list(active):
                    try:
                        next(ent[1])
                    except StopIteration:
                        active.remove(ent)
                        free.append(ent[0])
                if bg is not None:
                    try:
                        next(bg)
                    except StopIteration:
                        bg = None
            if bg is not None:
                for _ in bg:
                    pass
            S.dma("sp", sgT[:], FT[48 + h, :, off:off + slen], writes=[d_sgT])
            S.op("act", "activation", reads=[d_sgT], writes=[d_sgT], out=sgT[:], in_=sgT[:], func=AF.Silu)
            for blk in range(nb):
                S.op("act", "activation", reads=[d_oacc], writes=[d_junk, d_oss], out=junkc[:], in_=oacc[:, blk, :], func=AF.Square, accum_out=oss[:, 0, blk:blk + 1])
            S.op("dve", "tensor_scalar", reads=[d_oss], writes=[d_oss], out=oss[:, 1, :], in0=oss[:, 0, :], scalar1=1.0 / 128, scalar2=EPS, op0=ALU.mult, op1=ALU.add)
            S.op("act", "activation", reads=[d_oss], writes=[d_oss], out=oss[:, 1, :], in_=oss[:, 1, :], func=AF.Sqrt)
            S.op("dve", "reciprocal", reads=[d_oss], writes=[d_oss], out=oss[:, 1, :], in_=oss[:, 1, :])
            for blk in range(nb):
                on, d_on = ontm.next()
                S.op("act", "activation", reads=[d_oacc, d_oss], writes=[d_on], out=on[:], in_=oacc[:, blk, :], func=AF.Copy, scale=oss[:, 1, blk:blk + 1])
                pt = pany.next()
                S.op("pe", "transpose", reads=[d_on, d_const], writes=pt.W(), signal=True, out=pt.h[:, 0:128], in_=on[:], identity=identb[:])
                S.op("dve", "scalar_tensor_tensor", reads=[pt.d, d_c, d_sgT], writes=[d_sgT, pt.d], out=sgT[:, blk * 128:(blk + 1) * 128], in0=pt.h[:, 0:128], scalar=dng[:, 0:1],
                     in1=sgT[:, blk * 128:(blk + 1) * 128], op0=ALU.mult, op1=ALU.mult)
            S.dma("pool", MT[8 + h, :, off:off + slen], sgT[:], reads=[d_sgT])
        S.phase_end()


def _host_maps(xs_per_core, w_in, w_out, norm_in_gain, conv_w, a_log, dt_bias, dn_gain, final_gain):
    c = _consts()
    gin = np.ascontiguousarray(norm_in_gain.reshape(16, 128).T)
    cw = np.ascontiguousarray(conv_w.reshape(5, 24, 128).transpose(2, 1, 0).reshape(128, 120))
    al = np.ascontiguousarray(np.broadcast_to(a_log.reshape(1, 1, 16), (128, 32, 16)).reshape(128, 512))
    db = np.ascontiguousarray(np.broadcast_to(dt_bias.reshape(1, 1, 16), (128, 32, 16)).reshape(128, 512))
    dg = np.ascontiguousarray(dn_gain.reshape(128, 1))
    fg = np.ascontiguousarray(np.broadcast_to(final_gain.reshape(1, D), (128, D)))
    maps = []
    for xc in xs_per_core:
        m = {"x": xc, "w_in": w_in, "w_out": w_out, "gin": gin, "convw": cw, "alog": al, "dtb": db,
             "dngain": dg, "fgain": fg}
        m.update(c)
        maps.append(m)
    return maps


PHASES = "0ABCD"


def kernel(x_prompt, x_sample, norm_in_gain, w_in, conv_w, a_log, dt_bias, delta_norm_gain, w_out, final_norm_gain):
    f = lambda a: np.ascontiguousarray(np.asarray(a, dtype=np.float32))
    x_prompt, x_sample = f(x_prompt), f(x_sample)
    seqs = [4096, 2048, 2048]
    xs = []
    for c in range(8):
        xs.append(np.ascontiguousarray(np.concatenate(
            [x_prompt[c], x_sample[2 * c], x_sample[2 * c + 1]], axis=0)))
    maps = _host_maps(xs, f(w_in)[0], f(w_out)[0], f(norm_in_gain)[0], f(conv_w)[0], f(a_log)[0],
                      f(dt_bias)[0], f(delta_norm_gain)[0], f(final_norm_gain))
    nc = build(seqs, phases=PHASES)
    res = run_bass_kernel_spmd(nc, maps, core_ids=list(range(8)))
    yp = np.empty((8, 4096, D), np.float32)
    ys = np.empty((16, 2048, D), np.float32)
    for c in range(8):
        yc = res.results[c]["y"]
        yp[c] = yc[0:4096]
        ys[2 * c] = yc[4096:6144]
        ys[2 * c + 1] = yc[6144:8192]
    return (yp, ys)
```

```python
import contextlib
import numpy as np
import concourse.bass as bass
import concourse.mybir as mybir
from concourse.bass_utils import run_bass_kernel_spmd

F32 = mybir.dt.float32
BF16 = mybir.dt.bfloat16
AF = mybir.ActivationFunctionType
ALU = mybir.AluOpType

D = 2048
PW = 8224
NDS = 40
EPS = 1e-6
NEG = -30000.0
PATTERNS = (1, 4, 16)


class Dep:
    __slots__ = ("w", "r", "const")

    def __init__(self, const=False):
        self.w = None
        self.r = {}
        self.const = const


class Sched:
    def __init__(self, nc, stack):
        self.nc = nc
        self.names = ["pe", "act", "dve", "pool", "sp"]
        self.sem = {}
        self.cnt = {}
        self.waited = {}
        for e in self.names:
            self.sem[e] = stack.enter_context(nc.semaphore("s_" + e))
            self.cnt[e] = 0
            self.waited[e] = {}
        self.dsem = [stack.enter_context(nc.semaphore("d%d" % i)) for i in range(NDS)]
        self.dcnt = [0] * NDS
        self.ndma = 0
        self.ndma_sw = 0
        self.allsems = {}
        for e in self.names:
            self.allsems[id(self.sem[e])] = self.sem[e]
        for s in self.dsem:
            self.allsems[id(s)] = s
        self.nins = 0
        self.q = {e: [] for e in self.names}

    def flush(self):
        nc = self.nc
        q = self.q

        def replay(e, items):
            for it in items:
                if it[0] == "w":
                    e.wait_ge(it[1], it[2])
                else:
                    _, name, kw, inc = it
                    ins = getattr(e, name)(**kw)
                    if inc is not None:
                        ins.then_inc(inc[0], inc[1])

        with nc.Block() as block:
            @block.tensor
            def _(e):
                replay(e, q["pe"])

            @block.scalar
            def _(e):
                replay(e, q["act"])

            @block.vector
            def _(e):
                replay(e, q["dve"])

            @block.gpsimd
            def _(e):
                replay(e, q["pool"])

            @block.sync
            def _(e):
                replay(e, q["sp"])
        self.q = {e: [] for e in self.names}

    def _collect(self, eng, reads, writes, extra=()):
        waits = {}

        def need(ev):
            if ev is None:
                return
            sem, val, src = ev
            if src == "pe" and eng == "pe":
                return
            k = id(sem)
            if waits.get(k, 0) < val:
                waits[k] = val

        for d in reads:
            need(d.w)
        for d in writes:
            need(d.w)
            for ev in d.r.values():
                need(ev)
        for ev in extra:
            need(ev)
        out = []
        wd = self.waited[eng]
        for k, val in waits.items():
            if wd.get(k, 0) >= val:
                continue
            wd[k] = val
            out.append((self.allsems[k], val))
        return out

    def _record(self, ev, reads, writes):
        k = id(ev[0])
        for d in reads:
            if d.const:
                continue
            old = d.r.get(k)
            if old is None or old[1] < ev[1]:
                d.r[k] = ev
        for d in writes:
            d.w = ev
            d.r = {}

    def op(self, eng, name, reads=(), writes=(), signal=True, **kw):
        q = self.q[eng]
        for sem, val in self._collect(eng, reads, writes):
            q.append(("w", sem, val))
        self.nins += 1
        if signal:
            self.cnt[eng] += 1
            q.append(("i", name, kw, (self.sem[eng], 1)))
            ev = (self.sem[eng], self.cnt[eng], eng)
        else:
            assert eng == "pe"
            q.append(("i", name, kw, None))
            ev = (self.sem[eng], self.cnt[eng] + 1, eng)
        self._record(ev, reads, writes)
        return ev

    def dma(self, q, out, in_, reads=(), writes=(), **dkw):
        if q == "sp":
            s = self.ndma % (NDS - 8)
            self.ndma += 1
        else:
            s = (NDS - 8) + self.ndma_sw % 8
            self.ndma_sw += 1
        sem = self.dsem[s]
        prev = self.dcnt[s]
        self.dcnt[s] += 16
        val = self.dcnt[s]
        extra = [(sem, prev, "dma")] if prev > 0 else []
        for sm, v in self._collect(q, reads, writes, extra):
            self.q[q].append(("w", sm, v))
        self.q[q].append(("i", "dma_start", dict(out=out, in_=in_, **dkw), (sem, 16)))
        self.nins += 1
        ev = (sem, val, "dma")
        self._record(ev, reads, writes)
        return ev

    def barrier(self):
        for eng in self.names:
            wd = self.waited[eng]
            for src in self.names:
                if src == eng or self.cnt[src] == 0:
                    continue
                k = id(self.sem[src])
                if wd.get(k, 0) < self.cnt[src]:
                    wd[k] = self.cnt[src]
                    self.q[eng].append(("w", self.sem[src], self.cnt[src]))
            for s in range(NDS):
                if self.dcnt[s] == 0:
                    continue
                k = id(self.dsem[s])
                if wd.get(k, 0) < self.dcnt[s]:
                    wd[k] = self.dcnt[s]
                    self.q[eng].append(("w", self.dsem[s], self.dcnt[s]))

    def phase_end(self):
        self.barrier()
        self.flush()


_UID = [0]


def _uniq(name):
    _UID[0] += 1
    return "%s_u%d" % (name, _UID[0])


class Rot:
    def __init__(self, items):
        self.items = items
        self.i = 0

    def next(self):
        it = self.items[self.i % len(self.items)]
        self.i += 1
        return it


def _consts():
    c = {}
    c["ident"] = np.eye(128, dtype=np.float32)
    k = np.arange(128)[:, None]
    q = np.arange(128)[None, :]
    slopes = np.array([2.0 ** (-8.0 * (h + 1) / 8) for h in range(8)], np.float64)
    et = np.zeros((128, 8, 3, 2, 128), np.float32)
    for h in range(8):
        for p, d in enumerate(PATTERNS):
            sc_ = 128.0 ** -0.5
            lo = np.where(k >= q, -slopes[h] * d * np.abs(64 + q - k) / sc_, NEG)
            hi = np.where(k <= q, -slopes[h] * d * np.abs(q - k - 64) / sc_, NEG)
            et[:, h, p, 0, :] = lo
            et[:, h, p, 1, :] = hi
    c["etab"] = et.reshape(128, 8 * 3 * 2 * 128)
    j = np.arange(128)[:, None]
    i = np.arange(128)[None, :]
    same = (j // 64) == (i // 64)
    nm = np.zeros((128, 2, 2, 128), np.float32)
    nm[:, 0, 0, :] = np.where(same & (j < i), 0.0, NEG)
    nm[:, 0, 1, :] = np.where(same & (j <= i), 0.0, NEG)
    nm[:, 1, 0, :] = np.where(same & (j > i), 0.0, NEG)
    nm[:, 1, 1, :] = np.where(same & (j >= i), 0.0, NEG)
    c["negmask"] = nm.reshape(128, 512)
    cm = np.zeros((128, 5, 128), np.float32)
    cm[:, 0, :] = (same & (j <= i))
    cm[:, 1, :] = (same & (j >= i))
    cm[:, 2, :] = same
    cm[:, 3, :] = (j < 64)
    cm[:, 4, :] = (j >= 64)
    c["cmask"] = cm.reshape(128, 640)
    i2 = np.zeros((128, 2, 128), np.float32)
    for t in range(128):
        i2[t, t // 64, t] = 1.0
    c["ident2"] = i2.reshape(128, 256)
    hm = np.zeros((128, 2), np.float32)
    hm[:64, 0] = 1.0
    hm[64:, 1] = 1.0
    c["hmask"] = hm
    return c


def build(seqs, phases="0ABCD", dbg=False, ext_in=False):
    NTOK = sum(seqs)
    assert ext_in or (NTOK % 2048 == 0 and all(s % 2048 == 0 for s in seqs))
    NSB = NTOK // 2048
    nc = bass.Bass("TRN2", target_bir_lowering=False)
    inp = lambda name, shape: nc.dram_tensor(name, shape, F32, kind="ExternalInput").ap()
    x = inp("x", [NTOK, D])
    w_in = inp("w_in", [D, PW])
    w_out = inp("w_out", [D, D])
    gin = inp("gin", [128, 16])
    convw = inp("convw", [128, 24 * 5])
    alog = inp("alog", [128, 32 * 16])
    dtb = inp("dtb", [128, 32 * 16])
    dngain = inp("dngain", [128, 1])
    fgain = inp("fgain", [128, D])
    c_ident = inp("ident", [128, 128])
    c_etab = inp("etab", [128, 8 * 3 * 2 * 128])
    c_negmask = inp("negmask", [128, 512])
    c_cmask = inp("cmask", [128, 640])
    c_ident2 = inp("ident2", [128, 256])
    c_hmask = inp("hmask", [128, 2])
    okind = "ExternalOutput"
    y = nc.dram_tensor("y", [NTOK, D], F32, kind=okind).ap()
    skind = "ExternalOutput" if dbg else "Internal"
    winb = nc.dram_tensor("winb", [128, 16, PW], BF16).ap()
    ikind = "ExternalInput" if ext_in else skind
    FT = nc.dram_tensor("FT", [56, 128, NTOK], BF16, kind=ikind).ap()
    AV = nc.dram_tensor("AV", [NTOK, 1024], BF16, kind=ikind).ap()
    GB = nc.dram_tensor("GB", [NTOK, 32], F32, kind=ikind).ap()
    ROWS = nc.dram_tensor("ROWS", [32, NTOK], F32, kind=skind).ap()
    MT = nc.dram_tensor("MT", [16, 128, NTOK], BF16, kind=skind).ap()

    with contextlib.ExitStack() as gst:
        S = Sched(nc, gst)
        GT = lambda name, shape, dt: gst.enter_context(nc.sbuf_tensor(name, shape, dt))
        identf = GT("identf", [128, 128], F32)
        identb = GT("identb", [128, 128], BF16)
        onesb = GT("onesb", [128, 128], BF16)
        d_const = Dep(const=True)
        S.dma("sp", identf[:], c_ident, writes=[d_const])
        S.op("dve", "tensor_copy", reads=[d_const], writes=[d_const], out=identb[:], in_=identf[:])
        S.op("dve", "memset", writes=[d_const], ap=onesb[:], constant=1.0)
        S.phase_end()

        if "0" in phases:
            _phase0(nc, S, w_in, gin, winb)
        if "A" in phases:
            _phaseA(nc, S, x, winb, FT, AV, GB, NSB, identb, d_const)
        off = 0
        for si, slen in enumerate(seqs):
            if "B" in phases:
                _phaseB(nc, S, FT, AV, MT, off, slen, c_etab, onesb, identb, d_const)
            if "C" in phases:
                _phaseC(nc, S, FT, GB, ROWS, MT, off, slen, convw, alog, dtb, dngain,
                        c_negmask, c_cmask, c_ident2, c_hmask, identf, identb, d_const)
            off += slen
        if "D" in phases:
            _phaseD(nc, S, x, w_out, MT, fgain, y, NTOK, use_mixed=("B" in phases or "C" in phases),
                    heads=(range(16) if ("B" in phases and "C" in phases) else (range(8) if "B" in phases else range(8, 16))))
        S.phase_end()
    return nc


def _phase0(nc, S, w_in, gin, winb):
    with contextlib.ExitStack() as st:
        T = lambda name, shape, dt: st.enter_context(nc.sbuf_tensor(_uniq(name), shape, dt))
        gint = T("gint", [128, 16], F32)
        d_g = Dep()
        S.dma("sp", gint[:], gin, writes=[d_g])
        wf = Rot([(T("p0wf%d" % i, [128, 2056], F32), Dep()) for i in range(3)])
        wb = Rot([(T("p0wb%d" % i, [128, 2056], BF16), Dep()) for i in range(3)])
        n = 0
        for kc in range(16):
            for sl in range(4):
                f, d_f = wf.next()
                b, d_b = wb.next()
                S.dma("sp", f[:], w_in[kc * 128:(kc + 1) * 128, sl * 2056:(sl + 1) * 2056], writes=[d_f])
                if n % 2 == 0:
                    S.op("act", "activation", reads=[d_f, d_g], writes=[d_b], out=b[:], in_=f[:], func=AF.Copy, scale=gint[:, kc:kc + 1])
                else:
                    S.op("dve", "tensor_scalar", reads=[d_f, d_g], writes=[d_b], out=b[:], in0=f[:], scalar1=gint[:, kc:kc + 1], scalar2=None, op0=ALU.mult)
                S.dma("pool", winb[:, kc, sl * 2056:(sl + 1) * 2056], b[:], reads=[d_b])
                n += 1
        S.phase_end()


def _ft_index(cb):
    if cb < 16:
        return cb
    if cb < 24:
        return None
    return cb - 8


def _phaseA(nc, S, x, winb, FT, AV, GB, NSB, identb, d_const):
    with contextlib.ExitStack() as st:
        T = lambda name, shape, dt: st.enter_context(nc.sbuf_tensor(_uniq(name), shape, dt))
        P = lambda name, shape, dt: st.enter_context(nc.psum_tensor(_uniq(name), shape, dt))
        hTs = [(T("hT%d" % i, [128, 16, 2048], BF16), Dep()) for i in range(2)]
        xts = Rot([(T("xt%d" % i, [128, D], F32), Dep()) for i in range(2)])
        hbs = Rot([(T("hb%d" % i, [128, D], BF16), Dep()) for i in range(2)])
        junk = T("junkA", [128, D], BF16)
        d_junk = Dep()
        sms = Rot([(T("smA%d" % i, [128, 2], F32), Dep()) for i in range(2)])
        wsl = Rot([(T("wsl%d" % i, [128, 16, 512], BF16), Dep()) for i in range(2)])
        stg = Rot([(T("stg%d" % i, [128, 2048], BF16), Dep()) for i in range(2)])
        stv = Rot([(T("stv%d" % i, [128, 512], BF16), Dep()) for i in range(3)])
        gbt = T("gbt", [128, 16, 32], F32)
        d_gbt = Dep()
        wtail = T("wtail", [128, 16, 32], BF16)
        d_wtail = Dep()
        ptr = Rot([(P("ptr%d" % i, [128, 4, 128], BF16), Dep()) for i in range(2)])
        pac = Rot([(P("pac%d" % i, [128, 512], F32), Dep()) for i in range(6)])
        S.dma("sp", wtail[:], winb[:, :, 8192:8224], writes=[d_wtail])
        ne = 0

        def a1(sb):
            t0 = sb * 2048
            hT, d_hT = hTs[sb % 2]
            for tt in range(16):
                xt, d_xt = xts.next()
                hb, d_hb = hbs.next()
                sm, d_sm = sms.next()
                S.dma("sp", xt[:], x[t0 + tt * 128:t0 + (tt + 1) * 128, :], writes=[d_xt])
                S.op("act", "activation", reads=[d_xt], writes=[d_junk, d_sm], out=junk[:], in_=xt[:], func=AF.Square, accum_out=sm[:, 0:1])
                S.op("dve", "tensor_scalar", reads=[d_sm], writes=[d_sm], out=sm[:, 1:2], in0=sm[:, 0:1], scalar1=1.0 / D, scalar2=EPS, op0=ALU.mult, op1=ALU.add)
                S.op("act", "activation", reads=[d_sm], writes=[d_sm], out=sm[:, 1:2], in_=sm[:, 1:2], func=AF.Sqrt)
                S.op("dve", "reciprocal", reads=[d_sm], writes=[d_sm], out=sm[:, 1:2], in_=sm[:, 1:2])
                S.op("act", "activation", reads=[d_xt, d_sm], writes=[d_hb], out=hb[:], in_=xt[:], func=AF.Copy, scale=sm[:, 1:2])
                for g in range(4):
                    pt, d_pt = ptr.next()
                    for j in range(4):
                        kc = g * 4 + j
                        S.op("pe", "transpose", reads=[d_hb, d_const], writes=[d_pt], signal=(j == 3),
                             out=pt[:, j, :], in_=hb[:, kc * 128:(kc + 1) * 128], identity=identb[:])
                    S.op("dve", "tensor_copy", reads=[d_pt], writes=[d_hT],
                         out=hT[:, g * 4:(g + 1) * 4, tt * 128:(tt + 1) * 128], in_=pt[:])
                yield

        for _ in a1(0):
            pass
        for sb in range(NSB):
            t0 = sb * 2048
            hT, d_hT = hTs[sb % 2]
            bg = a1(sb + 1) if sb + 1 < NSB else None
            for slab in range(16):
                if bg is not None:
                    try:
                        next(bg)
                    except StopIteration:
                        bg = None
                ws, d_ws = wsl.next()
                S.dma("sp", ws[:], winb[:, :, slab * 512:(slab + 1) * 512], writes=[d_ws])
                if 4 <= slab < 6:
                    for tt in range(16):
                        pa, d_pa = pac.next()
                        for kc in range(16):
                            S.op("pe", "matmul", reads=[d_hT, d_ws], writes=[d_pa], signal=(kc == 15),
                                 out=pa[:], lhsT=hT[:, kc, tt * 128:(tt + 1) * 128], rhs=ws[:, kc, :], start=(kc == 0), stop=(kc == 15))
                        sv, d_sv = stv.next()
                        if ne % 2 == 0:
                            S.op("act", "activation", reads=[d_pa], writes=[d_sv], out=sv[:], in_=pa[:], func=AF.Copy)
                        else:
                            S.op("dve", "tensor_copy", reads=[d_pa], writes=[d_sv], out=sv[:], in_=pa[:])
                        ne += 1
                        S.dma("pool", AV[t0 + tt * 128:t0 + (tt + 1) * 128, (slab - 4) * 512:(slab - 3) * 512], sv[:], reads=[d_sv])
                    continue
                for cbi in range(4):
                    cb = slab * 4 + cbi
                    pas = [pac.next() for _ in range(4)]
                    for kc in range(16):
                        for tb in range(4):
                            pa, d_pa = pas[tb]
                            S.op("pe", "matmul", reads=[d_hT, d_ws], writes=[d_pa], signal=(kc == 15),
                                 out=pa[:], lhsT=ws[:, kc, cbi * 128:(cbi + 1) * 128], rhs=hT[:, kc, tb * 512:(tb + 1) * 512],
                                 start=(kc == 0), stop=(kc == 15))
                    sg, d_sg = stg.next()
                    for tb in range(4):
                        pa, d_pa = pas[tb]
                        if ne % 2 == 0:
                            S.op("act", "activation", reads=[d_pa], writes=[d_sg], out=sg[:, tb * 512:(tb + 1) * 512], in_=pa[:], func=AF.Copy)
                        else:
                            S.op("dve", "tensor_copy", reads=[d_pa], writes=[d_sg], out=sg[:, tb * 512:(tb + 1) * 512], in_=pa[:])
                        ne += 1
                    S.dma("pool", FT[_ft_index(cb), :, t0:t0 + 2048], sg[:], reads=[d_sg])
            if bg is not None:
                for _ in bg:
                    pass
            for tt in range(16):
                pa, d_pa = pac.next()
                for kc in range(16):
                    S.op("pe", "matmul", reads=[d_hT, d_wtail], writes=[d_pa], signal=(kc == 15),
                         out=pa[:, 0:32], lhsT=hT[:, kc, tt * 128:(tt + 1) * 128], rhs=wtail[:, kc, :], start=(kc == 0), stop=(kc == 15))
                S.op("dve", "tensor_copy", reads=[d_pa], writes=[d_gbt], out=gbt[:, tt, :], in_=pa[:, 0:32])
            for t8 in range(2):
                S.dma("pool", GB[t0 + t8 * 1024:t0 + (t8 + 1) * 1024, :].rearrange("(t p) c -> p t c", p=128), gbt[:, t8 * 8:(t8 + 1) * 8, :], reads=[d_gbt])
        S.phase_end()


def _phaseD(nc, S, x, w_out, MT, fgain, y, NTOK, use_mixed, heads):
    heads = list(heads)
    with contextlib.ExitStack() as st:
        T = lambda name, shape, dt: st.enter_context(nc.sbuf_tensor(_uniq(name), shape, dt))
        P = lambda name, shape, dt: st.enter_context(nc.psum_tensor(_uniq(name), shape, dt))
        fg = T("fg", [128, D], F32)
        d_fg = Dep()
        S.dma("sp", fg[:], fgain, writes=[d_fg])
        wo = T("wo", [128, 16, D], BF16)
        d_wo = Dep()
        if use_mixed:
            wf = Rot([(T("dwf%d" % i, [128, D], F32), Dep()) for i in range(2)])
            for kc in range(16):
                f, d_f = wf.next()
                S.dma("sp", f[:], w_out[kc * 128:(kc + 1) * 128, :], writes=[d_f])
                if kc % 2 == 0:
                    S.op("act", "activation", reads=[d_f], writes=[d_wo], out=wo[:, kc, :], in_=f[:], func=AF.Copy)
                else:
                    S.op("dve", "tensor_copy", reads=[d_f], writes=[d_wo], out=wo[:, kc, :], in_=f[:])
        xts = Rot([(T("dxt%d" % i, [128, D], F32), Dep()) for i in range(2)])
        yts = Rot([(T("dyt%d" % i, [128, D], F32), Dep()) for i in range(2)])
        mts = Rot([(T("dmt%d" % i, [128, 16, 512], BF16), Dep()) for i in range(2)])
        junk = T("junkD", [128, D], BF16)
        d_junk = Dep()
        sms = Rot([(T("smD%d" % i, [128, 2], F32), Dep()) for i in range(2)])
        pac = Rot([(P("dpac%d" % i, [128, 512], F32), Dep()) for i in range(6)])
        for t4 in range(NTOK // 512):
            if use_mixed:
                mt, d_mt = mts.next()
                for h in heads:
                    S.dma("sp", mt[:, h, :], MT[h, :, t4 * 512:(t4 + 1) * 512], writes=[d_mt])
            for ti in range(4):
                tt = t4 * 4 + ti
                xt, d_xt = xts.next()
                yt, d_yt = yts.next()
                sm, d_sm = sms.next()
                S.dma("sp", xt[:], x[tt * 128:(tt + 1) * 128, :], writes=[d_xt])
                if use_mixed:
                    for cb in range(4):
                        pa, d_pa = pac.next()
                        for n, h in enumerate(heads):
                            S.op("pe", "matmul", reads=[d_mt, d_wo], writes=[d_pa], signal=(n == len(heads) - 1),
                                 out=pa[:], lhsT=mt[:, h, ti * 128:(ti + 1) * 128], rhs=wo[:, h, cb * 512:(cb + 1) * 512],
                                 start=(n == 0), stop=(n == len(heads) - 1))
                        S.op("dve", "tensor_tensor", reads=[d_pa, d_xt], writes=[d_yt],
                             out=yt[:, cb * 512:(cb + 1) * 512], in0=pa[:], in1=xt[:, cb * 512:(cb + 1) * 512], op=ALU.add)
                    src, d_src = yt, d_yt
                else:
                    src, d_src = xt, d_xt
                S.op("act", "activation", reads=[d_src], writes=[d_junk, d_sm], out=junk[:], in_=src[:], func=AF.Square, accum_out=sm[:, 0:1])
                S.op("dve", "tensor_scalar", reads=[d_sm], writes=[d_sm], out=sm[:, 1:2], in0=sm[:, 0:1], scalar1=1.0 / D, scalar2=EPS, op0=ALU.mult, op1=ALU.add)
                S.op("act", "activation", reads=[d_sm], writes=[d_sm], out=sm[:, 1:2], in_=sm[:, 1:2], func=AF.Sqrt)
                S.op("dve", "reciprocal", reads=[d_sm], writes=[d_sm], out=sm[:, 1:2], in_=sm[:, 1:2])
                S.op("dve", "scalar_tensor_tensor", reads=[d_src, d_sm, d_fg], writes=[d_yt],
                     out=yt[:], in0=src[:], scalar=sm[:, 1:2], in1=fg[:], op0=ALU.mult, op1=ALU.mult)
                S.dma("pool", y[tt * 128:(tt + 1) * 128, :], yt[:], reads=[d_yt])
        S.phase_end()


def _sl(r, d, j0, j1):
    return slice(r + d * j0, r + d * (j1 - 1) + 1, d)


def _phaseB(nc, S, FT, AV, MT, off, slen, c_etab, onesb, identb, d_const, heads=range(8)):
    nb = slen // 128
    scale = 128.0 ** -0.5
    with contextlib.ExitStack() as st:
        T = lambda name, shape, dt: st.enter_context(nc.sbuf_tensor(_uniq(name), shape, dt))
        P = lambda name, shape, dt: st.enter_context(nc.psum_tensor(_uniq(name), shape, dt))
        etf = T("etf", [128, 768], F32)
        d_etf = Dep()
        etb = T("etb", [128, 8, 3, 2, 128], BF16)
        d_etb = Dep()
        for h in range(8):
            S.dma("sp", etf[:], c_etab[:, h * 768:(h + 1) * 768], writes=[d_etf])
            S.op("dve", "tensor_copy", reads=[d_etf], writes=[d_etb], out=etb[:, h].rearrange("p a b c -> p (a b c)"), in_=etf[:])
        qkg = Rot([([T("bq%d" % i, [128, slen], BF16), T("bk%d" % i, [128, slen], BF16), T("bg%d" % i, [128, slen], BF16)], Dep()) for i in range(2)])
        vts = Rot([([T("bv%d_%d" % (i, d), [128, d, nb // d, 128], BF16) for d in PATTERNS], Dep()) for i in range(2)])
        acc = T("bacc", [128, 2, slen], F32)
        d_acc = Dep()
        pes = Rot([(T("bpe%d" % i, [128, 2, 128], BF16), Dep()) for i in range(7)])
        tmps = Rot([(T("btm%d" % i, [128, 2, 128], F32), Dep()) for i in range(3)])
        nst = [0]
        sgt = T("bsg", [128, slen], BF16)
        d_sgt = Dep()
        outt = T("bout", [128, slen], BF16)
        d_out = Dep()
        pst = Rot([(P("bst%d" % i, [128, 2, 128], F32), Dep()) for i in range(3)])
        pol = Rot([(P("bol%d" % i, [128, 2, 256], F32), Dep()) for i in range(3)])
        for h in heads:
            (qt_, kt_, gt_), d_qkg = qkg.next()
            vt, d_v = vts.next()
            S.dma("sp", qt_[:], FT[h, :, off:off + slen], writes=[d_qkg])
            S.dma("sp", kt_[:], FT[8 + h, :, off:off + slen], writes=[d_qkg])
            S.dma("sp", gt_[:], FT[16 + h, :, off:off + slen], writes=[d_qkg])
            for pi, d in enumerate(PATTERNS):
                njh = nb // d
                for r in range(d):
                    src = AV[off:off + slen, h * 128:(h + 1) * 128].rearrange("(jh p r) c -> p r jh c", p=128, r=d)[:, r]
                    for j0 in range(0, njh, 8):
                        j1 = min(njh, j0 + 8)
                        S.dma("sp", vt[pi][:, r, j0:j1], src[:, j0:j1], writes=[d_v])
            tiles = []
            for pi, d in enumerate(PATTERNS):
                L = slen // d
                nkb = L // 128
                for r in range(d):
                    for qt in range(nkb + 1):
                        tiles.append((pi, d, L, nkb, r, qt))
            LAG = 4
            inflight = []
            groups = []
            for ti, (pi, d, L, nkb, r, qt) in enumerate(tiles):
                q0 = max(0, 128 * qt - 64)
                q1 = min(L, 128 * qt + 64)
                if groups and groups[-1]["key"] == (pi, r) and groups[-1]["q1"] == q0 and (q1 - groups[-1]["q0"]) <= 256:
                    groups[-1]["q1"] = q1
                    groups[-1]["last"] = ti
                else:
                    groups.append(dict(key=(pi, r), q0=q0, q1=q1, last=ti, po=None))
                tiles[ti] = (pi, d, L, nkb, r, qt, groups[-1])

            def stage2(ent):
                (ti, pi, d, r, q0, q1, nq, blocks, pe, d_pe, grp) = ent
                if grp["po"] is None:
                    grp["po"] = pol.next()
                po, d_po = grp["po"]
                go = q0 - grp["q0"]
                for bi, (b, kb) in enumerate(blocks):
                    S.op("pe", "matmul", reads=[d_pe, d_v], writes=[d_po], signal=False,
                         out=po[:, 0, go:go + nq], lhsT=vt[pi][:, r, kb, :], rhs=pe[:, b, 0:nq], start=(bi == 0), stop=(bi == len(blocks) - 1))
                for bi, (b, kb) in enumerate(blocks):
                    S.op("pe", "matmul", reads=[d_pe, d_const], writes=[d_po], signal=(bi == len(blocks) - 1),
                         out=po[:, 1, go:go + nq], lhsT=onesb[:], rhs=pe[:, b, 0:nq], start=(bi == 0), stop=(bi == len(blocks) - 1))
                if ti != grp["last"]:
                    return
                gn = grp["q1"] - grp["q0"]
                aap = acc[:, :, _sl(r, d, grp["q0"], grp["q1"])]
                if pi == 0:
                    S.op("dve", "tensor_copy", reads=[d_po], writes=[d_acc], out=aap, in_=po[:, :, 0:gn])
                else:
                    S.op("dve", "tensor_tensor", reads=[d_po, d_acc], writes=[d_acc], out=aap, in0=po[:, :, 0:gn], in1=aap, op=ALU.add)

            for ti, (pi, d, L, nkb, r, qt, grp) in enumerate(tiles):
                q0 = max(0, 128 * qt - 64)
                q1 = min(L, 128 * qt + 64)
                nq = q1 - q0
                qoff = 64 if qt == 0 else 0
                blocks = []
                if qt >= 1:
                    blocks.append((0, qt - 1))
                if qt < nkb:
                    blocks.append((1, qt))
                ps, d_ps = pst.next()
                pe, d_pe = pes.next()
                qap = qt_[:, _sl(r, d, q0, q1)]
                for bi, (b, kb) in enumerate(blocks):
                    S.op("pe", "matmul", reads=[d_qkg], writes=[d_ps], signal=False,
                         out=ps[:, b, 0:nq], lhsT=kt_[:, _sl(r, d, 128 * kb, 128 * (kb + 1))], rhs=qap, start=True, stop=False)
                    S.op("pe", "matmul", reads=[d_etb, d_const], writes=[d_ps], signal=(bi == len(blocks) - 1),
                         out=ps[:, b, 0:nq], lhsT=identb[:], rhs=etb[:, h, pi, b, qoff:qoff + nq], start=False, stop=True)
                b0 = blocks[0][0]
                b1 = blocks[-1][0] + 1
                S.op("act", "activation", reads=[d_ps], writes=[d_pe], out=pe[:, b0:b1, 0:nq], in_=ps[:, b0:b1, 0:nq], func=AF.Exp, scale=scale)
                inflight.append((ti, pi, d, r, q0, q1, nq, blocks, pe, d_pe, grp))
                if len(inflight) > LAG:
                    stage2(inflight.pop(0))
            while inflight:
                stage2(inflight.pop(0))
            S.op("act", "activation", reads=[d_acc], writes=[d_acc], out=acc[:, 1, :], in_=acc[:, 1, :], func=AF.Ln)
            S.op("act", "activation", reads=[d_acc], writes=[d_acc], out=acc[:, 1, :], in_=acc[:, 1, :], func=AF.Exp, scale=-1.0)
            S.op("act", "activation", reads=[d_qkg], writes=[d_sgt], out=sgt[:], in_=gt_[:], func=AF.Silu)
            S.op("dve", "tensor_tensor", reads=[d_acc], writes=[d_acc], out=acc[:, 0, :], in0=acc[:, 0, :], in1=acc[:, 1, :], op=ALU.mult)
            S.op("dve", "tensor_tensor", reads=[d_acc, d_sgt], writes=[d_out], out=outt[:], in0=acc[:, 0, :], in1=sgt[:], op=ALU.mult)
            S.dma("pool", MT[h, :, off:off + slen], outt[:], reads=[d_out])
        S.phase_end()


def _phaseC(nc, S, FT, GB, ROWS, MT, off, slen, convw, alog, dtb, dngain,
            c_negmask, c_cmask, c_ident2, c_hmask, identf, identb, d_const, heads=range(8)):
    nb = slen // 128
    NTOK = ROWS.shape[1]
    qscale = 128.0 ** -0.5
    with contextlib.ExitStack() as st:
        T = lambda name, shape, dt: st.enter_context(nc.sbuf_tensor(_uniq(name), shape, dt))
        P = lambda name, shape, dt: st.enter_context(nc.psum_tensor(_uniq(name), shape, dt))
        d_c = Dep()
        negm = T("negm", [128, 2, 256], F32)
        cmask = T("cmask", [128, 5, 128], F32)
        id2f = T("id2f", [128, 256], F32)
        ident2b = T("ident2b", [128, 256], BF16)
        hmask = T("hmask", [128, 2], F32)
        cw = T("cw", [128, 24, 5], F32)
        dng = T("dng", [128, 1], F32)
        S.dma("sp", negm[:].rearrange("p a b -> p (a b)"), c_negmask, writes=[d_c])
        S.dma("sp", cmask[:].rearrange("p a b -> p (a b)"), c_cmask, writes=[d_c])
        S.dma("sp", id2f[:], c_ident2, writes=[d_c])
        S.dma("sp", hmask[:], c_hmask, writes=[d_c])
        S.dma("sp", cw[:].rearrange("p a b -> p (a b)"), convw, writes=[d_c])
        S.dma("sp", dng[:], dngain, writes=[d_c])
        S.op("dve", "tensor_copy", reads=[d_c], writes=[d_c], out=ident2b[:], in_=id2f[:])
        gc = T("gc", [128, nb, 16], F32)
        egc = T("egc", [128, nb, 16], F32)
        edec = T("edec", [128, nb, 16], F32)
        beta = T("beta", [128, nb, 16], F32)
        sdec = [T("sdec0", [128, nb, 16], F32), T("sdec1", [128, nb, 16], F32)]
        begc = T("begc", [128, nb, 16], F32)
        d_gate = Dep()
        bank = [P("cbank%d" % i, [128, 512], F32) for i in range(8)]
        d_bank = [Dep() for _ in range(8)]
        bankb = [bank[i][:].bitcast(BF16) for i in range(8)]
        with contextlib.ExitStack() as st2:
            T2 = lambda name, shape, dt: st2.enter_context(nc.sbuf_tensor(_uniq(name), shape, dt))
            gbt = T2("gbt", [128, nb, 32], F32)
            al = T2("al", [128, nb, 16], F32)
            db = T2("db", [128, nb, 16], F32)
            t1 = T2("t1", [128, nb, 16], F32)
            t2 = T2("t2", [128, nb, 16], F32)
            z = T2("z", [128, nb, 16], F32)
            lnb = T2("lnb", [128, nb, 16], F32)
            g = T2("g", [128, nb, 16], F32)
            r1 = T2("r1", [128, nb, 16], F32)
            rowst = T2("rowst", [32, 32, 128], F32)
            d_g = Dep()
            S.dma("sp", gbt[:], GB[off:off + slen, :].rearrange("(t p) c -> p t c", p=128), writes=[d_g])
            S.dma("sp", al[:].rearrange("p a b -> p (a b)"), alog[:, 0:nb * 16], writes=[d_g])
            S.dma("sp", db[:].rearrange("p a b -> p (a b)"), dtb[:, 0:nb * 16], writes=[d_g])
            braw = gbt[:, :, 0:16]
            araw = gbt[:, :, 16:32]
            G = dict(reads=[d_g], writes=[d_g])
            S.op("dve", "scalar_tensor_tensor", out=t1[:], in0=braw, scalar=-1.0, in1=braw, op0=ALU.mult, op1=ALU.max, **G)
            S.op("act", "activation", out=t1[:], in_=t1[:], func=AF.Exp, scale=-1.0, **G)
            S.op("dve", "tensor_scalar", out=t1[:], in0=t1[:], scalar1=1.0, scalar2=None, op0=ALU.add, **G)
            S.op("act", "activation", out=t1[:], in_=t1[:], func=AF.Ln, **G)
            S.op("dve", "scalar_tensor_tensor", out=lnb[:], in0=braw, scalar=0.0, in1=t1[:], op0=ALU.min, op1=ALU.subtract, **G)
            S.op("act", "activation", reads=[d_g], writes=[d_gate], out=beta[:], in_=lnb[:], func=AF.Exp)
            S.op("dve", "tensor_tensor", out=z[:], in0=araw, in1=db[:], op=ALU.add, **G)
            S.op("dve", "scalar_tensor_tensor", out=t2[:], in0=z[:], scalar=-1.0, in1=z[:], op0=ALU.mult, op1=ALU.max, **G)
            S.op("act", "activation", out=t2[:], in_=t2[:], func=AF.Exp, scale=-1.0, **G)
            S.op("dve", "tensor_scalar", out=t2[:], in0=t2[:], scalar1=1.0, scalar2=None, op0=ALU.add, **G)
            S.op("act", "activation", out=t2[:], in_=t2[:], func=AF.Ln, **G)
            S.op("dve", "scalar_tensor_tensor", out=t2[:], in0=z[:], scalar=0.0, in1=t2[:], op0=ALU.max, op1=ALU.add, **G)
            S.op("act", "activation", out=al[:], in_=al[:], func=AF.Exp, **G)
            S.op("dve", "scalar_tensor_tensor", out=g[:], in0=t2[:], scalar=-1.0, in1=al[:], op0=ALU.mult, op1=ALU.mult, **G)
            d_pb = [Dep() for _ in range(5)]
            pgi = [0, 1, 2, 3, 5]
            pg2 = [bank[i][:, 0:nb * 16] for i in pgi]
            pg = [bank[i][:, 0:nb * 16].rearrange("p (a b) -> p a b", b=16) for i in pgi]
            g2 = g[:].rearrange("p a b -> p (a b)")
            S.op("pe", "matmul", reads=[d_g, d_c], writes=[d_pb[0], d_bank[0]], signal=True, out=pg2[0], lhsT=cmask[:, 0, :], rhs=g2, start=True, stop=True)
            S.op("pe", "matmul", reads=[d_g, d_c], writes=[d_pb[4], d_bank[5]], signal=True, out=pg2[4], lhsT=cmask[:, 1, :], rhs=g2, start=True, stop=True)
            for i in range(1, 4):
                S.op("pe", "matmul", reads=[d_g, d_c], writes=[d_pb[i], d_bank[i]], signal=True,
                     out=pg2[i], lhsT=cmask[:, 1 + i, :], rhs=g2, start=True, stop=True)
            S.op("dve", "tensor_copy", reads=[d_pb[0]], writes=[d_gate, d_bank[0]], out=gc[:, :, 0:8], in_=pg[0][:, :, 0:8])
            S.op("dve", "tensor_copy", reads=[d_pb[4]], writes=[d_gate, d_bank[5]], out=gc[:, :, 8:16], in_=pg[4][:, :, 8:16])
            S.op("act", "activation", reads=[d_gate], writes=[d_gate], out=egc[:], in_=gc[:], func=AF.Exp)
            S.op("dve", "tensor_tensor", reads=[d_gate], writes=[d_gate], out=begc[:], in0=egc[:], in1=beta[:], op=ALU.mult)
            S.op("dve", "tensor_tensor", reads=[d_pb[1], d_gate], writes=[d_gate, d_bank[1]], out=edec[:], in0=pg[1], in1=gc[:], op=ALU.subtract)
            S.op("act", "activation", reads=[d_gate], writes=[d_gate], out=edec[:], in_=edec[:], func=AF.Exp)
            S.op("act", "activation", reads=[d_pb[2]], writes=[d_gate, d_bank[2]], out=sdec[0][:], in_=pg[2], func=AF.Exp)
            S.op("act", "activation", reads=[d_pb[3]], writes=[d_gate, d_bank[3]], out=sdec[1][:], in_=pg[3], func=AF.Exp)
            S.op("dve", "tensor_tensor", reads=[d_gate, d_g], writes=[d_g], out=r1[:], in0=gc[:], in1=lnb[:], op=ALU.add)
            d_rowst = Dep()
            prow = [(bank[4][0:nb, 0:128], Dep()), (bank[4][0:nb, 128:256], Dep()), (bank[4][0:nb, 256:384], Dep()), (bank[4][0:nb, 384:512], Dep())]
            for q in range(32):
                dh, which = q // 2, q % 2
                src = (r1 if which == 0 else gc)[:, :, dh]
                pr, d_pr = prow[q % 4]
                S.op("pe", "transpose", reads=[d_g, d_gate, d_const], writes=[d_pr, d_bank[4]], out=pr, in_=src, identity=identf[:])
                if q % 2 == 0:
                    S.op("act", "activation", reads=[d_pr], writes=[d_rowst, d_bank[4]], out=rowst[0:nb, q, :], in_=pr, func=AF.Copy)
                else:
                    S.op("dve", "tensor_copy", reads=[d_pr], writes=[d_rowst, d_bank[4]], out=rowst[0:nb, q, :], in_=pr)
            d_rows = Dep()
            S.dma("sp", ROWS[:, off:off + slen].rearrange("q (b i) -> b q i", i=128), rowst[0:nb], reads=[d_rowst], writes=[d_rows])
            S.phase_end()

        W = 6
        xin = T("cx", [128, slen + 4], BF16)
        d_xin = Dep()
        S.op("pool", "memset", writes=[d_xin], ap=xin[:, 0:2], constant=0.0)
        S.op("pool", "memset", writes=[d_xin], ap=xin[:, slen + 2:slen + 4], constant=0.0)
        CH = min(1024, slen)
        caccs = Rot([(T("cacc%d" % i, [128, CH], F32), Dep()) for i in range(2)])
        ybf = T("cy", [128, slen], BF16)
        d_ybf = Dep()
        ytms = [(T("ytm%d" % i, [128, nb, 3, 128], BF16), Dep()) for i in range(2)]
        sss = [(T("css%d" % i, [128, nb, 2], F32), Dep()) for i in range(2)]
        junkc = T("cjunk", [128, 128], BF16)
        d_junk = Dep()
        oacc = T("oacc", [128, nb, 128], F32)
        d_oacc = Dep()
        oss = T("coss", [128, 2, nb], F32)
        d_oss = Dep()
        sgT = T("csg", [128, slen], BF16)
        d_sgT = Dep()
        ontm = Rot([(T("con%d" % i, [128, 128], BF16), Dep()) for i in range(2)])
        vnew = [[(T("cvn%d_%d" % (i, c), [128, 128], BF16), Dep()) for c in range(2)] for i in range(2)]
        Sf = [(T("cSf%d" % i, [128, 128], F32), Dep()) for i in range(2)]
        Sbf = [(T("cSb%d" % i, [128, 128], BF16), Dep()) for i in range(2)]

        class Slot:
            pass

        slots = []
        for si in range(W):
            sl = Slot()
            mk = lambda name, shape, dt: (T("%s_s%d" % (name, si), shape, dt), Dep())
            sl.arg = mk("carg", [128, 256], F32)
            sl.at = mk("cat", [128, 128], F32)
            sl.it = mk("cit", [128, 128], BF16)
            sl.asb = mk("casb", [128, 128], F32)
            sl.p0 = mk("cp0", [128, 128], F32)
            sl.bp = [mk("cbp%d" % i, [128, 256], F32) for i in range(2)]
            sl.bt = [mk("cbt%d" % i, [128, 128], F32) for i in range(2)]
            sl.p4 = mk("cp4", [128, 128], F32)
            sl.pbf = mk("cpbf", [128, 128], BF16)
            sl.kbg = mk("ckbg", [128, 128], BF16)
            sl.vb = mk("cvb", [128, 128], BF16)
            sl.kd = mk("ckd", [128, 128], BF16)
            sl.yqg = mk("cyqg", [128, 128], BF16)
            sl.us = mk("cus", [128, 128], F32)
            sl.wt = mk("cwt", [128, 128], BF16)
            sl.qg = mk("cqg", [128, 256], BF16)
            slots.append(sl)

        class Reg:
            def __init__(self, b):
                self.f = bank[b]
                self.h = bankb[b]
                self.d = d_bank[b]

            def W(self):
                return [self.d]

        p_o = [Reg(2), Reg(3)]
        pany = Rot([Reg(b) for b in (0, 1, 4, 5, 6, 7)])
        kkalls = [(T("kkall%d" % i, [128, nb, 256], BF16), Dep()) for i in range(2)]
        kqs = Rot([(T("ckq%d" % i, [128, 256], BF16), Dep()) for i in range(2)])
        chain_done = [0, 0]

        def blockdir(sl, h, dr, it, ytm, d_ytm, kkall, d_kkall):
            blk = it if dr == 0 else nb - 1 - it
            dh = dr * 8 + h
            arg, d_arg = sl.arg
            S.dma("sp", arg[:].rearrange("p (a b) -> p a b", b=128), bass.AP(ROWS.tensor, (2 * dh) * NTOK + off + blk * 128, [[0, 128], [NTOK, 2], [1, 128]]),
                  reads=[d_rows], writes=[d_arg])
            kbg, d_kbg = sl.kbg
            vb, d_vb = sl.vb
            kd, d_kd = sl.kd
            yqg, d_yqg = sl.yqg
            S.op("act", "activation", reads=[d_ytm, d_gate], writes=[d_yqg], out=yqg[:], in_=ytm[:, blk, 0, :], func=AF.Copy, scale=egc[:, blk, dh:dh + 1])
            S.op("dve", "tensor_scalar", reads=[d_ytm, d_gate], writes=[d_kbg], out=kbg[:], in0=ytm[:, blk, 1, :], scalar1=begc[:, blk, dh:dh + 1], scalar2=None, op0=ALU.mult)
            yield
            gg = arg
            d_gg = d_arg
            pq = pany.next()
            S.op("pe", "matmul", reads=[d_yqg, d_c], writes=pq.W(), signal=True, out=pq.f[:, 0:256], lhsT=yqg[:], rhs=ident2b[:], start=True, stop=True)
            qg, d_qg = sl.qg
            S.op("act", "activation", reads=[pq.d], writes=[d_qg, pq.d], out=qg[:], in_=pq.f[:, 0:256], func=AF.Copy)
            S.op("act", "activation", reads=[d_ytm, d_gate], writes=[d_vb], out=vb[:], in_=ytm[:, blk, 2, :], func=AF.Copy, scale=beta[:, blk, dh:dh + 1])
            S.op("dve", "tensor_scalar", reads=[d_ytm, d_gate], writes=[d_kd], out=kd[:], in0=ytm[:, blk, 1, :], scalar1=edec[:, blk, dh:dh + 1], scalar2=None, op0=ALU.mult)
            yield
            S.op("dve", "scalar_tensor_tensor", reads=[d_arg, d_gate, d_c], writes=[d_arg], out=arg[:], in0=arg[:],
                 scalar=gc[:, blk, dh:dh + 1], in1=negm[:, dr, :], op0=ALU.subtract, op1=ALU.add)
            S.op("act", "activation", reads=[d_arg], writes=[d_arg], out=arg[:], in_=arg[:], func=AF.Exp)
            yield
            at, d_at = sl.at
            itt, d_it = sl.it
            S.op("dve", "tensor_tensor", reads=[d_kkall, d_gg], writes=[d_at], out=at[:], in0=kkall[:, blk, 0:128], in1=gg[:, 0:128], op=ALU.mult)
            S.op("pool", "tensor_tensor", reads=[d_kkall, d_gg], writes=[d_it], out=itt[:], in0=kkall[:, blk, 128:256], in1=gg[:, 128:256], op=ALU.mult)
            pA = pany.next()
            S.op("pe", "transpose", reads=[d_at, d_const], writes=pA.W(), signal=True, out=pA.f[:, 0:128], in_=at[:], identity=identf[:])
            asb, d_asb = sl.asb
            S.op("act", "activation", reads=[pA.d], writes=[d_asb, pA.d], out=asb[:], in_=pA.f[:, 0:128], func=AF.Copy)
            p0, d_p0 = sl.p0
            S.op("pool", "tensor_tensor", reads=[d_at, d_const], writes=[d_p0], out=p0[:], in0=identf[:], in1=at[:], op=ALU.subtract)
            yield
            bp1, d_bp1 = sl.bp[1]
            px = pany.next()
            S.op("pe", "matmul", reads=[d_asb, d_at], writes=px.W(), signal=True, out=px.f[:, 0:128], lhsT=asb[:], rhs=at[:], start=True, stop=True)
            py = pany.next()
            S.op("pe", "matmul", reads=[d_asb, d_at], writes=py.W(), signal=True, out=py.f[:, 0:128], lhsT=at[:], rhs=asb[:], start=True, stop=True)
            S.op("dve", "tensor_copy", reads=[px.d], writes=[d_bp1, px.d], out=bp1[:, 0:128], in_=px.f[:, 0:128])
            S.op("pool", "tensor_copy", reads=[d_p0], writes=[d_bp1], out=bp1[:, 128:256], in_=p0[:])
            bt, d_bt = sl.bt[1]
            S.op("act", "activation", reads=[py.d], writes=[d_bt, py.d], out=bt[:], in_=py.f[:, 0:128], func=AF.Copy)
            yield
            cur, d_cur = bp1, d_bp1
            for k in range(1, 4):
                nxt, d_nxt = sl.bp[(k + 1) % 2]
                px = pany.next()
                S.op("pe", "matmul", reads=[d_bt, d_cur], writes=px.W(), signal=True, out=px.f[:, 0:256], lhsT=bt[:], rhs=cur[:], start=True, stop=True)
                if k % 2 == 1:
                    S.op("dve", "tensor_copy", reads=[px.d], writes=[d_nxt, px.d], out=nxt[:], in_=px.f[:, 0:256])
                else:
                    S.op("act", "activation", reads=[px.d], writes=[d_nxt, px.d], out=nxt[:], in_=px.f[:, 0:256], func=AF.Copy)
                S.op("pool", "tensor_tensor", reads=[d_cur, d_nxt], writes=[d_nxt], out=nxt[:, 128:256], in0=nxt[:, 128:256], in1=cur[:, 128:256], op=ALU.add)
                yield
                py = pany.next()
                S.op("pe", "matmul", reads=[d_bt, d_cur], writes=py.W(), signal=True, out=py.f[:, 0:128], lhsT=cur[:, 0:128], rhs=bt[:], start=True, stop=True)
                nbt, d_nbt = sl.bt[(k + 1) % 2]
                if k % 2 == 1:
                    S.op("act", "activation", reads=[py.d], writes=[d_nbt, py.d], out=nbt[:], in_=py.f[:, 0:128], func=AF.Copy)
                else:
                    S.op("dve", "tensor_copy", reads=[py.d], writes=[d_nbt, py.d], out=nbt[:], in_=py.f[:, 0:128])
                cur, d_cur = nxt, d_nxt
                bt, d_bt = nbt, d_nbt
                yield
            px = pany.next()
            S.op("pe", "matmul", reads=[d_bt, d_cur], writes=px.W(), signal=True, out=px.f[:, 0:128], lhsT=bt[:], rhs=cur[:, 128:256], start=True, stop=True)
            py = pany.next()
            S.op("pe", "matmul", reads=[d_bt, d_cur], writes=py.W(), signal=True, out=py.f[:, 0:128], lhsT=cur[:, 0:128], rhs=bt[:], start=True, stop=True)
            p4, d_p4 = sl.p4
            S.op("dve", "tensor_tensor", reads=[px.d, d_cur], writes=[d_p4, px.d], out=p4[:], in0=px.f[:, 0:128], in1=cur[:, 128:256], op=ALU.add)
            bt5, d_bt5 = sl.bt[1]
            S.op("act", "activation", reads=[py.d], writes=[d_bt5, py.d], out=bt5[:], in_=py.f[:, 0:128], func=AF.Copy)
            yield
            px = pany.next()
            S.op("pe", "matmul", reads=[d_bt5, d_p4], writes=px.W(), signal=True, out=px.f[:, 0:128], lhsT=bt5[:], rhs=p4[:], start=True, stop=True)
            pbf, d_pbf = sl.pbf
            S.op("dve", "tensor_tensor", reads=[px.d, d_p4], writes=[d_pbf, px.d], out=pbf[:], in0=px.f[:, 0:128], in1=p4[:], op=ALU.add)
            yield
            pu = pany.next()
            S.op("pe", "matmul", reads=[d_pbf, d_vb], writes=pu.W(), signal=True, out=pu.f[:, 0:128], lhsT=pbf[:], rhs=vb[:], start=True, stop=True)
            us, d_us = sl.us
            S.op("act", "activation", reads=[pu.d], writes=[d_us, pu.d], out=us[:], in_=pu.f[:, 0:128], func=AF.Copy)
            pw = pany.next()
            S.op("pe", "matmul", reads=[d_pbf, d_kbg], writes=pw.W(), signal=True, out=pw.f[:, 0:128], lhsT=kbg[:], rhs=pbf[:], start=True, stop=True)
            wt, d_wt = sl.wt
            S.op("dve", "tensor_copy", reads=[pw.d], writes=[d_wt, pw.d], out=wt[:], in_=pw.f[:, 0:128])
            yield
            while chain_done[dr] < it:
                yield
            sf, d_sf = Sf[dr]
            sb, d_sb = Sbf[dr]
            po = p_o[dr]
            order = (0, 1) if dr == 0 else (1, 0)
            for n, c in enumerate(order):
                vn, d_vn = vnew[dr][c]
                pws = pany.next()
                rows = slice(64 * c, 64 * c + 64)
                S.op("pe", "matmul", reads=[d_wt, d_sb], writes=pws.W(), signal=True, out=pws.f[:, 0:128], lhsT=wt[:], rhs=sb[:], start=True, stop=True)
                S.op("pe", "matmul", reads=[d_qg, d_sb], writes=po.W(), signal=False, out=po.f[:, 0:128], lhsT=qg[:, c * 128:(c + 1) * 128], rhs=sb[:], start=(n == 0), stop=False)
                S.op("dve", "tensor_tensor", reads=[d_us, pws.d], writes=[d_vn, pws.d], out=vn[rows, :], in0=us[rows, :], in1=pws.f[rows, 0:128], op=ALU.subtract)
                yield
                psu = pany.next()
                S.op("pe", "matmul", reads=[d_kd, d_vn], writes=psu.W(), signal=True, out=psu.f[:, 0:128], lhsT=kd[:], rhs=vn[:], start=True, stop=True)
                S.op("pe", "matmul", reads=[d_it, d_vn], writes=po.W(), signal=(n == 1), out=po.f[:, 0:128], lhsT=itt[:], rhs=vn[:], start=False, stop=(n == 1))
                S.op("dve", "scalar_tensor_tensor", reads=[psu.d, d_sf, d_gate], writes=[d_sf, psu.d], out=sf[:], in0=sf[:], scalar=sdec[c][:, blk, dh:dh + 1], in1=psu.f[:, 0:128], op0=ALU.mult, op1=ALU.add)
                S.op("act", "activation", reads=[d_sf], writes=[d_sb], out=sb[:], in_=sf[:], func=AF.Copy)
                yield
            S.op("dve", "tensor_tensor", reads=[po.d, d_oacc], writes=[d_oacc, po.d], out=oacc[:, blk, :], in0=po.f[:, 0:128], in1=oacc[:, blk, :], op=ALU.add)
            chain_done[dr] = it + 1

        def prologue(h, buf):
            ytm, d_ytm = ytms[buf]
            ss, d_ss = sss[buf]
            for i in range(3):
                S.dma("sp", xin[:, 2:slen + 2], FT[24 + 8 * i + h, :, off:off + slen], writes=[d_xin])
                wcol = lambda j: cw[:, 8 * i + h, j:j + 1]
                for ch in range(slen // CH):
                    c0 = ch * CH
                    cacc, d_cacc = caccs.next()
                    S.op("dve", "tensor_scalar", reads=[d_xin, d_c], writes=[d_cacc], out=cacc[:], in0=xin[:, c0:c0 + CH], scalar1=wcol(0), scalar2=None, op0=ALU.mult)
                    yield
                    for j in range(1, 5):
                        S.op("dve", "scalar_tensor_tensor", reads=[d_xin, d_c, d_cacc], writes=[d_cacc], out=cacc[:], in0=xin[:, c0 + j:c0 + j + CH], scalar=wcol(j), in1=cacc[:], op0=ALU.mult, op1=ALU.add)
                        yield
                    S.op("act", "activation", reads=[d_cacc], writes=[d_ybf], out=ybf[:, c0:c0 + CH], in_=cacc[:], func=AF.Silu)
                    yield
                for b4 in range(nb // 4):
                    pt = pany.next()
                    ptv = pt.h[:, 0:512].rearrange("p (a b) -> p a b", b=128)
                    for j in range(4):
                        blk = b4 * 4 + j
                        S.op("pe", "transpose", reads=[d_ybf, d_const], writes=pt.W(), signal=(j == 3),
                             out=ptv[:, j, :], in_=ybf[:, blk * 128:(blk + 1) * 128], identity=identb[:])
                    if b4 % 2 == 0:
                        S.op("dve", "tensor_copy", reads=[pt.d], writes=[d_ytm, pt.d], out=ytm[:, b4 * 4:b4 * 4 + 4, i, :], in_=ptv)
                    else:
                        S.op("act", "activation", reads=[pt.d], writes=[d_ytm, pt.d], out=ytm[:, b4 * 4:b4 * 4 + 4, i, :], in_=ptv, func=AF.Copy)
                    yield
            for blk in range(nb):
                for i in range(2):
                    S.op("act", "activation", reads=[d_ytm], writes=[d_junk, d_ss], out=junkc[:], in_=ytm[:, blk, i, :], func=AF.Square, accum_out=ss[:, blk, i:i + 1])
                yield
            S.op("dve", "tensor_scalar", reads=[d_ss], writes=[d_ss], out=ss[:], in0=ss[:], scalar1=EPS, scalar2=None, op0=ALU.add)
            S.op("act", "activation", reads=[d_ss], writes=[d_ss], out=ss[:], in_=ss[:], func=AF.Sqrt)
            S.op("dve", "reciprocal", reads=[d_ss], writes=[d_ss], out=ss[:], in_=ss[:])
            S.op("dve", "tensor_scalar", reads=[d_ss], writes=[d_ss], out=ss[:, :, 0], in0=ss[:, :, 0], scalar1=qscale, scalar2=None, op0=ALU.mult)
            yield
            for blk in range(nb):
                S.op("dve", "tensor_scalar", reads=[d_ss, d_ytm], writes=[d_ytm], out=ytm[:, blk, 0, :], in0=ytm[:, blk, 0, :], scalar1=ss[:, blk, 0:1], scalar2=None, op0=ALU.mult)
                S.op("act", "activation", reads=[d_ss, d_ytm], writes=[d_ytm], out=ytm[:, blk, 1, :], in_=ytm[:, blk, 1, :], func=AF.Copy, scale=ss[:, blk, 1:2])
                yield
            kkall, d_kkall = kkalls[buf]
            for blk in range(nb):
                pk = pany.next()
                S.op("pe", "transpose", reads=[d_ytm, d_const], writes=pk.W(), signal=False, out=pk.h[:, 0:128], in_=ytm[:, blk, 1, :], identity=identb[:])
                S.op("pe", "transpose", reads=[d_ytm, d_const], writes=pk.W(), signal=True, out=pk.h[:, 128:256], in_=ytm[:, blk, 0, :], identity=identb[:])
                kq, d_kq = kqs.next()
                S.op("act", "activation", reads=[pk.d], writes=[d_kq, pk.d], out=kq[:], in_=pk.h[:, 0:256], func=AF.Copy)
                yield
                pkk = pany.next()
                S.op("pe", "matmul", reads=[d_kq], writes=pkk.W(), signal=True, out=pkk.f[:, 0:256], lhsT=kq[:, 0:128], rhs=kq[:], start=True, stop=True)
                S.op("dve", "tensor_copy", reads=[pkk.d], writes=[d_kkall, pkk.d], out=kkall[:, blk, :], in_=pkk.f[:, 0:256])
                yield

        hl = list(heads)
        for _ in prologue(hl[0], 0):
            pass
        for hi, h in enumerate(hl):
            ytm, d_ytm = ytms[hi % 2]
            kkall, d_kkall = kkalls[hi % 2]
            bg = prologue(hl[hi + 1], (hi + 1) % 2) if hi + 1 < len(hl) else None
            S.op("pool", "memset", reads=[], writes=[d_oacc], ap=oacc[:].rearrange("p a b -> p (a b)"), constant=0.0)
            for dr in range(2):
                S.op("pool", "memset", writes=[Sf[dr][1]], ap=Sf[dr][0][:], constant=0.0)
                S.op("pool", "memset", writes=[Sbf[dr][1]], ap=Sbf[dr][0][:], constant=0.0)
                for c in range(2):
                    S.op("pool", "memset", writes=[vnew[dr][c][1]], ap=vnew[dr][c][0][:], constant=0.0)
            chain_done[0] = 0
            chain_done[1] = 0
            pending = []
            for it in range(nb):
                pending.append((0, it))
                pending.append((1, it))
            pending.reverse()
            active = []
            free = list(range(W))
            while pending or active:
                if pending and free:
                    dr, it = pending.pop()
                    si = free.pop()
                    active.append((si, blockdir(slots[si], h, dr, it, ytm, d_ytm, kkall, d_kkall)))
                for ent in list(active):
                    try:
                        next(ent[1])
                    except StopIteration:
                        active.remove(ent)
                        free.append(ent[0])
                if bg is not None:
                    try:
                        next(bg)
                    except StopIteration:
                        bg = None
            if bg is not None:
                for _ in bg:
                    pass
            S.dma("sp", sgT[:], FT[48 + h, :, off:off + slen], writes=[d_sgT])
            S.op("act", "activation", reads=[d_sgT], writes=[d_sgT], out=sgT[:], in_=sgT[:], func=AF.Silu)
            for blk in range(nb):
                S.op("act", "activation", reads=[d_oacc], writes=[d_junk, d_oss], out=junkc[:], in_=oacc[:, blk, :], func=AF.Square, accum_out=oss[:, 0, blk:blk + 1])
            S.op("dve", "tensor_scalar", reads=[d_oss], writes=[d_oss], out=oss[:, 1, :], in0=oss[:, 0, :], scalar1=1.0 / 128, scalar2=EPS, op0=ALU.mult, op1=ALU.add)
            S.op("act", "activation", reads=[d_oss], writes=[d_oss], out=oss[:, 1, :], in_=oss[:, 1, :], func=AF.Sqrt)
            S.op("dve", "reciprocal", reads=[d_oss], writes=[d_oss], out=oss[:, 1, :], in_=oss[:, 1, :])
            for blk in range(nb):
                on, d_on = ontm.next()
                S.op("act", "activation", reads=[d_oacc, d_oss], writes=[d_on], out=on[:], in_=oacc[:, blk, :], func=AF.Copy, scale=oss[:, 1, blk:blk + 1])
                pt = pany.next()
                S.op("pe", "transpose", reads=[d_on, d_const], writes=pt.W(), signal=True, out=pt.h[:, 0:128], in_=on[:], identity=identb[:])
                S.op("dve", "scalar_tensor_tensor", reads=[pt.d, d_c, d_sgT], writes=[d_sgT, pt.d], out=sgT[:, blk * 128:(blk + 1) * 128], in0=pt.h[:, 0:128], scalar=dng[:, 0:1],
                     in1=sgT[:, blk * 128:(blk + 1) * 128], op0=ALU.mult, op1=ALU.mult)
            S.dma("pool", MT[8 + h, :, off:off + slen], sgT[:], reads=[d_sgT])
        S.phase_end()


def _host_maps(xs_per_core, w_in, w_out, norm_in_gain, conv_w, a_log, dt_bias, dn_gain, final_gain):
    c = _consts()
    gin = np.ascontiguousarray(norm_in_gain.reshape(16, 128).T)
    cw = np.ascontiguousarray(conv_w.reshape(5, 24, 128).transpose(2, 1, 0).reshape(128, 120))
    al = np.ascontiguousarray(np.broadcast_to(a_log.reshape(1, 1, 16), (128, 32, 16)).reshape(128, 512))
    db = np.ascontiguousarray(np.broadcast_to(dt_bias.reshape(1, 1, 16), (128, 32, 16)).reshape(128, 512))
    dg = np.ascontiguousarray(dn_gain.reshape(128, 1))
    fg = np.ascontiguousarray(np.broadcast_to(final_gain.reshape(1, D), (128, D)))
    maps = []
    for xc in xs_per_core:
        m = {"x": xc, "w_in": w_in, "w_out": w_out, "gin": gin, "convw": cw, "alog": al, "dtb": db,
             "dngain": dg, "fgain": fg}
        m.update(c)
        maps.append(m)
    return maps


PHASES = "0ABCD"


def kernel(x_prompt, x_sample, norm_in_gain, w_in, conv_w, a_log, dt_bias, delta_norm_gain, w_out, final_norm_gain):
    f = lambda a: np.ascontiguousarray(np.asarray(a, dtype=np.float32))
    x_prompt, x_sample = f(x_prompt), f(x_sample)
    seqs = [4096, 2048, 2048]
    xs = []
    for c in range(8):
        xs.append(np.ascontiguousarray(np.concatenate(
            [x_prompt[c], x_sample[2 * c], x_sample[2 * c + 1]], axis=0)))
    maps = _host_maps(xs, f(w_in)[0], f(w_out)[0], f(norm_in_gain)[0], f(conv_w)[0], f(a_log)[0],
                      f(dt_bias)[0], f(delta_norm_gain)[0], f(final_norm_gain))
    nc = build(seqs, phases=PHASES)
    res = run_bass_kernel_spmd(nc, maps, core_ids=list(range(8)))
    yp = np.empty((8, 4096, D), np.float32)
    ys = np.empty((16, 2048, D), np.float32)
    for c in range(8):
        yc = res.results[c]["y"]
        yp[c] = yc[0:4096]
        ys[2 * c] = yc[4096:6144]
        ys[2 * c + 1] = yc[6144:8192]
    return (yp, ys)
```
